# Optimizing a Trainium2 kernel written in Bass

```python
import jax, jax.numpy as jnp
from jax import lax
import numpy as np

D_MODEL = 1024
BATCH = 8
SEQ = 2048
DEPTH = 4

DN_HEADS = 4
DN_HEAD_DIM = 128
DN_WIDTH = DN_HEADS * DN_HEAD_DIM
CONV_WIDTH = 4
CHUNK = 64
SW_Q_HEADS = 8
SW_KV_HEADS = 2
SW_GROUP = SW_Q_HEADS // SW_KV_HEADS
SW_HEAD_DIM = 64
SW_WIDTH = SW_Q_HEADS * SW_HEAD_DIM
SW_KV_WIDTH = SW_KV_HEADS * SW_HEAD_DIM
WINDOW = 128
ROPE_THETA = 10000.0
MIX_WIDTH = DN_WIDTH + SW_WIDTH
D_FF = 3584
N_EXPERTS = 8
TOP_K = 2
N_DENSE = (DEPTH + 1) // 2
N_MOE = DEPTH // 2
ALPHA = (2.0 * DEPTH) ** 0.25
BETA_INIT = (8.0 * DEPTH) ** -0.25
LN_EPS = 1e-5
RMS_EPS = 1e-6
IN_SIZES = (DN_WIDTH, DN_WIDTH, DN_WIDTH, DN_WIDTH, DN_HEADS, DN_HEADS,
            SW_WIDTH, SW_KV_WIDTH, SW_KV_WIDTH)
IN_COLS = sum(IN_SIZES)

kernel_name = "hymba_deltanet_swa_sinks_moe_deepnorm"


def layer_norm(x, g, b):
    xf = x.astype(jnp.float32)
    mu = jnp.mean(xf, axis=-1, keepdims=True)
    var = jnp.mean(jnp.square(xf - mu), axis=-1, keepdims=True)
    y = (xf - mu) * lax.rsqrt(var + LN_EPS) * g.astype(jnp.float32) + b.astype(jnp.float32)
    return y.astype(x.dtype)


def l2norm(x):
    return x * lax.rsqrt(jnp.sum(x * x, axis=-1, keepdims=True) + 1e-6)


def causal_conv_silu(x, w):
    S = x.shape[1]
    xp = jnp.pad(x, ((0, 0), (CONV_WIDTH - 1, 0), (0, 0)))
    y = sum(xp[:, k:k + S] * w[k] for k in range(CONV_WIDTH))
    return jax.nn.silu(y)


def chunk_gated_delta_rule(q, k, v, g, beta):
    B, S, H, dk = q.shape
    dv = v.shape[-1]
    N = S // CHUNK
    f32 = jnp.float32
    q = l2norm(q.astype(f32)) * (dk ** -0.5)
    k = l2norm(k.astype(f32))
    v = v.astype(f32)

    def chunks(t):
        t = t.reshape((B, N, CHUNK, H) + t.shape[3:])
        return jnp.moveaxis(t, 3, 1)

    q, k, v = chunks(q), chunks(k), chunks(v)
    g = chunks(g.astype(f32))
    beta = chunks(beta.astype(f32))
    g_cum = jnp.cumsum(g, axis=-1)
    k_beta = k * beta[..., None]
    v_beta = v * beta[..., None]

    idx = jnp.arange(CHUNK)
    lower_incl = idx[:, None] >= idx[None, :]
    strict_lower = idx[:, None] > idx[None, :]
    decay = jnp.exp(jnp.where(lower_incl, g_cum[..., :, None] - g_cum[..., None, :], -jnp.inf))

    L = jnp.where(strict_lower, jnp.einsum('bhncd,bhnjd->bhncj', k_beta, k) * decay, 0.0)
    eye = jnp.eye(CHUNK, dtype=f32)
    T = lax.linalg.triangular_solve(eye + L, jnp.broadcast_to(eye, L.shape),
                                    left_side=True, lower=True, unit_diagonal=True)
    u = jnp.einsum('bhncj,bhnjd->bhncd', T, v_beta)
    w = jnp.einsum('bhncj,bhnjd->bhncd', T, k_beta * jnp.exp(g_cum)[..., None])
    attn_intra = jnp.where(lower_incl, jnp.einsum('bhncd,bhnjd->bhncj', q, k) * decay, 0.0)
    q_dec = q * jnp.exp(g_cum)[..., None]
    k_end = k * jnp.exp(g_cum[..., -1:] - g_cum)[..., None]
    g_last = jnp.exp(g_cum[..., -1])

    def step(state, inp):
        u_n, w_n, qd_n, ke_n, a_n, gl_n = inp
        v_new = u_n - jnp.einsum('bhck,bhkv->bhcv', w_n, state)
        o = jnp.einsum('bhck,bhkv->bhcv', qd_n, state) + jnp.einsum('bhcj,bhjv->bhcv', a_n, v_new)
        state = state * gl_n[..., None, None] + jnp.einsum('bhck,bhcv->bhkv', ke_n, v_new)
        return state, o

    xs = tuple(jnp.moveaxis(t, 2, 0) for t in (u, w, q_dec, k_end, attn_intra, g_last))
    state0 = jnp.zeros((B, H, dk, dv), f32)
    _, o = lax.scan(step, state0, xs)
    o = jnp.moveaxis(o, 0, 2)
    return jnp.moveaxis(o, 1, 3).reshape(B, S, H, dv)


def rope_tables(positions):
    inv_freq = ROPE_THETA ** (-jnp.arange(0, SW_HEAD_DIM, 2, dtype=jnp.float32) / SW_HEAD_DIM)
    ang = positions.astype(jnp.float32)[..., None] * inv_freq
    return jnp.cos(ang)[:, :, None, :], jnp.sin(ang)[:, :, None, :]


def apply_rope(x, cos, sin):
    xf = x.astype(jnp.float32)
    x1, x2 = jnp.split(xf, 2, axis=-1)
    return jnp.concatenate([x1 * cos - x2 * sin, x2 * cos + x1 * sin], axis=-1).astype(x.dtype)


def sliding_window_attention(q, k, v, sinks, cos, sin):
    B, S = q.shape[:2]
    BLK = WINDOW
    NB = S // BLK
    q = apply_rope(q, cos, sin)
    k = apply_rope(k, cos, sin)
    qb = q.reshape(B, NB, BLK, SW_KV_HEADS, SW_GROUP, SW_HEAD_DIM)

    def band(t):
        tb = t.reshape(B, NB, BLK, SW_KV_HEADS, SW_HEAD_DIM)
        prev = jnp.concatenate([jnp.zeros_like(tb[:, :1]), tb[:, :-1]], axis=1)
        return jnp.concatenate([prev, tb], axis=2)

    kband, vband = band(k), band(v)
    s = jnp.einsum('bnqhgd,bnkhd->bnhgqk', qb, kband).astype(jnp.float32) * (SW_HEAD_DIM ** -0.5)
    qi = jnp.arange(BLK)[:, None]
    kj = jnp.arange(2 * BLK)[None, :]
    rel = qi + BLK - kj
    in_window = (rel >= 0) & (rel < WINDOW)
    exists = (jnp.arange(NB)[:, None, None] > 0) | (kj >= BLK)[None]
    valid = in_window[None] & exists
    s = jnp.where(valid[None, :, None, None], s, -jnp.inf)
    sink = sinks.astype(jnp.float32).reshape(SW_KV_HEADS, SW_GROUP)[None, None, :, :, None, None]
    m = jnp.maximum(jnp.max(s, axis=-1, keepdims=True), sink)
    p = jnp.exp(s - m)
    denom = jnp.sum(p, axis=-1, keepdims=True) + jnp.exp(sink - m)
    probs = (p / denom).astype(v.dtype)
    o = jnp.einsum('bnhgqk,bnkhd->bnqhgd', probs, vband)
    return o.reshape(B, S, SW_WIDTH)


def hybrid_mixer(x, cos, sin, w_in, conv_w, a_log, dt_bias, dn_norm_w, sinks, w_out):
    B, S, _ = x.shape
    proj = x @ w_in
    points = [int(c) for c in np.cumsum(IN_SIZES)[:-1]]
    q_dn, k_dn, v_dn, z_dn, a_dn, b_dn, q_sw, k_sw, v_sw = jnp.split(proj, points, axis=-1)

    qkv = causal_conv_silu(jnp.concatenate([q_dn, k_dn, v_dn], axis=-1), conv_w)
    q_dn, k_dn, v_dn = jnp.split(qkv, 3, axis=-1)
    heads = lambda t: t.reshape(B, S, DN_HEADS, DN_HEAD_DIM)
    g = -jnp.exp(a_log.astype(jnp.float32)) * jax.nn.softplus(
        a_dn.astype(jnp.float32) + dt_bias.astype(jnp.float32))
    beta = jax.nn.sigmoid(b_dn.astype(jnp.float32))
    o_dn = chunk_gated_delta_rule(heads(q_dn), heads(k_dn), heads(v_dn), g, beta)
    o_dn = o_dn * lax.rsqrt(jnp.mean(o_dn * o_dn, axis=-1, keepdims=True) + RMS_EPS)
    o_dn = o_dn * dn_norm_w.astype(jnp.float32) * jax.nn.silu(heads(z_dn).astype(jnp.float32))
    o_dn = o_dn.reshape(B, S, DN_WIDTH).astype(x.dtype)

    o_sw = sliding_window_attention(
        q_sw.reshape(B, S, SW_Q_HEADS, SW_HEAD_DIM),
        k_sw.reshape(B, S, SW_KV_HEADS, SW_HEAD_DIM),
        v_sw.reshape(B, S, SW_KV_HEADS, SW_HEAD_DIM), sinks, cos, sin)

    return jnp.concatenate([o_dn, o_sw], axis=-1) @ w_out


def swiglu(x, w_gate, w_up, w_down):
    return (jax.nn.silu(x @ w_gate) * (x @ w_up)) @ w_down


def moe_swiglu(x, router_w, w_gate, w_up, w_down):
    B, S, D = x.shape
    xt = x.reshape(-1, D)
    logits = (xt @ router_w).astype(jnp.float32)
    top_vals, top_idx = lax.top_k(logits, TOP_K)
    gates = jax.nn.softmax(top_vals, axis=-1)
    combine = jnp.sum(jax.nn.one_hot(top_idx, N_EXPERTS, dtype=jnp.float32) * gates[..., None], axis=1)
    combine = combine.astype(x.dtype)
    out = jnp.zeros_like(xt)
    for e in range(N_EXPERTS):
        out = out + combine[:, e:e + 1] * swiglu(xt, w_gate[e], w_up[e], w_down[e])
    return out.reshape(B, S, D)


def setup_inputs(seed: int = 0) -> dict:
    key = jax.random.key(seed)
    ks = jax.random.split(key, 20)
    f32 = jnp.float32
    nrm = lambda k, shape, scale: jax.random.normal(k, shape, f32) * scale
    x = jax.random.normal(ks[0], (BATCH, SEQ, D_MODEL), f32)
    positions = jnp.broadcast_to(jnp.arange(SEQ, dtype=jnp.int32)[None, :], (BATCH, SEQ))
    col_scale = jnp.concatenate([
        jnp.ones((2 * DN_WIDTH,), f32), jnp.full((DN_WIDTH,), BETA_INIT, f32),
        jnp.ones((DN_WIDTH + 2 * DN_HEADS + SW_WIDTH + SW_KV_WIDTH,), f32),
        jnp.full((SW_KV_WIDTH,), BETA_INIT, f32)])
    w_in = nrm(ks[1], (DEPTH, D_MODEL, IN_COLS), D_MODEL ** -0.5) * col_scale
    conv_w = nrm(ks[2], (DEPTH, CONV_WIDTH, 3 * DN_WIDTH), CONV_WIDTH ** -0.5)
    a_log = jnp.log(jax.random.uniform(ks[3], (DEPTH, DN_HEADS), f32, minval=1.0, maxval=16.0))
    dt = jnp.exp(jax.random.uniform(ks[4], (DEPTH, DN_HEADS), f32,
                                    minval=float(np.log(1e-3)), maxval=float(np.log(1e-1))))
    dt_bias = dt + jnp.log(-jnp.expm1(-dt))
    dn_norm_w = 1.0 + nrm(ks[5], (DEPTH, DN_HEAD_DIM), 0.02)
    sinks = nrm(ks[6], (DEPTH, SW_Q_HEADS), 0.5)
    w_out = nrm(ks[7], (DEPTH, MIX_WIDTH, D_MODEL), MIX_WIDTH ** -0.5 * BETA_INIT)
    ln_g = 1.0 + nrm(ks[8], (DEPTH, 2, D_MODEL), 0.02)
    ln_b = nrm(ks[9], (DEPTH, 2, D_MODEL), 0.02)
    ffn_w_gate = nrm(ks[10], (N_DENSE, D_MODEL, D_FF), D_MODEL ** -0.5 * BETA_INIT)
    ffn_w_up = nrm(ks[11], (N_DENSE, D_MODEL, D_FF), D_MODEL ** -0.5 * BETA_INIT)
    ffn_w_down = nrm(ks[12], (N_DENSE, D_FF, D_MODEL), D_FF ** -0.5 * BETA_INIT)
    router_w = nrm(ks[13], (N_MOE, D_MODEL, N_EXPERTS), D_MODEL ** -0.5)
    moe_w_gate = nrm(ks[14], (N_MOE, N_EXPERTS, D_MODEL, D_FF), D_MODEL ** -0.5 * BETA_INIT)
    moe_w_up = nrm(ks[15], (N_MOE, N_EXPERTS, D_MODEL, D_FF), D_MODEL ** -0.5 * BETA_INIT)
    moe_w_down = nrm(ks[16], (N_MOE, N_EXPERTS, D_FF, D_MODEL), D_FF ** -0.5 * BETA_INIT)
    return {"x": x, "positions": positions, "w_in": w_in, "conv_w": conv_w, "a_log": a_log,
            "dt_bias": dt_bias, "dn_norm_w": dn_norm_w, "sinks": sinks, "w_out": w_out,
            "ln_g": ln_g, "ln_b": ln_b, "ffn_w_gate": ffn_w_gate, "ffn_w_up": ffn_w_up,
            "ffn_w_down": ffn_w_down, "router_w": router_w, "moe_w_gate": moe_w_gate,
            "moe_w_up": moe_w_up, "moe_w_down": moe_w_down}


def reference(x, positions, w_in, conv_w, a_log, dt_bias, dn_norm_w, sinks, w_out,
              ln_g, ln_b, ffn_w_gate, ffn_w_up, ffn_w_down, router_w, moe_w_gate,
              moe_w_up, moe_w_down):
    cos, sin = rope_tables(positions)
    for layer in range(DEPTH):
        mix = hybrid_mixer(x, cos, sin, w_in[layer], conv_w[layer], a_log[layer],
                           dt_bias[layer], dn_norm_w[layer], sinks[layer], w_out[layer])
        x = layer_norm(ALPHA * x + mix, ln_g[layer, 0], ln_b[layer, 0])
        if layer % 2 == 0:
            i = layer // 2
            f = swiglu(x, ffn_w_gate[i], ffn_w_up[i], ffn_w_down[i])
        else:
            i = layer // 2
            f = moe_swiglu(x, router_w[i], moe_w_gate[i], moe_w_up[i], moe_w_down[i])
        x = layer_norm(ALPHA * x + f, ln_g[layer, 1], ln_b[layer, 1])
    return x
```

```python
import math
from contextlib import ExitStack
import numpy as np
import concourse.bass as bass
import concourse.mybir as mybir
from concourse.bass_utils import run_bass_kernel_spmd

F32 = mybir.dt.float32
BF16 = mybir.dt.bfloat16
I32 = mybir.dt.int32
ALU = mybir.AluOpType
AF = mybir.ActivationFunctionType
AX = mybir.AxisListType

CE = ['pe', 'act', 'dve', 'pool', 'sp']
EIDX = {e: i for i, e in enumerate(CE)}
EPOCH = 12000

DEPTH = 4
S = 2048
NT = 16
D = 1024
KC = 8
INC = 2824
DFF = 3584
NE = 8
ALPHA = (2.0 * DEPTH) ** 0.25
TMW = 1288
BIG = 30000.0


class Op:
    __slots__ = ('eng', 'fn', 'is_dma', 'dma_key', 'eidx', 'signal', 'sem', 'val', 'waits', 'vc', 'dvc')


class Sched:
    def __init__(self, nc):
        self.nc = nc
        self.eops = {e: [] for e in CE}
        self.last_writer = {}
        self.readers = {}
        self.desc = {}
        self.known = set()
        self.aliases = {}
        self.dma_cnt = {}
        self.last = {e: None for e in CE}
        self.nops = 0

    def alias(self, a, b):
        self.aliases.setdefault(a, set()).add(b)
        self.aliases.setdefault(b, set()).add(a)

    def _reg(self, k):
        if k in self.known:
            return
        self.known.add(k)
        for i in range(1, len(k)):
            self.desc.setdefault(k[:i], set()).add(k)
        self.desc.setdefault(k, set())

    def _related(self, k):
        out = [k]
        for i in range(1, len(k)):
            out.append(k[:i])
        out.extend(self.desc.get(k, ()))
        for r in self.aliases.get(k[0], ()):
            out.append((r,))
            out.extend(self.desc.get((r,), ()))
        return out

    def add(self, eng, fn, reads=(), writes=(), dma_key=None):
        reads = [k if isinstance(k, tuple) else (k,) for k in reads]
        writes = [k if isinstance(k, tuple) else (k,) for k in writes]
        writes = writes + [k for k in reads if k[0] == 'pb' and k not in writes]
        for k in reads:
            self._reg(k)
        for k in writes:
            self._reg(k)
        op = Op()
        op.eng = eng
        op.fn = fn
        op.is_dma = dma_key is not None
        op.dma_key = dma_key
        op.eidx = len(self.eops[eng])
        op.signal = False
        op.sem = None
        op.val = None
        deps = {}
        for b in reads:
            for k in self._related(b):
                w = self.last_writer.get(k)
                if w is not None:
                    deps[id(w)] = (w, 'raw')
        for b in writes:
            for k in self._related(b):
                w = self.last_writer.get(k)
                if w is not None and id(w) not in deps:
                    deps[id(w)] = (w, 'waw')
                for r in self.readers.get(k, {}).values():
                    if id(r) not in deps:
                        deps[id(r)] = (r, 'war')
        prev = self.last[eng]
        vc = list(prev.vc) if prev is not None else [-1] * len(CE)
        dvc = dict(prev.dvc) if prev is not None else {}
        best = {}
        for d, kind in deps.values():
            if d is op:
                continue
            if d.is_dma:
                if dvc.get(d.dma_key, 0) >= d.val:
                    continue
                key = ('d', d.dma_key)
                cur = best.get(key)
                if cur is None or d.val > cur.val:
                    best[key] = d
            else:
                if d.eng == eng and not op.is_dma and kind != 'raw':
                    continue
                if vc[EIDX[d.eng]] >= d.eidx:
                    continue
                key = ('c', d.eng)
                cur = best.get(key)
                if cur is None or d.eidx > cur.eidx:
                    best[key] = d
        op.waits = list(best.values())
        for d in op.waits:
            if d.is_dma:
                dvc[d.dma_key] = max(dvc.get(d.dma_key, 0), d.val)
            else:
                d.signal = True
                vc[EIDX[d.eng]] = max(vc[EIDX[d.eng]], d.eidx)
            for i, v in enumerate(d.vc):
                if v > vc[i]:
                    vc[i] = v
            for k, v in d.dvc.items():
                if v > dvc.get(k, 0):
                    dvc[k] = v
        if op.is_dma:
            c = self.dma_cnt.get(dma_key, 0) + 1
            self.dma_cnt[dma_key] = c
            op.val = 16 * c
        op.vc = vc
        op.dvc = dvc
        slot = ('d', dma_key) if op.is_dma else eng
        for b in reads:
            self.readers.setdefault(b, {})[slot] = op
        for b in writes:
            self.last_writer[b] = op
            self.readers[b] = {}
        self.eops[eng].append(op)
        self.last[eng] = op
        self.nops += 1
        return op

    def emit(self, final_wait_ops=()):
        nc = self.nc
        with ExitStack() as st:
            for e in CE:
                cnt = 0
                sem = None
                ep = 0
                for op in self.eops[e]:
                    if op.is_dma or not op.signal:
                        continue
                    if sem is None or cnt >= EPOCH:
                        sem = st.enter_context(nc.semaphore("s_%s_%d" % (e, ep)))
                        ep += 1
                        cnt = 0
                    cnt += 1
                    op.sem = sem
                    op.val = cnt
            dsem = {}
            for i, k in enumerate(self.dma_cnt):
                dsem[k] = st.enter_context(nc.semaphore("d_%d" % i))
            for e in CE:
                for op in self.eops[e]:
                    if op.is_dma:
                        op.sem = dsem[op.dma_key]
            block = st.enter_context(nc.Block())

            def run(engname):
                def body(eng):
                    for op in self.eops[engname]:
                        for d in op.waits:
                            eng.wait_ge(d.sem, d.val)
                        ins = op.fn(eng)
                        if op.is_dma:
                            ins.then_inc(op.sem, 16)
                        elif op.signal:
                            ins.then_inc(op.sem, 1)
                    if engname == 'sp':
                        for d in final_wait_ops:
                            eng.wait_ge(d.sem, d.val)
                return body

            block.tensor(run('pe'))
            block.scalar(run('act'))
            block.vector(run('dve'))
            block.gpsimd(run('pool'))
            block.sync(run('sp'))


def build(nlayers=DEPTH, dbg=False):
    nc = bass.Bass("TRN2", target_bir_lowering=False)
    dram = lambda name, shape, dt, kind="ExternalInput": nc.dram_tensor(name, shape, dt, kind=kind).ap()
    x_d = dram("x", [S, D], F32)
    pos_d = dram("pos", [128, NT], I32)
    w_in_d = dram("w_in", [DEPTH, D, INC], F32)
    w_out_d = dram("w_out", [DEPTH, D, D], F32)
    fg_d = dram("ffn_w_gate", [2, D, DFF], F32)
    fu_d = dram("ffn_w_up", [2, D, DFF], F32)
    fd_d = dram("ffn_w_down", [2, DFF, D], F32)
    mg_d = dram("moe_w_gate", [2, NE, D, DFF], F32)
    mu_d = dram("moe_w_up", [2, NE, D, DFF], F32)
    md_d = dram("moe_w_down", [2, NE, DFF, D], F32)
    convw_d = dram("convw", [DEPTH, 128, 12, 4], F32)
    pv_d = dram("pv", [DEPTH, 128, 144], F32)
    lnp_d = dram("lnp", [DEPTH, 2, 128, 2048], F32)
    rwb_d = dram("rwb", [2, 128, NE, D], F32)
    cst_d = dram("cst", [128, 928], F32)
    y_d = dram("y", [S, D], F32, kind="ExternalOutput")
    fm_s = dram("fm_s", [1536, 3 + S], F32, kind="Internal")
    tm_s = dram("tm_s", [S, TMW], F32, kind="Internal")
    if dbg:
        dbg_o = dram("dbg_o", [S, D], F32, kind="ExternalOutput")
        dbg_x1 = dram("dbg_x1", [S, D], F32, kind="ExternalOutput")
        dbg_raw = dram("dbg_raw", [S, 512], F32, kind="ExternalOutput")

    st = ExitStack()
    with st:
        sb = lambda name, shape, dt: st.enter_context(nc.sbuf_tensor("s_" + name, shape, dt))
        xres = sb("xres", [128, NT, D], F32)
        xT = sb("xT", [128, KC, S], BF16)
        R = sb("R", [128, 32768], BF16)
        wa = [R[:, s * 4096:(s + 1) * 4096].rearrange("p (k n) -> p k n", k=KC) for s in range(2)]
        wout = R[:, 8192:16384].rearrange("p (k n) -> p k n", k=KC)
        wg = [R[:, s * 4096:(s + 1) * 4096].rearrange("p (k n) -> p k n", k=KC) for s in range(2)]
        wu = [R[:, 8192 + s * 4096:8192 + (s + 1) * 4096].rearrange("p (k n) -> p k n", k=KC) for s in range(2)]
        wd = [R[:, 16384 + s * 4096:16384 + (s + 1) * 4096].rearrange("p (f n) -> p f n", f=4) for s in range(2)]
        hT = R[:, 24576:32768].rearrange("p (f n) -> p f n", f=4)
        rwb = R[:, 16384:32768].bitcast(F32).rearrange("p (e d) -> p e d", e=NE)
        rv = lambda a, n: R[:, a:a + n]
        rf = lambda a, n: R[:, a:a + n].bitcast(F32)
        kTa = rv(0, 2048)
        va = rv(2048, 2048).rearrange("p (t n) -> p t n", t=NT)
        Sf = rf(4096, 1024).rearrange("p (h n) -> p h n", h=4)
        Sb = rv(5120, 512).rearrange("p (h n) -> p h n", h=4)
        o4 = rf(5632, 1024).rearrange("p (h n) -> p h n", h=4)
        zs = rv(6656, 512)
        qr = rv(7168, 512)
        qTs = rv(7680, 512).rearrange("p (a n) -> p a n", a=4)
        stg = [rf(16384 + i * 1024, 1024) for i in range(4)]
        raw = rf(16384, 3144).rearrange("p (c w) -> p c w", c=12)
        cacc = rf(19528, 3072).rearrange("p (c w) -> p c w", c=12)
        yb = rv(25672, 1536).rearrange("p (c w) -> p c w", c=12)
        rn = rf(27208, 2048)
        tmv = rf(29256, 2576)
        cst = sb("cst", [128, 928], F32)
        identf = cst[:, 0:128]
        tri = cst[:, 128:256]
        onesf = cst[:, 256:384]
        maskA = cst[:, 384:512]
        maskB = cst[:, 512:640]
        swam = cst[:, 640:896]
        invf = cst[:, 896:928]
        identb = sb("identb", [128, 128], BF16)
        onesb = sb("onesb", [128, 128], BF16)
        cvals = sb("cvals", [128, 4], F32)
        posi = sb("posi", [128, NT], I32)
        posf = sb("posf", [128, NT], F32)
        cosT = sb("cosT", [128, NT, 32], F32)
        sinT = sb("sinT", [128, NT, 32], F32)
        lnp = sb("lnp", [128, 2048], F32)
        convw = sb("convw", [128, 12, 4], F32)
        pv = sb("pv", [128, 144], F32)
        ab_all = sb("ab_all", [128, NT, 8], F32)
        pp = {n: sb("pp_" + n, [128, 64], F32) for n in
              ['ta', 'sp', 'g', 'eb', 'beta', 'nbeta', 'gcum', 'ngc', 'egc', 'bege']}
        negA = sb("negA", [128, 4], F32)
        sq = sb("sq", [128, 1024], BF16)
        qkT = sb("qkT", [128, 8, 128], BF16)
        ytmp = sb("ytmp", [128, D], F32)
        ptmp = None
        o4b = ytmp[:, 0:512].rearrange("p (h n) -> p h n", h=4)
        sgt = [ytmp[:, 0:512], ytmp[:, 512:1024]]
        kr = sb("kr", [128, 128], BF16)
        pb_ = sb("pb_", [128, 1024], BF16)
        mask8b = sb("mask8b", [128, 256], BF16)
        pTb = sb("pTb", [128, 1024], BF16)
        st8 = sb("st8", [128, 24], F32)
        o_tok = sb("o_tok", [128, D], BF16)
        oT = sb("oT", [128, KC, 128], BF16)
        lnst = sb("lnst", [128, 16], F32)
        DN = []
        for i_ in range(2):
            n_ = str(i_)
            B_ = {'n': n_, 'bps': 3 + 2 * i_, 'brd': 4 + 2 * i_}
            if i_ == 0:
                f32t = lambda nm, w_: sb(nm + n_, [128, w_], F32)
                b16t = lambda nm, w_: sb(nm + n_, [128, w_], BF16)
            else:
                off = [22600]
                offb = [31832]
                def f32t(nm, w_):
                    a_ = rf(off[0], 2 * w_)
                    off[0] += 2 * w_
                    return a_
                def b16t(nm, w_):
                    a_ = rv(offb[0], w_)
                    offb[0] += w_
                    return a_
            B_['trg'] = f32t('trg', 128)
            B_['dL'] = f32t('dL', 128)
            B_['dAT'] = f32t('dAT', 128)
            B_['dsc'] = f32t('dsc', 4)
            B_['abp'] = [f32t('abpa', 384), f32t('abpb', 384)]
            B_['u_sb'] = f32t('u_sb', 128)
            B_['t1'] = f32t('t1', 128)
            B_['ATm'] = b16t('ATm', 128)
            B_['TT'] = b16t('TT', 128)
            B_['Xv'] = b16t('Xv', 128)
            B_['Xw'] = b16t('Xw', 128)
            B_['ke'] = b16t('ke', 128)
            B_['wT_sb'] = b16t('wT_sb', 128)
            B_['vn'] = b16t('vn', 128)
            if i_ == 1:
                assert off[0] <= 25672 and offb[0] <= 32768, (off[0], offb[0])
            DN.append(B_)
        rs4 = sb("rs4", [128, 8], F32)
        lg = sb("lg", [128, NT, NE], F32)
        comb = sb("comb", [128, NT, NE], F32)
        mt = {n: sb("mt_" + n, [128, NT, NE], F32) for n in ['eq1', 'l2', 'eq2']}
        ms = {n: sb("ms_" + n, [128, NT], F32) for n in ['m1', 'm2', 'd', 'ed', 'g1', 'g2']}
        PB = [st.enter_context(nc.psum_tensor("pb%d" % i, [128, 512], F32)) for i in range(8)]
        PBb = [p[:].bitcast(BF16) for p in PB]

        print("sbuf bytes remaining:", nc.sbuf_bytes_remaining)
        Sc = Sched(nc)
        def alias_groups(groups):
            for i in range(len(groups)):
                for j in range(i + 1, len(groups)):
                    for a_ in groups[i]:
                        for b_ in groups[j]:
                            Sc.alias(a_, b_)
        alias_groups([['wa0', 'wa1'], ['kTa', 'va', 'Sf', 'Sb', 'o4', 'zs', 'qr', 'qTs'], ['wg0', 'wg1']])
        alias_groups([['wout'], ['wu0', 'wu1']])
        set1 = [nm + '1' for nm in ['trg', 'dL', 'dAT', 'dsc', 'abp', 'u_sb', 't1', 'ATm', 'TT', 'Xv', 'Xw', 'ke', 'wT_sb', 'vn']]
        alias_groups([['stg'], ['raw', 'cacc', 'yb', 'rn', 'tmv'] + set1, ['wd0', 'wd1', 'hT'], ['rwb']])

        def A(eng, meth, *args, r=(), w=(), **kw):
            return Sc.add(eng, lambda e: getattr(e, meth)(*args, **kw), reads=r, writes=w)

        def DMA(eng, out, in_, r=(), w=(), key=None):
            return Sc.add(eng, lambda e: e.dma_start(out=out, in_=in_), reads=r, writes=w, dma_key=key)

        def MM(out, lhsT, rhs, start, stop, r, w):
            return Sc.add('pe', lambda e: e.matmul(out, lhsT, rhs, start=start, stop=stop), reads=r, writes=w)

        def TR(out, in_, ident, r, w):
            return Sc.add('pe', lambda e: e.transpose(out, in_, ident), reads=r, writes=w)

        def ACT(out, in_, func, r, w, **kw):
            return Sc.add('act', lambda e: e.activation(out, in_, func, **kw), reads=r, writes=w)

        rr = [0]

        def evac_eng():
            rr[0] += 1
            return 'act' if rr[0] % 2 == 0 else 'dve'

        def COPY(eng, out, in_, r, w):
            if eng == 'act':
                return ACT(out, in_, AF.Copy, r, w)
            return A(eng, 'tensor_copy', out, in_, r=r, w=w)

        DMA('sp', cst[:], cst_d, w=['cst'], key='cst')
        DMA('sp', posi[:], pos_d, w=['posi'], key='posi')
        DMA('sp', xres[:], x_d.rearrange("(n p) d -> p n d", p=128), w=['xres'], key='xin')
        A('dve', 'tensor_copy', identb[:], identf, r=['cst'], w=['identb'])
        A('dve', 'tensor_copy', onesb[:], onesf, r=['cst'], w=['onesb'])
        A('dve', 'tensor_scalar', mask8b[:], swam, 8.0, None, ALU.mult, r=['cst'], w=['mask8b'])
        A('dve', 'memset', cvals[:, 0:1], -math.pi, w=['cvals'])
        A('dve', 'memset', cvals[:, 1:2], 1.0, w=['cvals'])
        A('dve', 'memset', cvals[:, 2:3], 1e-5, w=['cvals'])
        A('dve', 'memset', cvals[:, 3:4], 1e-6, w=['cvals'])
        A('dve', 'tensor_copy', posf[:], posi[:], r=['posi'], w=['posf'])
        A('dve', 'tensor_tensor', cosT[:], posf[:].unsqueeze(2).to_broadcast([128, NT, 32]),
          invf.unsqueeze(1).to_broadcast([128, NT, 32]), ALU.mult, r=['posf', 'cst'], w=['cosT'])
        TWO_PI = 2 * math.pi
        angi = ytmp[:, 512:1024].bitcast(I32).rearrange("p (t n) -> p t n", t=NT)
        angk = ytmp[:, 0:512].rearrange("p (t n) -> p t n", t=NT)
        A('dve', 'tensor_copy', sinT[:], cosT[:], r=['cosT'], w=['sinT'])
        A('dve', 'tensor_scalar', cosT[:], cosT[:], 0.5 * math.pi, None, ALU.add, r=['cosT'], w=['cosT'])
        for tab, key in ((sinT, 'sinT'), (cosT, 'cosT')):
            A('dve', 'tensor_scalar', angk[:], tab[:], 1.0 / TWO_PI, None, ALU.mult, r=[key], w=[('ytmp', 'h0')])
            A('dve', 'tensor_copy', angi[:], angk[:], r=[('ytmp', 'h0')], w=[('ytmp', 'h1')])
            A('dve', 'tensor_copy', angk[:], angi[:], r=[('ytmp', 'h1')], w=[('ytmp', 'h0')])
            A('dve', 'scalar_tensor_tensor', tab[:], angk[:], -TWO_PI, tab[:], ALU.mult, ALU.add, r=[('ytmp', 'h0'), key], w=[key])
            A('dve', 'tensor_single_scalar', angk[:], tab[:], math.pi, ALU.is_gt, r=[key], w=[('ytmp', 'h0')])
            A('dve', 'scalar_tensor_tensor', tab[:], angk[:], -TWO_PI, tab[:], ALU.mult, ALU.add, r=[('ytmp', 'h0'), key], w=[key])
            A('dve', 'tensor_single_scalar', angk[:], tab[:], -math.pi, ALU.is_lt, r=[key], w=[('ytmp', 'h0')])
            A('dve', 'scalar_tensor_tensor', tab[:], angk[:], TWO_PI, tab[:], ALU.mult, ALU.add, r=[('ytmp', 'h0'), key], w=[key])
            A('dve', 'tensor_scalar', tab[:], tab[:], math.pi, -math.pi, ALU.min, ALU.max, r=[key], w=[key])
            ACT(tab[:], tab[:], AF.Sin, [key], [key])
        A('dve', 'memset', raw[:, :, 0:3], 0.0, w=['raw'])
        DMA('sp', fm_s.rearrange("(c p) w -> p c w", p=128)[:, :, 0:3], raw[:, :, 0:3], r=['raw'], w=[('fm_s', 'z')], key='fmz')

        def emit_xT(t):
            COPY('act', o_tok[:], xres[:, t, :], [('xres', t)], ['o_tok'])
            pbank = 7
            for kc in range(KC):
                TR(PBb[pbank][:, kc * 128:(kc + 1) * 128], o_tok[:, kc * 128:(kc + 1) * 128], identb[:],
                   ['o_tok', 'identb'], [('pb', pbank)])
            COPY('dve', xT[:, :, t * 128:(t + 1) * 128],
                 PBb[pbank][:, 0:1024].rearrange("p (k n) -> p k n", k=KC), [('pb', pbank)], [('xT', t)])

        def layer_norm(src_ap, src_key, t, lidx):
            for hh in range(2):
                A('dve', 'bn_stats', lnst[:, hh * 6:(hh + 1) * 6], src_ap[:, hh * 512:(hh + 1) * 512],
                  r=[src_key], w=['lnst'])
            A('dve', 'bn_aggr', lnst[:, 12:14], lnst[:, 0:12], r=['lnst'], w=['lnst2'])
            ACT(lnst[:, 15:16], lnst[:, 13:14], AF.Sqrt, ['lnst2', 'cvals'], ['lnst4'], bias=cvals[:, 2:3])
            A('dve', 'reciprocal', lnst[:, 14:15], lnst[:, 15:16], r=['lnst4'], w=['lnst3'])
            A('dve', 'tensor_scalar', ytmp[:], src_ap, lnst[:, 12:13], lnst[:, 14:15], ALU.subtract, ALU.mult,
              r=[src_key, 'lnst2', 'lnst3'], w=['ytmp'])
            A('pool', 'tensor_tensor', ytmp[:], ytmp[:], lnp[:, 0:1024], ALU.mult, r=['ytmp', 'lnp'], w=['ytmp'])
            A('pool', 'tensor_tensor', xres[:, t, :], ytmp[:], lnp[:, 1024:2048], ALU.add,
              r=['ytmp', 'lnp'], w=[('xres', t)])

        for t in range(NT):
            emit_xT(t)

        wq = [0]

        for l in range(nlayers):
            DMA('sp', convw[:], convw_d[l], w=['convw'], key='convw')
            DMA('sp', pv[:], pv_d[l], w=['pv'], key='pv')
            DMA('sp', lnp[:], lnp_d[l, 0], w=['lnp'], key='lnp')
            DMA('pool', wout, w_out_d[l].rearrange("(k p) n -> p k n", p=128), w=['wout'], key='wout')
            alog = pv[:, 0:4]
            dtb = pv[:, 4:8]
            sinks = pv[:, 8:16]
            dnw = pv[:, 16:144]
            w_in_v = w_in_d[l].rearrange("(k p) n -> p k n", p=128)
            blocks = [(0, 512, 'fm'), (512, 1024, 'fm'), (1024, 1536, 'fm'),
                      (1536, 2048, 'tm'), (2048, 2560, 'tm'), (2560, 2824, 'tm')]
            cnt = 0
            for bi, (c0, c1, kind) in enumerate(blocks):
                s_ = wq[0] % 2
                wq[0] += 1
                wkey = 'wa%d' % s_
                wcols = c1 - c0
                DMA('pool', wa[s_][:, :, 0:wcols], w_in_v[:, :, c0:c1], w=[wkey], key=wkey)
                if kind == 'fm':
                    for j in range(4):
                        for tg in range(4):
                            pbk = cnt % 4
                            q_ = cnt % 4
                            cnt += 1
                            for kc in range(KC):
                                MM(PB[pbk][:], wa[s_][:, kc, j * 128:(j + 1) * 128], xT[:, kc, tg * 512:(tg + 1) * 512],
                                   kc == 0, kc == KC - 1, [wkey] + [('xT', tg * 4 + i) for i in range(4)], [('pb', pbk)])
                            COPY(evac_eng(), stg[q_][:], PB[pbk][:], [('pb', pbk)], [('stg', q_)])
                            ch = (c0 // 128) + j
                            DMA('sp', fm_s[ch * 128:(ch + 1) * 128, 3 + tg * 512:3 + (tg + 1) * 512], stg[q_][:],
                                r=[('stg', q_)], w=[('fm_s', ch, tg)], key='stg%d' % q_)
                else:
                    for t in range(NT):
                        pbk = cnt % 4
                        q_ = cnt % 4
                        cnt += 1
                        for kc in range(KC):
                            MM(PB[pbk][:, 0:wcols], xT[:, kc, t * 128:(t + 1) * 128], wa[s_][:, kc, 0:wcols],
                               kc == 0, kc == KC - 1, [wkey, ('xT', t)], [('pb', pbk)])
                        COPY(evac_eng(), stg[q_][:, 0:wcols], PB[pbk][:, 0:wcols], [('pb', pbk)], [('stg', q_)])
                        if c0 == 2048:
                            COPY('act', ab_all[:, t, :], PB[pbk][:, 0:8], [('pb', pbk)], [('ab_all', t)])
                        DMA('sp', tm_s[t * 128:(t + 1) * 128, c0 - 1536:c1 - 1536], stg[q_][:, 0:wcols],
                            r=[('stg', q_)], w=[('tm_s', t, bi)], key='stg%d' % q_)
            v3 = lambda n: pp[n][:].rearrange("p (t h) -> p t h", h=4)
            A('dve', 'tensor_tensor', v3('ta'), ab_all[:, :, 0:4], dtb.unsqueeze(1).to_broadcast([128, NT, 4]), ALU.add,
              r=['ab_all', 'pv'], w=['pp_ta'])
            ACT(pp['ta'][:], pp['ta'][:], AF.Exp, ['pp_ta'], ['pp_ta'])
            ACT(pp['sp'][:], pp['ta'][:], AF.Ln, ['pp_ta', 'cvals'], ['pp_sp'], bias=cvals[:, 1:2])
            ACT(negA[:], alog, AF.Exp, ['pv'], ['negA'])
            A('dve', 'tensor_scalar', negA[:], negA[:], -1.0, None, ALU.mult, r=['negA'], w=['negA'])
            A('dve', 'tensor_tensor', v3('g'), v3('sp'), negA[:].unsqueeze(1).to_broadcast([128, NT, 4]), ALU.mult,
              r=['pp_sp', 'negA'], w=['pp_g'])
            ACT(v3('eb'), ab_all[:, :, 4:8], AF.Exp, ['ab_all'], ['pp_eb'], scale=-1.0)
            A('dve', 'tensor_scalar', pp['eb'][:], pp['eb'][:], 1.0, None, ALU.add, r=['pp_eb'], w=['pp_eb'])
            A('dve', 'reciprocal', pp['beta'][:], pp['eb'][:], r=['pp_eb'], w=['pp_beta'])
            A('dve', 'tensor_scalar', pp['nbeta'][:], pp['beta'][:], -1.0, None, ALU.mult, r=['pp_beta'], w=['pp_nbeta'])
            MM(PB[4][:, 0:64], tri, pp['g'][:], True, True, ['cst', 'pp_g'], [('pb', 4)])
            COPY('dve', pp['gcum'][:], PB[4][:, 0:64], [('pb', 4)], ['pp_gcum'])
            A('dve', 'tensor_scalar', pp['ngc'][:], pp['gcum'][:], -1.0, None, ALU.mult, r=['pp_gcum'], w=['pp_ngc'])
            ACT(pp['egc'][:], pp['gcum'][:], AF.Exp, ['pp_gcum'], ['pp_egc'])
            A('dve', 'tensor_tensor', pp['bege'][:], pp['beta'][:], pp['egc'][:], ALU.mult,
              r=['pp_beta', 'pp_egc'], w=['pp_bege'])
            A('pool', 'memset', Sf[:], 0.0, w=['Sf'])
            A('pool', 'memset', Sb[:], 0.0, w=['Sb'])

            def run_rr(gens):
                gens = list(gens)
                while gens:
                    for g_ in list(gens):
                        try:
                            next(g_)
                        except StopIteration:
                            gens.remove(g_)

            def prologue(t):
                DMA('sp', raw[:], fm_s.rearrange("(c p) w -> p c w", p=128)[:, :, t * 128:t * 128 + 131],
                    r=['fm_s'], w=['raw'], key='raw')
                DMA('sp', tmv[:], tm_s[t * 128:(t + 1) * 128, :], r=[('tm_s', t)], w=['tmv'], key='tmv')
                yield
                for cc in range(12):
                    eng = 'dve'
                    A(eng, 'tensor_scalar', cacc[:, cc, :], raw[:, cc, 0:128], convw[:, cc, 0:1], None, ALU.mult,
                      r=['raw', 'convw'], w=[('cacc', cc)])
                    for k in range(1, 4):
                        if eng == 'dve':
                            A(eng, 'scalar_tensor_tensor', cacc[:, cc, :], raw[:, cc, k:k + 128], convw[:, cc, k:k + 1], cacc[:, cc, :],
                              ALU.mult, ALU.add, r=['raw', 'convw', ('cacc', cc)], w=[('cacc', cc)])
                        else:
                            A(eng, 'tensor_scalar', ptmp[:], raw[:, cc, k:k + 128], convw[:, cc, k:k + 1], None, ALU.mult,
                              r=['raw', 'convw'], w=['ptmp'])
                            A(eng, 'tensor_tensor', cacc[:, cc, :], cacc[:, cc, :], ptmp[:], ALU.add,
                              r=['ptmp', ('cacc', cc)], w=[('cacc', cc)])
                    if cc % 4 == 3:
                        yield
                ACT(yb[:], cacc[:], AF.Silu, ['cacc'], ['yb'])
                ACT(zs[:], tmv[:, 0:512], AF.Silu, ['tmv'], ['zs'])
                yield
                A('pool', 'tensor_tensor', sq[:].rearrange("p (a n) -> p a n", a=8), yb[:, 0:8, :], yb[:, 0:8, :], ALU.mult,
                  r=['yb'], w=['sq'])
                for hh in range(2):
                    bk = 3 + hh
                    MM(PB[bk][:], onesb[:], sq[:, hh * 512:(hh + 1) * 512], True, True, ['onesb', 'sq'], [('pb', bk)])
                    ACT(rn[:, hh * 512:(hh + 1) * 512], PB[bk][:], AF.Sqrt, [('pb', bk), 'cvals'], [('rn', hh)], bias=cvals[:, 3:4])
                    A('dve', 'reciprocal', rn[:, hh * 512:(hh + 1) * 512], rn[:, hh * 512:(hh + 1) * 512],
                      r=[('rn', hh)], w=[('rn', hh)])
                yield
                A('dve', 'tensor_tensor', qkT[:], yb[:, 0:8, :], rn[:].rearrange("p (a n) -> p a n", a=8), ALU.mult,
                  r=['yb', 'rn'], w=['qkT'])
                for h in range(4):
                    TR(PBb[2][:, h * 128:(h + 1) * 128], qkT[:, 4 + h, :], identb[:], ['qkT', 'identb'], [('pb', 2)])
                for h in range(4):
                    TR(PBb[2][:, (4 + h) * 128:(5 + h) * 128], yb[:, 8 + h, :], identb[:], ['yb', 'identb'], [('pb', 2)])
                yield

            def dn_head(t, h, B_):
                ix = t * 4 + h
                col = lambda n: pp[n][:, ix:ix + 1]
                bps, brd = B_['bps'], B_['brd']
                kps, krd = ('pb', bps), ('pb', brd)
                n_ = B_['n']
                K_ = lambda nm: nm + n_
                A('pool', 'tensor_scalar', B_['trg'][:], tri, col('g'), -1.0, ALU.mult, ALU.mult, r=['cst', 'pp_g'], w=[K_('trg')])
                MM(PB[bps][:, 0:128], onesf, B_['trg'][:], True, False, ['cst', K_('trg')], [kps])
                MM(PB[bps][:, 0:128], identf, maskA, False, True, ['cst'], [kps])
                MM(PB[bps][:, 128:256], onesf, B_['trg'][:], True, False, ['cst', K_('trg')], [kps])
                MM(PB[bps][:, 128:256], identf, maskB, False, True, ['cst'], [kps])
                kTh = qkT[:, 4 + h, :]
                qTh = qkT[:, h, :]
                MM(PB[bps][:, 256:384], kTh, kTh, True, True, ['qkT'], [kps])
                MM(PB[bps][:, 384:512], kTh, qTh, True, True, ['qkT'], [kps])
                yield
                dsc_ = B_['dsc']
                ACT(B_['dL'][:], PB[bps][:, 0:128], AF.Exp, [kps, 'pp_gcum'], [K_('dL')], bias=col('gcum'))
                ACT(B_['dAT'][:], PB[bps][:, 128:256], AF.Exp, [kps, 'pp_ngc'], [K_('dAT')], bias=col('ngc'), scale=-1.0)
                ACT(dsc_[:, 0:1], PB[bps][:, 255:256], AF.Exp, [kps, 'pp_ngc'], [K_('dsc')], bias=col('ngc'), scale=-1.0)
                ACT(dsc_[:, 1:2], PB[bps][:, 255:256], AF.Exp, [kps], [K_('dsc')], scale=-1.0)
                abp_ = B_['abp']
                A('dve', 'scalar_tensor_tensor', abp_[0][:, 0:128], PB[bps][:, 256:384], col('nbeta'), B_['dL'][:], ALU.mult, ALU.mult,
                  r=[kps, 'pp_nbeta', K_('dL')], w=[(K_('abp'), 0)])
                A('dve', 'tensor_tensor', B_['ATm'][:], PB[bps][:, 384:512], B_['dAT'][:], ALU.mult, r=[kps, K_('dAT')], w=[K_('ATm')])
                yield
                TR(PB[brd][:, 0:128], abp_[0][:, 0:128], identf, [(K_('abp'), 0), 'cst'], [krd])
                COPY('act', abp_[0][:, 128:256], PB[brd][:, 0:128], [krd], [(K_('abp'), 0)])
                A('pool', 'tensor_copy', abp_[0][:, 256:384], identf, r=['cst'], w=[(K_('abp'), 0)])
                yield
                cur = 0
                for rnd in range(1, 8):
                    Ac = abp_[cur][:, 0:128]
                    Bc = abp_[cur][:, 128:256]
                    Pc = abp_[cur][:, 256:384]
                    nxt = 1 - cur
                    rk = [(K_('abp'), cur), 'cst']
                    if rnd <= 6:
                        MM(PB[brd][:, 0:128], Bc, Ac, True, True, rk, [krd])
                        if rnd <= 5:
                            MM(PB[brd][:, 128:256], Ac, Bc, True, True, rk, [krd])
                    MM(PB[brd][:, 256:384], identf, Pc, True, False, rk, [krd])
                    MM(PB[brd][:, 256:384], Ac, Pc, False, True, rk, [krd])
                    if rnd <= 6:
                        COPY(evac_eng(), abp_[nxt][:, 0:384], PB[brd][:, 0:384], [krd], [(K_('abp'), nxt)])
                    else:
                        COPY(evac_eng(), B_['TT'][:], PB[brd][:, 256:384], [krd], [K_('TT')])
                    cur = nxt
                    yield
                ktok = PBb[2][:, h * 128:(h + 1) * 128]
                vtok = PBb[2][:, (4 + h) * 128:(5 + h) * 128]
                A('dve', 'tensor_scalar', B_['Xv'][:], vtok, col('beta'), None, ALU.mult, r=[('pb', 2), 'pp_beta'], w=[K_('Xv')])
                A('dve', 'tensor_scalar', B_['Xw'][:], ktok, col('bege'), None, ALU.mult, r=[('pb', 2), 'pp_bege'], w=[K_('Xw')])
                A('dve', 'tensor_scalar', B_['ke'][:], ktok, dsc_[:, 0:1], None, ALU.mult, r=[('pb', 2), K_('dsc')], w=[K_('ke')])
                MM(PB[brd][:, 0:128], B_['TT'][:], B_['Xv'][:], True, True, [K_('TT'), K_('Xv')], [krd])
                MM(PB[brd][:, 128:256], B_['Xw'][:], B_['TT'][:], True, True, [K_('TT'), K_('Xw')], [krd])
                yield
                COPY('act', B_['u_sb'][:], PB[brd][:, 0:128], [krd], [K_('u_sb')])
                COPY('act', B_['wT_sb'][:], PB[brd][:, 128:256], [krd], [K_('wT_sb')])
                MM(PB[brd][:, 256:384], B_['wT_sb'][:], Sb[:, h, :], True, True, [K_('wT_sb'), ('Sb', h)], [krd])
                MM(PB[brd][:, 384:512], qTh, Sb[:, h, :], True, True, ['qkT', ('Sb', h)], [krd])
                yield
                A('dve', 'tensor_tensor', B_['vn'][:], B_['u_sb'][:], PB[brd][:, 256:384], ALU.subtract, r=[K_('u_sb'), krd], w=[K_('vn')])
                A('dve', 'tensor_scalar', B_['t1'][:], PB[brd][:, 384:512], col('egc'), None, ALU.mult, r=[krd, 'pp_egc'], w=[K_('t1')])
                MM(PB[bps][:, 0:128], B_['ATm'][:], B_['vn'][:], True, True, [K_('ATm'), K_('vn')], [kps])
                MM(PB[bps][:, 128:256], B_['ke'][:], B_['vn'][:], True, True, [K_('ke'), K_('vn')], [kps])
                yield
                A('dve', 'tensor_tensor', o4[:, h, :], B_['t1'][:], PB[bps][:, 0:128], ALU.add, r=[K_('t1'), kps], w=[('o4', h)])
                A('dve', 'scalar_tensor_tensor', Sf[:, h, :], Sf[:, h, :], dsc_[:, 1:2], PB[bps][:, 128:256], ALU.mult, ALU.add,
                  r=[('Sf', h), K_('dsc'), kps], w=[('Sf', h)])
                COPY('act', Sb[:, h, :], Sf[:, h, :], [('Sf', h)], [('Sb', h)])
                yield

            def dn_stream(t, heads, B_):
                for h in heads:
                    yield from dn_head(t, h, B_)

            def rms_gate(t):
                A('pool', 'tensor_tensor', o4b[:], o4[:], o4[:], ALU.mult, r=['o4'], w=['ytmp'])
                A('dve', 'tensor_reduce', rs4[:, 0:4], o4b[:], AX.X, ALU.add, r=['ytmp'], w=['rs4'])
                ACT(rs4[:, 0:4], rs4[:, 0:4], AF.Sqrt, ['rs4', 'cvals'], ['rs4'], bias=cvals[:, 3:4], scale=1.0 / 16384.0)
                A('dve', 'reciprocal', rs4[:, 0:4], rs4[:, 0:4], r=['rs4'], w=['rs4'])
                A('dve', 'tensor_scalar', rs4[:, 4:8], rs4[:, 0:4], 128.0 ** -0.5, None, ALU.mult, r=['rs4'], w=['rs4b'])
                A('dve', 'tensor_tensor', o4b[:], o4[:], rs4[:, 4:8].unsqueeze(2).to_broadcast([128, 4, 128]), ALU.mult,
                  r=['o4', 'rs4b'], w=['ytmp'])
                A('pool', 'tensor_tensor', o4b[:], o4b[:], dnw.unsqueeze(1).to_broadcast([128, 4, 128]), ALU.mult,
                  r=['ytmp', 'pv'], w=['ytmp'])
                A('dve', 'tensor_tensor', o_tok[:, 0:512].rearrange("p (h d) -> p h d", h=4), o4b[:],
                  zs[:].rearrange("p (h d) -> p h d", h=4), ALU.mult, r=['ytmp', 'zs'], w=[('o_tok', 0)])

            def swa(t):
                def rope(src, nh, dst4, keyw):
                    s4 = src.rearrange("p (h two d) -> p h two d", two=2, d=32)
                    x1 = s4[:, :, 0, :]
                    x2 = s4[:, :, 1, :]
                    cs = cosT[:, t, :].unsqueeze(1).to_broadcast([128, nh, 32])
                    sn = sinT[:, t, :].unsqueeze(1).to_broadcast([128, nh, 32])
                    w4 = lambda i: ytmp[:, i * 256:i * 256 + nh * 32].rearrange("p (h d) -> p h d", d=32)
                    A('pool', 'tensor_tensor', w4(0), x1, cs, ALU.mult, r=['tmv', 'cosT'], w=[('ytmp', 0)])
                    A('pool', 'tensor_tensor', w4(1), x2, sn, ALU.mult, r=['tmv', 'sinT'], w=[('ytmp', 1)])
                    A('pool', 'tensor_tensor', w4(2), x2, cs, ALU.mult, r=['tmv', 'cosT'], w=[('ytmp', 2)])
                    A('pool', 'tensor_tensor', w4(3), x1, sn, ALU.mult, r=['tmv', 'sinT'], w=[('ytmp', 3)])
                    A('pool', 'tensor_tensor', dst4[0], w4(0), w4(1), ALU.subtract, r=[('ytmp', 0), ('ytmp', 1)], w=[keyw])
                    A('pool', 'tensor_tensor', dst4[1], w4(2), w4(3), ALU.add, r=[('ytmp', 2), ('ytmp', 3)], w=[keyw])
                qv = qr[:].rearrange("p (pr two half d) -> p pr two half d", pr=4, two=2, half=2)
                for two in range(2):
                    srcq = tmv[:, 520 + two * 256:520 + (two + 1) * 256]
                    rope(srcq, 4, (qv[:, :, two, 0, :], qv[:, :, two, 1, :]), 'qr')
                    yield
                kv4 = kr[:].rearrange("p (h half d) -> p h half d", h=2, half=2)
                rope(tmv[:, 1032:1160], 2, (kv4[:, :, 0, :], kv4[:, :, 1, :]), 'kr')
                COPY('act', va[:, t, :], tmv[:, 1160:1288], ['tmv'], [('va', t)])
                yield
                for pr in range(4):
                    TR(PBb[7][:, pr * 128:(pr + 1) * 128], qr[:, pr * 128:(pr + 1) * 128], identb[:], ['qr', 'identb'], [('pb', 7)])
                TR(PBb[7][:, 512:640], kr[:], identb[:], ['kr', 'identb'], [('pb', 7)])
                COPY('act', qTs[:], PBb[7][:, 0:512].rearrange("p (a n) -> p a n", a=4), [('pb', 7)], ['qTs'])
                COPY('act', kTa[:, t * 128:(t + 1) * 128], PBb[7][:, 512:640], [('pb', 7)], [('kTa', t)])
                yield
                for g_ in range(2):
                    ps_ = slice(g_ * 64, (g_ + 1) * 64)
                    if t == 0:
                        k0, nk = 0, 128
                        mkb = mask8b[:, 128:256]
                        kkeys = [('kTa', 0)]
                    else:
                        k0, nk = (t - 1) * 128, 256
                        mkb = mask8b[:]
                        kkeys = [('kTa', t - 1), ('kTa', t)]
                    nb = nk // 128
                    for j in range(4):
                        bk = j // 2
                        c0_ = (j % 2) * 256
                        MM(PB[bk][:, c0_:c0_ + nk], identb[:], mkb, True, False, ['identb', 'mask8b'], [('pb', bk)])
                        MM(PB[bk][:, c0_:c0_ + nk], qTs[ps_, j, :], kTa[ps_, k0:k0 + nk], False, True, ['qTs'] + kkeys, [('pb', bk)])
                    yield
                    for bk in range(2):
                        A('dve', 'tensor_reduce', st8[:, bk * 2:bk * 2 + 2],
                          PB[bk][:].rearrange("p (a n) -> p a n", a=2)[:, :, 0:nk], AX.X, ALU.max, r=[('pb', bk)], w=[('st8', 'mx')])
                    A('dve', 'scalar_tensor_tensor', st8[:, 4:8], st8[:, 0:4], 0.125, sinks[:, g_ * 4:g_ * 4 + 4], ALU.mult, ALU.max,
                      r=[('st8', 'mx'), 'pv'], w=[('st8', 'm')])
                    A('dve', 'tensor_scalar', st8[:, 8:12], st8[:, 4:8], -1.0, None, ALU.mult, r=[('st8', 'm')], w=[('st8', 'nm')])
                    A('dve', 'tensor_tensor', st8[:, 12:16], st8[:, 8:12], sinks[:, g_ * 4:g_ * 4 + 4], ALU.add,
                      r=[('st8', 'nm'), 'pv'], w=[('st8', 'sk')])
                    yield
                    for j in range(4):
                        bk = j // 2
                        c0_ = (j % 2) * 256
                        ACT(pb_[:, j * 256:j * 256 + nk], PB[bk][:, c0_:c0_ + nk], AF.Exp, [('pb', bk), ('st8', 'nm')],
                            [('pb_', j), ('st8', 'rs', j)], bias=st8[:, 8 + j:9 + j], scale=0.125, accum_out=st8[:, 16 + j:17 + j])
                    ACT(st8[:, 12:16], st8[:, 12:16], AF.Exp, [('st8', 'sk')], [('st8', 'sk')])
                    yield
                    for j in range(4):
                        for b_ in range(nb):
                            TR(PBb[7][:, j * 256 + b_ * 128:j * 256 + (b_ + 1) * 128], pb_[:, j * 256 + b_ * 128:j * 256 + (b_ + 1) * 128],
                               identb[:], [('pb_', j), 'identb'], [('pb', 7)])
                    A('dve', 'tensor_tensor', st8[:, 20:24], st8[:, 16:20], st8[:, 12:16], ALU.add,
                      r=[('st8', 'rs'), ('st8', 'sk')], w=[('st8', 'den')])
                    A('dve', 'reciprocal', st8[:, 20:24], st8[:, 20:24], r=[('st8', 'den')], w=[('st8', 'den')])
                    yield
                    if nb == 2:
                        COPY('act', pTb[:, 0:512], PBb[7][:, 0:512], [('pb', 7)], [('pTb', 0)])
                        COPY('dve', pTb[:, 512:1024], PBb[7][:, 512:1024], [('pb', 7)], [('pTb', 1)])
                    else:
                        for j in range(4):
                            COPY('act' if j % 2 == 0 else 'dve', pTb[:, j * 256:j * 256 + 128], PBb[7][:, j * 256:j * 256 + 128],
                                 [('pb', 7)], [('pTb', j // 2)])
                    yield
                    for j in range(4):
                        for b_ in range(nb):
                            tt_ = t - (nb - 1) + b_
                            MM(PB[0][:, j * 64:(j + 1) * 64], pTb[:, j * 256 + b_ * 128:j * 256 + (b_ + 1) * 128],
                               va[:, tt_, g_ * 64:(g_ + 1) * 64], b_ == 0, b_ == nb - 1, [('pTb', j // 2), ('va', tt_)], [('pb', 0)])
                    yield
                    A('dve', 'tensor_tensor', o_tok[:, 512 + g_ * 256:512 + (g_ + 1) * 256].rearrange("p (j d) -> p j d", j=4),
                      PB[0][:, 0:256].rearrange("p (j d) -> p j d", j=4),
                      st8[:, 20:24].unsqueeze(2).to_broadcast([128, 4, 64]), ALU.mult,
                      r=[('pb', 0), ('st8', 'den')], w=[('o_tok', 1, g_)])
                    yield

            def epilogue(t):
                if dbg and l == nlayers - 1:
                    COPY('dve', ytmp[:], o_tok[:], ['o_tok'], ['ytmp'])
                    DMA('sp', dbg_o[t * 128:(t + 1) * 128, :], ytmp[:], r=['ytmp'], key='dbg_o')
                    COPY('dve', ytmp[:, 0:512].rearrange("p (h d) -> p h d", h=4), o4[:], ['o4'], ['ytmp'])
                    DMA('sp', dbg_raw[t * 128:(t + 1) * 128, :], ytmp[:, 0:512], r=['ytmp'], key='dbg_raw')
                for kc in range(KC):
                    TR(PBb[7][:, kc * 128:(kc + 1) * 128], o_tok[:, kc * 128:(kc + 1) * 128], identb[:], ['o_tok', 'identb'], [('pb', 7)])
                COPY('act', oT[:], PBb[7][:, 0:1024].rearrange("p (k n) -> p k n", k=KC), [('pb', 7)], ['oT'])
                yield
                for hh in range(2):
                    for kc in range(KC):
                        MM(PB[hh][:], oT[:, kc, :], wout[:, kc, hh * 512:(hh + 1) * 512], kc == 0, kc == KC - 1,
                           ['oT', 'wout'], [('pb', hh)])
                    A('dve', 'scalar_tensor_tensor', ytmp[:, hh * 512:(hh + 1) * 512], xres[:, t, hh * 512:(hh + 1) * 512], ALPHA,
                      PB[hh][:], ALU.mult, ALU.add, r=[('xres', t), ('pb', hh)], w=['ytmp'])
                    yield
                layer_norm(ytmp[:], 'ytmp', t, 0)
                yield
                emit_xT(t)
                if dbg and l == nlayers - 1:
                    DMA('sp', dbg_x1[t * 128:(t + 1) * 128, :], xres[:, t, :], r=[('xres', t)], key='dbg_x1')
                yield

            run_rr([prologue(0)])
            for t in range(NT):
                run_rr([dn_stream(t, [0, 2], DN[0]), dn_stream(t, [1, 3], DN[1]), swa(t)])
                rms_gate(t)
                if t + 1 < NT:
                    run_rr([epilogue(t), prologue(t + 1)])
                else:
                    run_rr([epilogue(t)])

            DMA('sp', lnp[:], lnp_d[l, 1], w=['lnp'], key='lnp')
            moe = (l % 2 == 1)
            li = l // 2
            if moe:
                DMA('sp', rwb, rwb_d[li], w=['rwb'], key='rwb')
                for t in range(NT):
                    for e_ in range(NE):
                        A('pool', 'tensor_tensor', ytmp[:], xres[:, t, :], rwb[:, e_, :], ALU.mult, r=[('xres', t), 'rwb'],
                          w=['ytmp'])
                        ACT(sq[:], ytmp[:], AF.Copy, ['ytmp'], ['sq', ('lg', t, e_)], accum_out=lg[:, t, e_:e_ + 1])
                bc = lambda a: a[:].unsqueeze(2).to_broadcast([128, NT, NE])
                A('dve', 'tensor_reduce', ms['m1'][:], lg[:], AX.X, ALU.max, r=['lg'], w=['ms_m1'])
                A('dve', 'tensor_tensor', mt['eq1'][:], lg[:], bc(ms['m1']), ALU.is_equal, r=['lg', 'ms_m1'], w=['mt_eq1'])
                A('dve', 'scalar_tensor_tensor', mt['l2'][:], mt['eq1'][:], -1e30, lg[:], ALU.mult, ALU.add,
                  r=['mt_eq1', 'lg'], w=['mt_l2'])
                A('dve', 'tensor_reduce', ms['m2'][:], mt['l2'][:], AX.X, ALU.max, r=['mt_l2'], w=['ms_m2'])
                A('dve', 'tensor_tensor', mt['eq2'][:], mt['l2'][:], bc(ms['m2']), ALU.is_equal, r=['mt_l2', 'ms_m2'], w=['mt_eq2'])
                A('dve', 'tensor_tensor', ms['d'][:], ms['m2'][:], ms['m1'][:], ALU.subtract, r=['ms_m1', 'ms_m2'], w=['ms_d'])
                ACT(ms['ed'][:], ms['d'][:], AF.Exp, ['ms_d'], ['ms_ed'])
                A('dve', 'tensor_scalar', ms['g1'][:], ms['ed'][:], 1.0, None, ALU.add, r=['ms_ed'], w=['ms_g1'])
                A('dve', 'reciprocal', ms['g1'][:], ms['g1'][:], r=['ms_g1'], w=['ms_g1'])
                A('dve', 'tensor_tensor', ms['g2'][:], ms['ed'][:], ms['g1'][:], ALU.mult, r=['ms_ed', 'ms_g1'], w=['ms_g2'])
                A('dve', 'tensor_tensor', mt['eq1'][:], mt['eq1'][:], bc(ms['g1']), ALU.mult, r=['mt_eq1', 'ms_g1'], w=['mt_eq1'])
                A('dve', 'tensor_tensor', mt['eq2'][:], mt['eq2'][:], bc(ms['g2']), ALU.mult, r=['mt_eq2', 'ms_g2'], w=['mt_eq2'])
                A('dve', 'tensor_tensor', comb[:], mt['eq1'][:], mt['eq2'][:], ALU.add, r=['mt_eq1', 'mt_eq2'], w=['comb'])
            nexp = NE if moe else 1
            for t in range(NT):
                ACT(xres[:, t, :], xres[:, t, :], AF.Copy, [('xres', t)], [('xres', t)], scale=ALPHA)
            for e_ in range(nexp):
                if moe:
                    Wg, Wu, Wd = mg_d[li, e_], mu_d[li, e_], md_d[li, e_]
                else:
                    Wg, Wu, Wd = fg_d[li], fu_d[li], fd_d[li]
                Wgv = Wg.rearrange("(k p) f -> p k f", p=128)
                Wuv = Wu.rearrange("(k p) f -> p k f", p=128)
                Wdv = Wd.rearrange("(f p) d -> p f d", p=128)
                for grp in range(7):
                    s_ = wq[0] % 2
                    wq[0] += 1
                    kg, ku, kd = 'wg%d' % s_, 'wu%d' % s_, 'wd%d' % s_
                    DMA('pool', wg[s_], Wgv[:, :, grp * 512:(grp + 1) * 512], w=[kg], key=kg)
                    DMA('pool', wu[s_], Wuv[:, :, grp * 512:(grp + 1) * 512], w=[ku], key=ku)
                    DMA('pool', wd[s_], Wdv[:, grp * 4:(grp + 1) * 4, :], w=[kd], key=kd)
                    ci = 0
                    for fc in range(4):
                        for tg in range(4):
                            bg = (ci % 2) * 2
                            bu = bg + 1
                            ci += 1
                            xk = [('xT', tg * 4 + i) for i in range(4)]
                            for kc in range(KC):
                                MM(PB[bg][:], wg[s_][:, kc, fc * 128:(fc + 1) * 128], xT[:, kc, tg * 512:(tg + 1) * 512],
                                   kc == 0, kc == KC - 1, [kg] + xk, [('pb', bg)])
                            for kc in range(KC):
                                MM(PB[bu][:], wu[s_][:, kc, fc * 128:(fc + 1) * 128], xT[:, kc, tg * 512:(tg + 1) * 512],
                                   kc == 0, kc == KC - 1, [ku] + xk, [('pb', bu)])
                            sg = sgt[ci % 2]
                            ACT(sg[:], PB[bg][:], AF.Silu, [('pb', bg)], [('ytmp', 'h%d' % (ci % 2))])
                            A('dve', 'tensor_tensor', hT[:, fc, tg * 512:(tg + 1) * 512], sg[:], PB[bu][:], ALU.mult,
                              r=[('ytmp', 'h%d' % (ci % 2)), ('pb', bu)], w=[('hT', fc, tg)])
                    di = 0
                    for t in range(NT):
                        for hh in range(2):
                            bd = 4 + (di % 4)
                            di += 1
                            for fc in range(4):
                                MM(PB[bd][:], hT[:, fc, t * 128:(t + 1) * 128], wd[s_][:, fc, hh * 512:(hh + 1) * 512],
                                   fc == 0, fc == 3, [('hT', fc, t // 4), kd], [('pb', bd)])
                            cs_ = comb[:, t, e_:e_ + 1] if moe else 1.0
                            A('dve', 'scalar_tensor_tensor', xres[:, t, hh * 512:(hh + 1) * 512], PB[bd][:], cs_,
                              xres[:, t, hh * 512:(hh + 1) * 512], ALU.mult, ALU.add,
                              r=[('pb', bd), ('xres', t)] + (['comb'] if moe else []), w=[('xres', t)])
            for t in range(NT):
                layer_norm(xres[:, t, :], ('xres', t), t, 1)
                if l < nlayers - 1:
                    emit_xT(t)
        fin = DMA('sp', y_d.rearrange("(n p) d -> p n d", p=128), xres[:], r=['xres'], key='yout')
        Sc.emit(final_wait_ops=[fin])
        print("ops:", Sc.nops, {e: len(v) for e, v in Sc.eops.items()})
    return nc


_CONSTS = None


def host_consts():
    c = np.zeros((128, 928), np.float32)
    i = np.arange(128)
    c[:, 0:128] = np.eye(128, dtype=np.float32)
    c[:, 128:256] = (i[:, None] <= i[None, :]).astype(np.float32)
    c[:, 256:384] = 1.0
    c[:, 384:512] = np.where(i[None, :] >= i[:, None], -BIG, 0.0)
    c[:, 512:640] = np.where(i[None, :] < i[:, None], BIG, 0.0)
    j = np.arange(256)
    rel = i[:, None] + 128 - j[None, :]
    c[:, 640:896] = np.where((rel >= 0) & (rel < 128), 0.0, -BIG)
    c[:, 896:928] = (10000.0 ** (-np.arange(0, 64, 2, dtype=np.float32) / 64.0)).astype(np.float32)[None, :]
    return c


def prep_inputs(inputs):
    f32 = lambda a: np.ascontiguousarray(np.asarray(a, dtype=np.float32))
    conv_w = f32(inputs["conv_w"])
    convw = np.ascontiguousarray(conv_w.reshape(DEPTH, 4, 12, 128).transpose(0, 3, 2, 1))
    pvrow = np.concatenate([f32(inputs["a_log"]), f32(inputs["dt_bias"]), f32(inputs["sinks"]),
                            f32(inputs["dn_norm_w"])], axis=1)
    pv = np.ascontiguousarray(np.broadcast_to(pvrow[:, None, :], (DEPTH, 128, 144)))
    lnrow = np.concatenate([f32(inputs["ln_g"]), f32(inputs["ln_b"])], axis=2)
    lnp = np.ascontiguousarray(np.broadcast_to(lnrow[:, :, None, :], (DEPTH, 2, 128, 2048)))
    rw = f32(inputs["router_w"]).transpose(0, 2, 1)
    rwb = np.ascontiguousarray(np.broadcast_to(rw[:, None, :, :], (2, 128, NE, D)))
    shared = {
        "w_in": f32(inputs["w_in"]), "w_out": f32(inputs["w_out"]),
        "ffn_w_gate": f32(inputs["ffn_w_gate"]), "ffn_w_up": f32(inputs["ffn_w_up"]),
        "ffn_w_down": f32(inputs["ffn_w_down"]),
        "moe_w_gate": f32(inputs["moe_w_gate"]), "moe_w_up": f32(inputs["moe_w_up"]),
        "moe_w_down": f32(inputs["moe_w_down"]),
        "convw": convw, "pv": pv, "lnp": lnp, "rwb": rwb, "cst": host_consts(),
    }
    return shared


def kernel(**inputs):
    shared = prep_inputs(inputs)
    x = np.asarray(inputs["x"], dtype=np.float32)
    pos = np.asarray(inputs["positions"], dtype=np.int32)
    nb = x.shape[0]
    nc = build()
    in_maps = []
    for b in range(nb):
        m = dict(shared)
        m["x"] = np.ascontiguousarray(x[b])
        m["pos"] = np.ascontiguousarray(pos[b].reshape(NT, 128).T)
        in_maps.append(m)
    res = run_bass_kernel_spmd(nc, in_maps, core_ids=list(range(nb)))
    return np.stack([np.asarray(r["y"], dtype=np.float32) for r in res.results], axis=0)
```

```python
import math
from contextlib import ExitStack
import numpy as np
import concourse.bass as bass
import concourse.mybir as mybir
from concourse.bass_utils import run_bass_kernel_spmd

F32 = mybir.dt.float32
BF16 = mybir.dt.bfloat16
I32 = mybir.dt.int32
ALU = mybir.AluOpType
AF = mybir.ActivationFunctionType
AX = mybir.AxisListType

CE = ['pe', 'act', 'dve', 'pool', 'sp']
EIDX = {e: i for i, e in enumerate(CE)}
EPOCH = 12000

DEPTH = 4
S = 2048
NT = 16
D = 1024
KC = 8
INC = 2824
DFF = 3584
NE = 8
ALPHA = (2.0 * DEPTH) ** 0.25
TMW = 1288
BIG = 30000.0


class Op:
    __slots__ = ('eng', 'fn', 'is_dma', 'dma_key', 'eidx', 'signal', 'sem', 'val', 'waits', 'vc', 'dvc')


class Sched:
    def __init__(self, nc):
        self.nc = nc
        self.eops = {e: [] for e in CE}
        self.last_writer = {}
        self.readers = {}
        self.desc = {}
        self.known = set()
        self.aliases = {}
        self.dma_cnt = {}
        self.last = {e: None for e in CE}
        self.nops = 0

    def alias(self, a, b):
        self.aliases.setdefault(a, set()).add(b)
        self.aliases.setdefault(b, set()).add(a)

    def _reg(self, k):
        if k in self.known:
            return
        self.known.add(k)
        for i in range(1, len(k)):
            self.desc.setdefault(k[:i], set()).add(k)
        self.desc.setdefault(k, set())

    def _related(self, k):
        out = [k]
        for i in range(1, len(k)):
            out.append(k[:i])
        out.extend(self.desc.get(k, ()))
        for r in self.aliases.get(k[0], ()):
            out.append((r,))
            out.extend(self.desc.get((r,), ()))
        return out

    def add(self, eng, fn, reads=(), writes=(), dma_key=None):
        reads = [k if isinstance(k, tuple) else (k,) for k in reads]
        writes = [k if isinstance(k, tuple) else (k,) for k in writes]
        writes = writes + [k for k in reads if k[0] == 'pb' and k not in writes]
        for k in reads:
            self._reg(k)
        for k in writes:
            self._reg(k)
        op = Op()
        op.eng = eng
        op.fn = fn
        op.is_dma = dma_key is not None
        op.dma_key = dma_key
        op.eidx = len(self.eops[eng])
        op.signal = False
        op.sem = None
        op.val = None
        deps = {}
        for b in reads:
            for k in self._related(b):
                w = self.last_writer.get(k)
                if w is not None:
                    deps[id(w)] = (w, 'raw')
        for b in writes:
            for k in self._related(b):
                w = self.last_writer.get(k)
                if w is not None and id(w) not in deps:
                    deps[id(w)] = (w, 'waw')
                for r in self.readers.get(k, {}).values():
                    if id(r) not in deps:
                        deps[id(r)] = (r, 'war')
        prev = self.last[eng]
        vc = list(prev.vc) if prev is not None else [-1] * len(CE)
        dvc = dict(prev.dvc) if prev is not None else {}
        best = {}
        for d, kind in deps.values():
            if d is op:
                continue
            if d.is_dma:
                if dvc.get(d.dma_key, 0) >= d.val:
                    continue
                key = ('d', d.dma_key)
                cur = best.get(key)
                if cur is None or d.val > cur.val:
                    best[key] = d
            else:
                if d.eng == eng and not op.is_dma and kind != 'raw':
                    continue
                if vc[EIDX[d.eng]] >= d.eidx:
                    continue
                key = ('c', d.eng)
                cur = best.get(key)
                if cur is None or d.eidx > cur.eidx:
                    best[key] = d
        op.waits = list(best.values())
        for d in op.waits:
            if d.is_dma:
                dvc[d.dma_key] = max(dvc.get(d.dma_key, 0), d.val)
            else:
                d.signal = True
                vc[EIDX[d.eng]] = max(vc[EIDX[d.eng]], d.eidx)
            for i, v in enumerate(d.vc):
                if v > vc[i]:
                    vc[i] = v
            for k, v in d.dvc.items():
                if v > dvc.get(k, 0):
                    dvc[k] = v
        if op.is_dma:
            c = self.dma_cnt.get(dma_key, 0) + 1
            self.dma_cnt[dma_key] = c
            op.val = 16 * c
        op.vc = vc
        op.dvc = dvc
        slot = ('d', dma_key) if op.is_dma else eng
        for b in reads:
            self.readers.setdefault(b, {})[slot] = op
        for b in writes:
            self.last_writer[b] = op
            self.readers[b] = {}
        self.eops[eng].append(op)
        self.last[eng] = op
        self.nops += 1
        return op

    def emit(self, final_wait_ops=()):
        nc = self.nc
        with ExitStack() as st:
            for e in CE:
                cnt = 0
                sem = None
                ep = 0
                for op in self.eops[e]:
                    if op.is_dma or not op.signal:
                        continue
                    if sem is None or cnt >= EPOCH:
                        sem = st.enter_context(nc.semaphore("s_%s_%d" % (e, ep)))
                        ep += 1
                        cnt = 0
                    cnt += 1
                    op.sem = sem
                    op.val = cnt
            dsem = {}
            for i, k in enumerate(self.dma_cnt):
                dsem[k] = st.enter_context(nc.semaphore("d_%d" % i))
            for e in CE:
                for op in self.eops[e]:
                    if op.is_dma:
                        op.sem = dsem[op.dma_key]
            block = st.enter_context(nc.Block())

            def run(engname):
                def body(eng):
                    for op in self.eops[engname]:
                        for d in op.waits:
                            eng.wait_ge(d.sem, d.val)
                        ins = op.fn(eng)
                        if op.is_dma:
                            ins.then_inc(op.sem, 16)
                        elif op.signal:
                            ins.then_inc(op.sem, 1)
                    if engname == 'sp':
                        for d in final_wait_ops:
                            eng.wait_ge(d.sem, d.val)
                return body

            block.tensor(run('pe'))
            block.scalar(run('act'))
            block.vector(run('dve'))
            block.gpsimd(run('pool'))
            block.sync(run('sp'))


def build(nlayers=DEPTH, dbg=False):
    nc = bass.Bass("TRN2", target_bir_lowering=False)
    dram = lambda name, shape, dt, kind="ExternalInput": nc.dram_tensor(name, shape, dt, kind=kind).ap()
    x_d = dram("x", [S, D], F32)
    pos_d = dram("pos", [128, NT], I32)
    w_in_d = dram("w_in", [DEPTH, D, INC], F32)
    w_out_d = dram("w_out", [DEPTH, D, D], F32)
    fg_d = dram("ffn_w_gate", [2, D, DFF], F32)
    fu_d = dram("ffn_w_up", [2, D, DFF], F32)
    fd_d = dram("ffn_w_down", [2, DFF, D], F32)
    mg_d = dram("moe_w_gate", [2, NE, D, DFF], F32)
    mu_d = dram("moe_w_up", [2, NE, D, DFF], F32)
    md_d = dram("moe_w_down", [2, NE, DFF, D], F32)
    convw_d = dram("convw", [DEPTH, 128, 12, 4], F32)
    pv_d = dram("pv", [DEPTH, 128, 144], F32)
    lnp_d = dram("lnp", [DEPTH, 2, 128, 2048], F32)
    rw_d = dram("rw", [2, 128, KC, NE], F32)
    cst_d = dram("cst", [128, 928], F32)
    y_d = dram("y", [S, D], F32, kind="ExternalOutput")
    fm_s = dram("fm_s", [1536, 3 + S], F32, kind="Internal")
    tm_s = dram("tm_s", [S, TMW], F32, kind="Internal")
    if dbg:
        dbg_o = dram("dbg_o", [S, D], F32, kind="ExternalOutput")
        dbg_x1 = dram("dbg_x1", [S, D], F32, kind="ExternalOutput")
        dbg_raw = dram("dbg_raw", [S, 512], F32, kind="ExternalOutput")

    st = ExitStack()
    with st:
        sb = lambda name, shape, dt: st.enter_context(nc.sbuf_tensor("s_" + name, shape, dt))
        xres = sb("xres", [128, NT, D], F32)
        xT = sb("xT", [128, KC, S], BF16)
        R = sb("R", [128, 32768], BF16)
        wa = [R[:, s * 4096:(s + 1) * 4096].rearrange("p (k n) -> p k n", k=KC) for s in range(2)]
        wout = R[:, 8192:16384].rearrange("p (k n) -> p k n", k=KC)
        wg = [R[:, s * 4096:(s + 1) * 4096].rearrange("p (k n) -> p k n", k=KC) for s in range(2)]
        wu = [R[:, 8192 + s * 4096:8192 + (s + 1) * 4096].rearrange("p (k n) -> p k n", k=KC) for s in range(2)]
        wd = [R[:, 16384 + s * 4096:16384 + (s + 1) * 4096].rearrange("p (f n) -> p f n", f=4) for s in range(2)]
        hT = R[:, 24576:32768].rearrange("p (f n) -> p f n", f=4)
        rwb = R[:, 16384:32768].bitcast(F32).rearrange("p (e d) -> p e d", e=NE)
        rv = lambda a, n: R[:, a:a + n]
        rf = lambda a, n: R[:, a:a + n].bitcast(F32)
        kTa = rv(0, 2048)
        va = rv(2048, 2048).rearrange("p (t n) -> p t n", t=NT)
        Sf = rf(4096, 1024).rearrange("p (h n) -> p h n", h=4)
        Sb = rv(5120, 512).rearrange("p (h n) -> p h n", h=4)
        o4 = rf(5632, 1024).rearrange("p (h n) -> p h n", h=4)
        zs = rv(6656, 512)
        qr = rv(7168, 512)
        qTs = rv(7680, 512).rearrange("p (a n) -> p a n", a=4)
        stg = [rf(16384 + i * 1024, 1024) for i in range(4)]
        raw = rf(16384, 3144).rearrange("p (c w) -> p c w", c=12)
        cacc = rf(19528, 3072).rearrange("p (c w) -> p c w", c=12)
        yb = rv(25672, 1536).rearrange("p (c w) -> p c w", c=12)
        rn = rf(27208, 2048)
        tmv = rf(29256, 2576)
        cst = sb("cst", [128, 928], F32)
        identf = cst[:, 0:128]
        tri = cst[:, 128:256]
        onesf = cst[:, 256:384]
        maskA = cst[:, 384:512]
        maskB = cst[:, 512:640]
        swam = cst[:, 640:896]
        invf = cst[:, 896:928]
        identb = sb("identb", [128, 128], BF16)
        onesb = sb("onesb", [128, 128], BF16)
        cvals = sb("cvals", [128, 4], F32)
        posi = sb("posi", [128, NT], I32)
        posf = sb("posf", [128, NT], F32)
        cosT = sb("cosT", [128, NT, 32], F32)
        sinT = sb("sinT", [128, NT, 32], F32)
        lnp = sb("lnp", [128, 2048], F32)
        convw = sb("convw", [128, 12, 4], F32)
        pv = sb("pv", [128, 144], F32)
        ab_all = sb("ab_all", [128, NT, 8], F32)
        pp = {n: sb("pp_" + n, [128, 64], F32) for n in
              ['ta', 'sp', 'g', 'eb', 'beta', 'nbeta', 'gcum', 'ngc', 'egc', 'bege']}
        negA = sb("negA", [128, 4], F32)
        sq = sb("sq", [128, 1024], BF16)
        qkT = sb("qkT", [128, 8, 128], BF16)
        ytmp = sb("ytmp", [128, D], F32)
        ptmp = None
        o4b = ytmp[:, 0:512].rearrange("p (h n) -> p h n", h=4)
        sgt = [ytmp[:, 0:512], ytmp[:, 512:1024]]
        kr = sb("kr", [128, 128], BF16)
        pb_ = sb("pb_", [128, 1024], BF16)
        mask8b = sb("mask8b", [128, 256], BF16)
        pTb = sb("pTb", [128, 1024], BF16)
        st8 = sb("st8", [128, 24], F32)
        o_tok = sb("o_tok", [128, D], BF16)
        oT = sb("oT", [128, KC, 128], BF16)
        lnst = sb("lnst", [128, 16], F32)
        DN = []
        for i_ in range(2):
            n_ = str(i_)
            B_ = {'n': n_, 'bps': 3 + 2 * i_, 'brd': 4 + 2 * i_}
            if i_ == 0:
                f32t = lambda nm, w_: sb(nm + n_, [128, w_], F32)
                b16t = lambda nm, w_: sb(nm + n_, [128, w_], BF16)
            else:
                off = [22600]
                offb = [31832]
                def f32t(nm, w_):
                    a_ = rf(off[0], 2 * w_)
                    off[0] += 2 * w_
                    return a_
                def b16t(nm, w_):
                    a_ = rv(offb[0], w_)
                    offb[0] += w_
                    return a_
            B_['trg'] = f32t('trg', 128)
            B_['dL'] = f32t('dL', 128)
            B_['dAT'] = f32t('dAT', 128)
            B_['dsc'] = f32t('dsc', 4)
            B_['abp'] = [f32t('abpa', 384), f32t('abpb', 384)]
            B_['u_sb'] = f32t('u_sb', 128)
            B_['t1'] = f32t('t1', 128)
            B_['ATm'] = b16t('ATm', 128)
            B_['TT'] = b16t('TT', 128)
            B_['Xv'] = b16t('Xv', 128)
            B_['Xw'] = b16t('Xw', 128)
            B_['ke'] = b16t('ke', 128)
            B_['wT_sb'] = b16t('wT_sb', 128)
            B_['vn'] = b16t('vn', 128)
            if i_ == 1:
                assert off[0] <= 25672 and offb[0] <= 32768, (off[0], offb[0])
            DN.append(B_)
        rs4 = sb("rs4", [128, 8], F32)
        lg = sb("lg", [128, NT, NE], F32)
        rw32 = sb("rw32", [128, KC, NE], F32)
        comb = sb("comb", [128, NT, NE], F32)
        mt = {n: sb("mt_" + n, [128, NT, NE], F32) for n in ['eq1', 'l2', 'eq2']}
        ms = {n: sb("ms_" + n, [128, NT], F32) for n in ['m1', 'm2', 'd', 'ed', 'g1', 'g2']}
        PB = [st.enter_context(nc.psum_tensor("pb%d" % i, [128, 512], F32)) for i in range(8)]
        PBb = [p[:].bitcast(BF16) for p in PB]

        print("sbuf bytes remaining:", nc.sbuf_bytes_remaining)
        Sc = Sched(nc)
        def alias_groups(groups):
            for i in range(len(groups)):
                for j in range(i + 1, len(groups)):
                    for a_ in groups[i]:
                        for b_ in groups[j]:
                            Sc.alias(a_, b_)
        alias_groups([['wa0', 'wa1'], ['kTa', 'va', 'Sf', 'Sb', 'o4', 'zs', 'qr', 'qTs'], ['wg0', 'wg1']])
        alias_groups([['wout'], ['wu0', 'wu1']])
        set1 = [nm + '1' for nm in ['trg', 'dL', 'dAT', 'dsc', 'abp', 'u_sb', 't1', 'ATm', 'TT', 'Xv', 'Xw', 'ke', 'wT_sb', 'vn']]
        alias_groups([['stg'], ['raw', 'cacc', 'yb', 'rn', 'tmv'] + set1, ['wd0', 'wd1', 'hT'], ['rwb']])

        def A(eng, meth, *args, r=(), w=(), **kw):
            return Sc.add(eng, lambda e: getattr(e, meth)(*args, **kw), reads=r, writes=w)

        def DMA(eng, out, in_, r=(), w=(), key=None):
            return Sc.add(eng, lambda e: e.dma_start(out=out, in_=in_), reads=r, writes=w, dma_key=key)

        def MM(out, lhsT, rhs, start, stop, r, w):
            return Sc.add('pe', lambda e: e.matmul(out, lhsT, rhs, start=start, stop=stop), reads=r, writes=w)

        def TR(out, in_, ident, r, w):
            return Sc.add('pe', lambda e: e.transpose(out, in_, ident), reads=r, writes=w)

        def ACT(out, in_, func, r, w, **kw):
            return Sc.add('act', lambda e: e.activation(out, in_, func, **kw), reads=r, writes=w)

        rr = [0]

        def evac_eng():
            rr[0] += 1
            return 'act' if rr[0] % 2 == 0 else 'dve'

        def COPY(eng, out, in_, r, w):
            if eng == 'act':
                return ACT(out, in_, AF.Copy, r, w)
            return A(eng, 'tensor_copy', out, in_, r=r, w=w)

        DMA('sp', cst[:], cst_d, w=['cst'], key='cst')
        DMA('sp', posi[:], pos_d, w=['posi'], key='posi')
        DMA('sp', xres[:], x_d.rearrange("(n p) d -> p n d", p=128), w=['xres'], key='xin')
        A('dve', 'tensor_copy', identb[:], identf, r=['cst'], w=['identb'])
        A('dve', 'tensor_copy', onesb[:], onesf, r=['cst'], w=['onesb'])
        A('dve', 'tensor_scalar', mask8b[:], swam, 8.0, None, ALU.mult, r=['cst'], w=['mask8b'])
        A('dve', 'memset', cvals[:, 0:1], -math.pi, w=['cvals'])
        A('dve', 'memset', cvals[:, 1:2], 1.0, w=['cvals'])
        A('dve', 'memset', cvals[:, 2:3], 1e-5, w=['cvals'])
        A('dve', 'memset', cvals[:, 3:4], 1e-6, w=['cvals'])
        A('dve', 'tensor_copy', posf[:], posi[:], r=['posi'], w=['posf'])
        A('dve', 'tensor_tensor', cosT[:], posf[:].unsqueeze(2).to_broadcast([128, NT, 32]),
          invf.unsqueeze(1).to_broadcast([128, NT, 32]), ALU.mult, r=['posf', 'cst'], w=['cosT'])
        TWO_PI = 2 * math.pi
        angi = ytmp[:, 512:1024].bitcast(I32).rearrange("p (t n) -> p t n", t=NT)
        angk = ytmp[:, 0:512].rearrange("p (t n) -> p t n", t=NT)
        A('dve', 'tensor_copy', sinT[:], cosT[:], r=['cosT'], w=['sinT'])
        A('dve', 'tensor_scalar', cosT[:], cosT[:], 0.5 * math.pi, None, ALU.add, r=['cosT'], w=['cosT'])
        for tab, key in ((sinT, 'sinT'), (cosT, 'cosT')):
            A('dve', 'tensor_scalar', angk[:], tab[:], 1.0 / TWO_PI, None, ALU.mult, r=[key], w=[('ytmp', 'h0')])
            A('dve', 'tensor_copy', angi[:], angk[:], r=[('ytmp', 'h0')], w=[('ytmp', 'h1')])
            A('dve', 'tensor_copy', angk[:], angi[:], r=[('ytmp', 'h1')], w=[('ytmp', 'h0')])
            A('dve', 'scalar_tensor_tensor', tab[:], angk[:], -TWO_PI, tab[:], ALU.mult, ALU.add, r=[('ytmp', 'h0'), key], w=[key])
            A('dve', 'tensor_single_scalar', angk[:], tab[:], math.pi, ALU.is_gt, r=[key], w=[('ytmp', 'h0')])
            A('dve', 'scalar_tensor_tensor', tab[:], angk[:], -TWO_PI, tab[:], ALU.mult, ALU.add, r=[('ytmp', 'h0'), key], w=[key])
            A('dve', 'tensor_single_scalar', angk[:], tab[:], -math.pi, ALU.is_lt, r=[key], w=[('ytmp', 'h0')])
            A('dve', 'scalar_tensor_tensor', tab[:], angk[:], TWO_PI, tab[:], ALU.mult, ALU.add, r=[('ytmp', 'h0'), key], w=[key])
            A('dve', 'tensor_scalar', tab[:], tab[:], math.pi, -math.pi, ALU.min, ALU.max, r=[key], w=[key])
            ACT(tab[:], tab[:], AF.Sin, [key], [key])
        A('dve', 'memset', raw[:, :, 0:3], 0.0, w=['raw'])
        DMA('sp', fm_s.rearrange("(c p) w -> p c w", p=128)[:, :, 0:3], raw[:, :, 0:3], r=['raw'], w=[('fm_s', 'z')], key='fmz')

        def emit_xT(t):
            COPY('act', o_tok[:], xres[:, t, :], [('xres', t)], ['o_tok'])
            pbank = 7
            for kc in range(KC):
                TR(PBb[pbank][:, kc * 128:(kc + 1) * 128], o_tok[:, kc * 128:(kc + 1) * 128], identb[:],
                   ['o_tok', 'identb'], [('pb', pbank)])
            COPY('dve', xT[:, :, t * 128:(t + 1) * 128],
                 PBb[pbank][:, 0:1024].rearrange("p (k n) -> p k n", k=KC), [('pb', pbank)], [('xT', t)])

        def layer_norm(src_ap, src_key, t, lidx):
            for hh in range(2):
                A('dve', 'bn_stats', lnst[:, hh * 6:(hh + 1) * 6], src_ap[:, hh * 512:(hh + 1) * 512],
                  r=[src_key], w=['lnst'])
            A('dve', 'bn_aggr', lnst[:, 12:14], lnst[:, 0:12], r=['lnst'], w=['lnst2'])
            ACT(lnst[:, 15:16], lnst[:, 13:14], AF.Sqrt, ['lnst2', 'cvals'], ['lnst4'], bias=cvals[:, 2:3])
            A('dve', 'reciprocal', lnst[:, 14:15], lnst[:, 15:16], r=['lnst4'], w=['lnst3'])
            A('dve', 'tensor_scalar', ytmp[:], src_ap, lnst[:, 12:13], lnst[:, 14:15], ALU.subtract, ALU.mult,
              r=[src_key, 'lnst2', 'lnst3'], w=['ytmp'])
            A('pool', 'tensor_tensor', ytmp[:], ytmp[:], lnp[:, 0:1024], ALU.mult, r=['ytmp', 'lnp'], w=['ytmp'])
            A('pool', 'tensor_tensor', xres[:, t, :], ytmp[:], lnp[:, 1024:2048], ALU.add,
              r=['ytmp', 'lnp'], w=[('xres', t)])

        for t in range(NT):
            emit_xT(t)

        wq = [0]

        for l in range(nlayers):
            DMA('sp', convw[:], convw_d[l], w=['convw'], key='convw')
            DMA('sp', pv[:], pv_d[l], w=['pv'], key='pv')
            DMA('sp', lnp[:], lnp_d[l, 0], w=['lnp'], key='lnp')
            DMA('pool', wout, w_out_d[l].rearrange("(k p) n -> p k n", p=128), w=['wout'], key='wout')
            alog = pv[:, 0:4]
            dtb = pv[:, 4:8]
            sinks = pv[:, 8:16]
            dnw = pv[:, 16:144]
            w_in_v = w_in_d[l].rearrange("(k p) n -> p k n", p=128)
            blocks = [(0, 512, 'fm'), (512, 1024, 'fm'), (1024, 1536, 'fm'),
                      (1536, 2048, 'tm'), (2048, 2560, 'tm'), (2560, 2824, 'tm')]
            cnt = 0
            for bi, (c0, c1, kind) in enumerate(blocks):
                s_ = wq[0] % 2
                wq[0] += 1
                wkey = 'wa%d' % s_
                wcols = c1 - c0
                DMA('pool', wa[s_][:, :, 0:wcols], w_in_v[:, :, c0:c1], w=[wkey], key=wkey)
                if kind == 'fm':
                    for j in range(4):
                        for tg in range(4):
                            pbk = cnt % 4
                            q_ = cnt % 4
                            cnt += 1
                            for kc in range(KC):
                                MM(PB[pbk][:], wa[s_][:, kc, j * 128:(j + 1) * 128], xT[:, kc, tg * 512:(tg + 1) * 512],
                                   kc == 0, kc == KC - 1, [wkey] + [('xT', tg * 4 + i) for i in range(4)], [('pb', pbk)])
                            COPY(evac_eng(), stg[q_][:], PB[pbk][:], [('pb', pbk)], [('stg', q_)])
                            ch = (c0 // 128) + j
                            DMA('sp', fm_s[ch * 128:(ch + 1) * 128, 3 + tg * 512:3 + (tg + 1) * 512], stg[q_][:],
                                r=[('stg', q_)], w=[('fm_s', ch, tg)], key='stg%d' % q_)
                else:
                    for t in range(NT):
                        pbk = cnt % 4
                        q_ = cnt % 4
                        cnt += 1
                        for kc in range(KC):
                            MM(PB[pbk][:, 0:wcols], xT[:, kc, t * 128:(t + 1) * 128], wa[s_][:, kc, 0:wcols],
                               kc == 0, kc == KC - 1, [wkey, ('xT', t)], [('pb', pbk)])
                        COPY(evac_eng(), stg[q_][:, 0:wcols], PB[pbk][:, 0:wcols], [('pb', pbk)], [('stg', q_)])
                        if c0 == 2048:
                            COPY('act', ab_all[:, t, :], PB[pbk][:, 0:8], [('pb', pbk)], [('ab_all', t)])
                        DMA('sp', tm_s[t * 128:(t + 1) * 128, c0 - 1536:c1 - 1536], stg[q_][:, 0:wcols],
                            r=[('stg', q_)], w=[('tm_s', t, bi)], key='stg%d' % q_)
            v3 = lambda n: pp[n][:].rearrange("p (t h) -> p t h", h=4)
            A('dve', 'tensor_tensor', v3('ta'), ab_all[:, :, 0:4], dtb.unsqueeze(1).to_broadcast([128, NT, 4]), ALU.add,
              r=['ab_all', 'pv'], w=['pp_ta'])
            ACT(pp['ta'][:], pp['ta'][:], AF.Exp, ['pp_ta'], ['pp_ta'])
            ACT(pp['sp'][:], pp['ta'][:], AF.Ln, ['pp_ta', 'cvals'], ['pp_sp'], bias=cvals[:, 1:2])
            ACT(negA[:], alog, AF.Exp, ['pv'], ['negA'])
            A('dve', 'tensor_scalar', negA[:], negA[:], -1.0, None, ALU.mult, r=['negA'], w=['negA'])
            A('dve', 'tensor_tensor', v3('g'), v3('sp'), negA[:].unsqueeze(1).to_broadcast([128, NT, 4]), ALU.mult,
              r=['pp_sp', 'negA'], w=['pp_g'])
            ACT(v3('eb'), ab_all[:, :, 4:8], AF.Exp, ['ab_all'], ['pp_eb'], scale=-1.0)
            A('dve', 'tensor_scalar', pp['eb'][:], pp['eb'][:], 1.0, None, ALU.add, r=['pp_eb'], w=['pp_eb'])
            A('dve', 'reciprocal', pp['beta'][:], pp['eb'][:], r=['pp_eb'], w=['pp_beta'])
            A('dve', 'tensor_scalar', pp['nbeta'][:], pp['beta'][:], -1.0, None, ALU.mult, r=['pp_beta'], w=['pp_nbeta'])
            MM(PB[4][:, 0:64], tri, pp['g'][:], True, True, ['cst', 'pp_g'], [('pb', 4)])
            COPY('dve', pp['gcum'][:], PB[4][:, 0:64], [('pb', 4)], ['pp_gcum'])
            A('dve', 'tensor_scalar', pp['ngc'][:], pp['gcum'][:], -1.0, None, ALU.mult, r=['pp_gcum'], w=['pp_ngc'])
            ACT(pp['egc'][:], pp['gcum'][:], AF.Exp, ['pp_gcum'], ['pp_egc'])
            A('dve', 'tensor_tensor', pp['bege'][:], pp['beta'][:], pp['egc'][:], ALU.mult,
              r=['pp_beta', 'pp_egc'], w=['pp_bege'])
            A('pool', 'memset', Sf[:], 0.0, w=['Sf'])
            A('pool', 'memset', Sb[:], 0.0, w=['Sb'])

            def run_rr(gens):
                gens = list(gens)
                while gens:
                    for g_ in list(gens):
                        try:
                            next(g_)
                        except StopIteration:
                            gens.remove(g_)

            def prologue(t):
                DMA('sp', raw[:], fm_s.rearrange("(c p) w -> p c w", p=128)[:, :, t * 128:t * 128 + 131],
                    r=['fm_s'], w=['raw'], key='raw')
                DMA('sp', tmv[:], tm_s[t * 128:(t + 1) * 128, :], r=[('tm_s', t)], w=['tmv'], key='tmv')
                yield
                for cc in range(12):
                    eng = 'dve'
                    A(eng, 'tensor_scalar', cacc[:, cc, :], raw[:, cc, 0:128], convw[:, cc, 0:1], None, ALU.mult,
                      r=['raw', 'convw'], w=[('cacc', cc)])
                    for k in range(1, 4):
                        if eng == 'dve':
                            A(eng, 'scalar_tensor_tensor', cacc[:, cc, :], raw[:, cc, k:k + 128], convw[:, cc, k:k + 1], cacc[:, cc, :],
                              ALU.mult, ALU.add, r=['raw', 'convw', ('cacc', cc)], w=[('cacc', cc)])
                        else:
                            A(eng, 'tensor_scalar', ptmp[:], raw[:, cc, k:k + 128], convw[:, cc, k:k + 1], None, ALU.mult,
                              r=['raw', 'convw'], w=['ptmp'])
                            A(eng, 'tensor_tensor', cacc[:, cc, :], cacc[:, cc, :], ptmp[:], ALU.add,
                              r=['ptmp', ('cacc', cc)], w=[('cacc', cc)])
                    if cc % 4 == 3:
                        yield
                ACT(yb[:], cacc[:], AF.Silu, ['cacc'], ['yb'])
                ACT(zs[:], tmv[:, 0:512], AF.Silu, ['tmv'], ['zs'])
                yield
                A('pool', 'tensor_tensor', sq[:].rearrange("p (a n) -> p a n", a=8), yb[:, 0:8, :], yb[:, 0:8, :], ALU.mult,
                  r=['yb'], w=['sq'])
                for hh in range(2):
                    bk = 3 + hh
                    MM(PB[bk][:], onesb[:], sq[:, hh * 512:(hh + 1) * 512], True, True, ['onesb', 'sq'], [('pb', bk)])
                    ACT(rn[:, hh * 512:(hh + 1) * 512], PB[bk][:], AF.Sqrt, [('pb', bk), 'cvals'], [('rn', hh)], bias=cvals[:, 3:4])
                    A('dve', 'reciprocal', rn[:, hh * 512:(hh + 1) * 512], rn[:, hh * 512:(hh + 1) * 512],
                      r=[('rn', hh)], w=[('rn', hh)])
                yield
                A('dve', 'tensor_tensor', qkT[:], yb[:, 0:8, :], rn[:].rearrange("p (a n) -> p a n", a=8), ALU.mult,
                  r=['yb', 'rn'], w=['qkT'])
                for h in range(4):
                    TR(PBb[2][:, h * 128:(h + 1) * 128], qkT[:, 4 + h, :], identb[:], ['qkT', 'identb'], [('pb', 2)])
                for h in range(4):
                    TR(PBb[2][:, (4 + h) * 128:(5 + h) * 128], yb[:, 8 + h, :], identb[:], ['yb', 'identb'], [('pb', 2)])
                yield

            def dn_head(t, h, B_):
                ix = t * 4 + h
                col = lambda n: pp[n][:, ix:ix + 1]
                bps, brd = B_['bps'], B_['brd']
                kps, krd = ('pb', bps), ('pb', brd)
                n_ = B_['n']
                K_ = lambda nm: nm + n_
                A('pool', 'tensor_scalar', B_['trg'][:], tri, col('g'), -1.0, ALU.mult, ALU.mult, r=['cst', 'pp_g'], w=[K_('trg')])
                MM(PB[bps][:, 0:128], onesf, B_['trg'][:], True, False, ['cst', K_('trg')], [kps])
                MM(PB[bps][:, 0:128], identf, maskA, False, True, ['cst'], [kps])
                MM(PB[bps][:, 128:256], onesf, B_['trg'][:], True, False, ['cst', K_('trg')], [kps])
                MM(PB[bps][:, 128:256], identf, maskB, False, True, ['cst'], [kps])
                kTh = qkT[:, 4 + h, :]
                qTh = qkT[:, h, :]
                MM(PB[bps][:, 256:384], kTh, kTh, True, True, ['qkT'], [kps])
                MM(PB[bps][:, 384:512], kTh, qTh, True, True, ['qkT'], [kps])
                yield
                dsc_ = B_['dsc']
                ACT(B_['dL'][:], PB[bps][:, 0:128], AF.Exp, [kps, 'pp_gcum'], [K_('dL')], bias=col('gcum'))
                ACT(B_['dAT'][:], PB[bps][:, 128:256], AF.Exp, [kps, 'pp_ngc'], [K_('dAT')], bias=col('ngc'), scale=-1.0)
                ACT(dsc_[:, 0:1], PB[bps][:, 255:256], AF.Exp, [kps, 'pp_ngc'], [K_('dsc')], bias=col('ngc'), scale=-1.0)
                ACT(dsc_[:, 1:2], PB[bps][:, 255:256], AF.Exp, [kps], [K_('dsc')], scale=-1.0)
                abp_ = B_['abp']
                A('dve', 'scalar_tensor_tensor', abp_[0][:, 0:128], PB[bps][:, 256:384], col('nbeta'), B_['dL'][:], ALU.mult, ALU.mult,
                  r=[kps, 'pp_nbeta', K_('dL')], w=[(K_('abp'), 0)])
                A('dve', 'tensor_tensor', B_['ATm'][:], PB[bps][:, 384:512], B_['dAT'][:], ALU.mult, r=[kps, K_('dAT')], w=[K_('ATm')])
                yield
                TR(PB[brd][:, 0:128], abp_[0][:, 0:128], identf, [(K_('abp'), 0), 'cst'], [krd])
                COPY('act', abp_[0][:, 128:256], PB[brd][:, 0:128], [krd], [(K_('abp'), 0)])
                A('pool', 'tensor_copy', abp_[0][:, 256:384], identf, r=['cst'], w=[(K_('abp'), 0)])
                yield
                cur = 0
                for rnd in range(1, 8):
                    Ac = abp_[cur][:, 0:128]
                    Bc = abp_[cur][:, 128:256]
                    Pc = abp_[cur][:, 256:384]
                    nxt = 1 - cur
                    rk = [(K_('abp'), cur), 'cst']
                    if rnd <= 6:
                        MM(PB[brd][:, 0:128], Bc, Ac, True, True, rk, [krd])
                        if rnd <= 5:
                            MM(PB[brd][:, 128:256], Ac, Bc, True, True, rk, [krd])
                    MM(PB[brd][:, 256:384], identf, Pc, True, False, rk, [krd])
                    MM(PB[brd][:, 256:384], Ac, Pc, False, True, rk, [krd])
                    if rnd <= 6:
                        COPY(evac_eng(), abp_[nxt][:, 0:384], PB[brd][:, 0:384], [krd], [(K_('abp'), nxt)])
                    else:
                        COPY(evac_eng(), B_['TT'][:], PB[brd][:, 256:384], [krd], [K_('TT')])
                    cur = nxt
                    yield
                ktok = PBb[2][:, h * 128:(h + 1) * 128]
                vtok = PBb[2][:, (4 + h) * 128:(5 + h) * 128]
                A('dve', 'tensor_scalar', B_['Xv'][:], vtok, col('beta'), None, ALU.mult, r=[('pb', 2), 'pp_beta'], w=[K_('Xv')])
                A('dve', 'tensor_scalar', B_['Xw'][:], ktok, col('bege'), None, ALU.mult, r=[('pb', 2), 'pp_bege'], w=[K_('Xw')])
                A('dve', 'tensor_scalar', B_['ke'][:], ktok, dsc_[:, 0:1], None, ALU.mult, r=[('pb', 2), K_('dsc')], w=[K_('ke')])
                MM(PB[brd][:, 0:128], B_['TT'][:], B_['Xv'][:], True, True, [K_('TT'), K_('Xv')], [krd])
                MM(PB[brd][:, 128:256], B_['Xw'][:], B_['TT'][:], True, True, [K_('TT'), K_('Xw')], [krd])
                yield
                COPY('act', B_['u_sb'][:], PB[brd][:, 0:128], [krd], [K_('u_sb')])
                COPY('act', B_['wT_sb'][:], PB[brd][:, 128:256], [krd], [K_('wT_sb')])
                MM(PB[brd][:, 256:384], B_['wT_sb'][:], Sb[:, h, :], True, True, [K_('wT_sb'), ('Sb', h)], [krd])
                MM(PB[brd][:, 384:512], qTh, Sb[:, h, :], True, True, ['qkT', ('Sb', h)], [krd])
                yield
                A('dve', 'tensor_tensor', B_['vn'][:], B_['u_sb'][:], PB[brd][:, 256:384], ALU.subtract, r=[K_('u_sb'), krd], w=[K_('vn')])
                A('dve', 'tensor_scalar', B_['t1'][:], PB[brd][:, 384:512], col('egc'), None, ALU.mult, r=[krd, 'pp_egc'], w=[K_('t1')])
                MM(PB[bps][:, 0:128], B_['ATm'][:], B_['vn'][:], True, True, [K_('ATm'), K_('vn')], [kps])
                MM(PB[bps][:, 128:256], B_['ke'][:], B_['vn'][:], True, True, [K_('ke'), K_('vn')], [kps])
                yield
                A('dve', 'tensor_tensor', o4[:, h, :], B_['t1'][:], PB[bps][:, 0:128], ALU.add, r=[K_('t1'), kps], w=[('o4', h)])
                A('dve', 'scalar_tensor_tensor', Sf[:, h, :], Sf[:, h, :], dsc_[:, 1:2], PB[bps][:, 128:256], ALU.mult, ALU.add,
                  r=[('Sf', h), K_('dsc'), kps], w=[('Sf', h)])
                COPY('act', Sb[:, h, :], Sf[:, h, :], [('Sf', h)], [('Sb', h)])
                yield

            def dn_stream(t, heads, B_):
                for h in heads:
                    yield from dn_head(t, h, B_)

            def rms_gate(t):
                A('pool', 'tensor_tensor', o4b[:], o4[:], o4[:], ALU.mult, r=['o4'], w=['ytmp'])
                A('dve', 'tensor_reduce', rs4[:, 0:4], o4b[:], AX.X, ALU.add, r=['ytmp'], w=['rs4'])
                ACT(rs4[:, 0:4], rs4[:, 0:4], AF.Sqrt, ['rs4', 'cvals'], ['rs4'], bias=cvals[:, 3:4], scale=1.0 / 16384.0)
                A('dve', 'reciprocal', rs4[:, 0:4], rs4[:, 0:4], r=['rs4'], w=['rs4'])
                A('dve', 'tensor_scalar', rs4[:, 4:8], rs4[:, 0:4], 128.0 ** -0.5, None, ALU.mult, r=['rs4'], w=['rs4b'])
                A('dve', 'tensor_tensor', o4b[:], o4[:], rs4[:, 4:8].unsqueeze(2).to_broadcast([128, 4, 128]), ALU.mult,
                  r=['o4', 'rs4b'], w=['ytmp'])
                A('pool', 'tensor_tensor', o4b[:], o4b[:], dnw.unsqueeze(1).to_broadcast([128, 4, 128]), ALU.mult,
                  r=['ytmp', 'pv'], w=['ytmp'])
                A('dve', 'tensor_tensor', o_tok[:, 0:512].rearrange("p (h d) -> p h d", h=4), o4b[:],
                  zs[:].rearrange("p (h d) -> p h d", h=4), ALU.mult, r=['ytmp', 'zs'], w=[('o_tok', 0)])

            def swa(t):
                def rope(src, nh, dst4, keyw):
                    s4 = src.rearrange("p (h two d) -> p h two d", two=2, d=32)
                    x1 = s4[:, :, 0, :]
                    x2 = s4[:, :, 1, :]
                    cs = cosT[:, t, :].unsqueeze(1).to_broadcast([128, nh, 32])
                    sn = sinT[:, t, :].unsqueeze(1).to_broadcast([128, nh, 32])
                    w4 = lambda i: ytmp[:, i * 256:i * 256 + nh * 32].rearrange("p (h d) -> p h d", d=32)
                    A('pool', 'tensor_tensor', w4(0), x1, cs, ALU.mult, r=['tmv', 'cosT'], w=[('ytmp', 0)])
                    A('pool', 'tensor_tensor', w4(1), x2, sn, ALU.mult, r=['tmv', 'sinT'], w=[('ytmp', 1)])
                    A('pool', 'tensor_tensor', w4(2), x2, cs, ALU.mult, r=['tmv', 'cosT'], w=[('ytmp', 2)])
                    A('pool', 'tensor_tensor', w4(3), x1, sn, ALU.mult, r=['tmv', 'sinT'], w=[('ytmp', 3)])
                    A('pool', 'tensor_tensor', dst4[0], w4(0), w4(1), ALU.subtract, r=[('ytmp', 0), ('ytmp', 1)], w=[keyw])
                    A('pool', 'tensor_tensor', dst4[1], w4(2), w4(3), ALU.add, r=[('ytmp', 2), ('ytmp', 3)], w=[keyw])
                qv = qr[:].rearrange("p (pr two half d) -> p pr two half d", pr=4, two=2, half=2)
                for two in range(2):
                    srcq = tmv[:, 520 + two * 256:520 + (two + 1) * 256]
                    rope(srcq, 4, (qv[:, :, two, 0, :], qv[:, :, two, 1, :]), 'qr')
                    yield
                kv4 = kr[:].rearrange("p (h half d) -> p h half d", h=2, half=2)
                rope(tmv[:, 1032:1160], 2, (kv4[:, :, 0, :], kv4[:, :, 1, :]), 'kr')
                COPY('act', va[:, t, :], tmv[:, 1160:1288], ['tmv'], [('va', t)])
                yield
                for pr in range(4):
                    TR(PBb[7][:, pr * 128:(pr + 1) * 128], qr[:, pr * 128:(pr + 1) * 128], identb[:], ['qr', 'identb'], [('pb', 7)])
                TR(PBb[7][:, 512:640], kr[:], identb[:], ['kr', 'identb'], [('pb', 7)])
                COPY('act', qTs[:], PBb[7][:, 0:512].rearrange("p (a n) -> p a n", a=4), [('pb', 7)], ['qTs'])
                COPY('act', kTa[:, t * 128:(t + 1) * 128], PBb[7][:, 512:640], [('pb', 7)], [('kTa', t)])
                yield
                for g_ in range(2):
                    ps_ = slice(g_ * 64, (g_ + 1) * 64)
                    if t == 0:
                        k0, nk = 0, 128
                        mkb = mask8b[:, 128:256]
                        kkeys = [('kTa', 0)]
                    else:
                        k0, nk = (t - 1) * 128, 256
                        mkb = mask8b[:]
                        kkeys = [('kTa', t - 1), ('kTa', t)]
                    nb = nk // 128
                    for j in range(4):
                        bk = j // 2
                        c0_ = (j % 2) * 256
                        MM(PB[bk][:, c0_:c0_ + nk], identb[:], mkb, True, False, ['identb', 'mask8b'], [('pb', bk)])
                        MM(PB[bk][:, c0_:c0_ + nk], qTs[ps_, j, :], kTa[ps_, k0:k0 + nk], False, True, ['qTs'] + kkeys, [('pb', bk)])
                    yield
                    for bk in range(2):
                        A('dve', 'tensor_reduce', st8[:, bk * 2:bk * 2 + 2],
                          PB[bk][:].rearrange("p (a n) -> p a n", a=2)[:, :, 0:nk], AX.X, ALU.max, r=[('pb', bk)], w=[('st8', 'mx')])
                    A('dve', 'scalar_tensor_tensor', st8[:, 4:8], st8[:, 0:4], 0.125, sinks[:, g_ * 4:g_ * 4 + 4], ALU.mult, ALU.max,
                      r=[('st8', 'mx'), 'pv'], w=[('st8', 'm')])
                    A('dve', 'tensor_scalar', st8[:, 8:12], st8[:, 4:8], -1.0, None, ALU.mult, r=[('st8', 'm')], w=[('st8', 'nm')])
                    A('dve', 'tensor_tensor', st8[:, 12:16], st8[:, 8:12], sinks[:, g_ * 4:g_ * 4 + 4], ALU.add,
                      r=[('st8', 'nm'), 'pv'], w=[('st8', 'sk')])
                    yield
                    for j in range(4):
                        bk = j // 2
                        c0_ = (j % 2) * 256
                        ACT(pb_[:, j * 256:j * 256 + nk], PB[bk][:, c0_:c0_ + nk], AF.Exp, [('pb', bk), ('st8', 'nm')],
                            [('pb_', j), ('st8', 'rs', j)], bias=st8[:, 8 + j:9 + j], scale=0.125, accum_out=st8[:, 16 + j:17 + j])
                    ACT(st8[:, 12:16], st8[:, 12:16], AF.Exp, [('st8', 'sk')], [('st8', 'sk')])
                    yield
                    for j in range(4):
                        for b_ in range(nb):
                            TR(PBb[7][:, j * 256 + b_ * 128:j * 256 + (b_ + 1) * 128], pb_[:, j * 256 + b_ * 128:j * 256 + (b_ + 1) * 128],
                               identb[:], [('pb_', j), 'identb'], [('pb', 7)])
                    A('dve', 'tensor_tensor', st8[:, 20:24], st8[:, 16:20], st8[:, 12:16], ALU.add,
                      r=[('st8', 'rs'), ('st8', 'sk')], w=[('st8', 'den')])
                    A('dve', 'reciprocal', st8[:, 20:24], st8[:, 20:24], r=[('st8', 'den')], w=[('st8', 'den')])
                    yield
                    if nb == 2:
                        COPY('act', pTb[:, 0:512], PBb[7][:, 0:512], [('pb', 7)], [('pTb', 0)])
                        COPY('dve', pTb[:, 512:1024], PBb[7][:, 512:1024], [('pb', 7)], [('pTb', 1)])
                    else:
                        for j in range(4):
                            COPY('act' if j % 2 == 0 else 'dve', pTb[:, j * 256:j * 256 + 128], PBb[7][:, j * 256:j * 256 + 128],
                                 [('pb', 7)], [('pTb', j // 2)])
                    yield
                    for j in range(4):
                        for b_ in range(nb):
                            tt_ = t - (nb - 1) + b_
                            MM(PB[0][:, j * 64:(j + 1) * 64], pTb[:, j * 256 + b_ * 128:j * 256 + (b_ + 1) * 128],
                               va[:, tt_, g_ * 64:(g_ + 1) * 64], b_ == 0, b_ == nb - 1, [('pTb', j // 2), ('va', tt_)], [('pb', 0)])
                    yield
                    A('dve', 'tensor_tensor', o_tok[:, 512 + g_ * 256:512 + (g_ + 1) * 256].rearrange("p (j d) -> p j d", j=4),
                      PB[0][:, 0:256].rearrange("p (j d) -> p j d", j=4),
                      st8[:, 20:24].unsqueeze(2).to_broadcast([128, 4, 64]), ALU.mult,
                      r=[('pb', 0), ('st8', 'den')], w=[('o_tok', 1, g_)])
                    yield

            def epilogue(t):
                if dbg and l == nlayers - 1:
                    COPY('dve', ytmp[:], o_tok[:], ['o_tok'], ['ytmp'])
                    DMA('sp', dbg_o[t * 128:(t + 1) * 128, :], ytmp[:], r=['ytmp'], key='dbg_o')
                    COPY('dve', ytmp[:, 0:512].rearrange("p (h d) -> p h d", h=4), o4[:], ['o4'], ['ytmp'])
                    DMA('sp', dbg_raw[t * 128:(t + 1) * 128, :], ytmp[:, 0:512], r=['ytmp'], key='dbg_raw')
                for kc in range(KC):
                    TR(PBb[7][:, kc * 128:(kc + 1) * 128], o_tok[:, kc * 128:(kc + 1) * 128], identb[:], ['o_tok', 'identb'], [('pb', 7)])
                COPY('act', oT[:], PBb[7][:, 0:1024].rearrange("p (k n) -> p k n", k=KC), [('pb', 7)], ['oT'])
                yield
                for hh in range(2):
                    for kc in range(KC):
                        MM(PB[hh][:], oT[:, kc, :], wout[:, kc, hh * 512:(hh + 1) * 512], kc == 0, kc == KC - 1,
                           ['oT', 'wout'], [('pb', hh)])
                    A('dve', 'scalar_tensor_tensor', ytmp[:, hh * 512:(hh + 1) * 512], xres[:, t, hh * 512:(hh + 1) * 512], ALPHA,
                      PB[hh][:], ALU.mult, ALU.add, r=[('xres', t), ('pb', hh)], w=['ytmp'])
                    yield
                layer_norm(ytmp[:], 'ytmp', t, 0)
                yield
                emit_xT(t)
                if dbg and l == nlayers - 1:
                    DMA('sp', dbg_x1[t * 128:(t + 1) * 128, :], xres[:, t, :], r=[('xres', t)], key='dbg_x1')
                yield

            run_rr([prologue(0)])
            for t in range(NT):
                run_rr([dn_stream(t, [0, 2], DN[0]), dn_stream(t, [1, 3], DN[1]), swa(t)])
                rms_gate(t)
                if t + 1 < NT:
                    run_rr([epilogue(t), prologue(t + 1)])
                else:
                    run_rr([epilogue(t)])

            DMA('sp', lnp[:], lnp_d[l, 1], w=['lnp'], key='lnp')
            moe = (l % 2 == 1)
            li = l // 2
            if moe:
                DMA('sp', rw32[:], rw_d[li], w=['rw32'], key='rw32')
                xTf = ytmp[:].rearrange("p (k n) -> p k n", k=KC)
                for t in range(NT):
                    for kc in range(KC):
                        bk = kc // 4
                        TR(PB[bk][:, (kc % 4) * 128:(kc % 4 + 1) * 128], xres[:, t, kc * 128:(kc + 1) * 128], identf,
                           [('xres', t), 'cst'], [('pb', bk)])
                    COPY('act', xTf[:, 0:4, :], PB[0][:].rearrange("p (k n) -> p k n", k=4), [('pb', 0)], [('ytmp', 'h0')])
                    COPY('dve', xTf[:, 4:8, :], PB[1][:].rearrange("p (k n) -> p k n", k=4), [('pb', 1)], [('ytmp', 'h1')])
                    for kc in range(KC):
                        MM(PB[2][:, 0:NE], xTf[:, kc, :], rw32[:, kc, :], kc == 0, kc == KC - 1,
                           [('ytmp', 'h0'), ('ytmp', 'h1'), 'rw32'], [('pb', 2)])
                    COPY('dve', lg[:, t, :], PB[2][:, 0:NE], [('pb', 2)], [('lg', t)])
                bc = lambda a: a[:].unsqueeze(2).to_broadcast([128, NT, NE])
                A('dve', 'tensor_reduce', ms['m1'][:], lg[:], AX.X, ALU.max, r=['lg'], w=['ms_m1'])
                A('dve', 'tensor_tensor', mt['eq1'][:], lg[:], bc(ms['m1']), ALU.is_equal, r=['lg', 'ms_m1'], w=['mt_eq1'])
                A('dve', 'scalar_tensor_tensor', mt['l2'][:], mt['eq1'][:], -1e30, lg[:], ALU.mult, ALU.add,
                  r=['mt_eq1', 'lg'], w=['mt_l2'])
                A('dve', 'tensor_reduce', ms['m2'][:], mt['l2'][:], AX.X, ALU.max, r=['mt_l2'], w=['ms_m2'])
                A('dve', 'tensor_tensor', mt['eq2'][:], mt['l2'][:], bc(ms['m2']), ALU.is_equal, r=['mt_l2', 'ms_m2'], w=['mt_eq2'])
                A('dve', 'tensor_tensor', ms['d'][:], ms['m2'][:], ms['m1'][:], ALU.subtract, r=['ms_m1', 'ms_m2'], w=['ms_d'])
                ACT(ms['ed'][:], ms['d'][:], AF.Exp, ['ms_d'], ['ms_ed'])
                A('dve', 'tensor_scalar', ms['g1'][:], ms['ed'][:], 1.0, None, ALU.add, r=['ms_ed'], w=['ms_g1'])
                A('dve', 'reciprocal', ms['g1'][:], ms['g1'][:], r=['ms_g1'], w=['ms_g1'])
                A('dve', 'tensor_tensor', ms['g2'][:], ms['ed'][:], ms['g1'][:], ALU.mult, r=['ms_ed', 'ms_g1'], w=['ms_g2'])
                A('dve', 'tensor_tensor', mt['eq1'][:], mt['eq1'][:], bc(ms['g1']), ALU.mult, r=['mt_eq1', 'ms_g1'], w=['mt_eq1'])
                A('dve', 'tensor_tensor', mt['eq2'][:], mt['eq2'][:], bc(ms['g2']), ALU.mult, r=['mt_eq2', 'ms_g2'], w=['mt_eq2'])
                A('dve', 'tensor_tensor', comb[:], mt['eq1'][:], mt['eq2'][:], ALU.add, r=['mt_eq1', 'mt_eq2'], w=['comb'])
            nexp = NE if moe else 1
            for t in range(NT):
                ACT(xres[:, t, :], xres[:, t, :], AF.Copy, [('xres', t)], [('xres', t)], scale=ALPHA)
            for e_ in range(nexp):
                if moe:
                    Wg, Wu, Wd = mg_d[li, e_], mu_d[li, e_], md_d[li, e_]
                else:
                    Wg, Wu, Wd = fg_d[li], fu_d[li], fd_d[li]
                Wgv = Wg.rearrange("(k p) f -> p k f", p=128)
                Wuv = Wu.rearrange("(k p) f -> p k f", p=128)
                Wdv = Wd.rearrange("(f p) d -> p f d", p=128)
                for grp in range(7):
                    s_ = wq[0] % 2
                    wq[0] += 1
                    kg, ku, kd = 'wg%d' % s_, 'wu%d' % s_, 'wd%d' % s_
                    DMA('pool', wg[s_], Wgv[:, :, grp * 512:(grp + 1) * 512], w=[kg], key=kg)
                    DMA('pool', wu[s_], Wuv[:, :, grp * 512:(grp + 1) * 512], w=[ku], key=ku)
                    DMA('pool', wd[s_], Wdv[:, grp * 4:(grp + 1) * 4, :], w=[kd], key=kd)
                    ci = 0
                    for fc in range(4):
                        for tg in range(4):
                            bg = (ci % 2) * 2
                            bu = bg + 1
                            ci += 1
                            xk = [('xT', tg * 4 + i) for i in range(4)]
                            for kc in range(KC):
                                MM(PB[bg][:], wg[s_][:, kc, fc * 128:(fc + 1) * 128], xT[:, kc, tg * 512:(tg + 1) * 512],
                                   kc == 0, kc == KC - 1, [kg] + xk, [('pb', bg)])
                            for kc in range(KC):
                                MM(PB[bu][:], wu[s_][:, kc, fc * 128:(fc + 1) * 128], xT[:, kc, tg * 512:(tg + 1) * 512],
                                   kc == 0, kc == KC - 1, [ku] + xk, [('pb', bu)])
                            sg = sgt[ci % 2]
                            ACT(sg[:], PB[bg][:], AF.Silu, [('pb', bg)], [('ytmp', 'h%d' % (ci % 2))])
                            A('dve', 'tensor_tensor', hT[:, fc, tg * 512:(tg + 1) * 512], sg[:], PB[bu][:], ALU.mult,
                              r=[('ytmp', 'h%d' % (ci % 2)), ('pb', bu)], w=[('hT', fc, tg)])
                    di = 0
                    for t in range(NT):
                        for hh in range(2):
                            bd = 4 + (di % 4)
                            di += 1
                            for fc in range(4):
                                MM(PB[bd][:], hT[:, fc, t * 128:(t + 1) * 128], wd[s_][:, fc, hh * 512:(hh + 1) * 512],
                                   fc == 0, fc == 3, [('hT', fc, t // 4), kd], [('pb', bd)])
                            cs_ = comb[:, t, e_:e_ + 1] if moe else 1.0
                            A('dve', 'scalar_tensor_tensor', xres[:, t, hh * 512:(hh + 1) * 512], PB[bd][:], cs_,
                              xres[:, t, hh * 512:(hh + 1) * 512], ALU.mult, ALU.add,
                              r=[('pb', bd), ('xres', t)] + (['comb'] if moe else []), w=[('xres', t)])
            for t in range(NT):
                layer_norm(xres[:, t, :], ('xres', t), t, 1)
                if l < nlayers - 1:
                    emit_xT(t)
        fin = DMA('sp', y_d.rearrange("(n p) d -> p n d", p=128), xres[:], r=['xres'], key='yout')
        Sc.emit(final_wait_ops=[fin])
        print("ops:", Sc.nops, {e: len(v) for e, v in Sc.eops.items()})
    return nc


_CONSTS = None


def host_consts():
    c = np.zeros((128, 928), np.float32)
    i = np.arange(128)
    c[:, 0:128] = np.eye(128, dtype=np.float32)
    c[:, 128:256] = (i[:, None] <= i[None, :]).astype(np.float32)
    c[:, 256:384] = 1.0
    c[:, 384:512] = np.where(i[None, :] >= i[:, None], -BIG, 0.0)
    c[:, 512:640] = np.where(i[None, :] < i[:, None], BIG, 0.0)
    j = np.arange(256)
    rel = i[:, None] + 128 - j[None, :]
    c[:, 640:896] = np.where((rel >= 0) & (rel < 128), 0.0, -BIG)
    c[:, 896:928] = (10000.0 ** (-np.arange(0, 64, 2, dtype=np.float32) / 64.0)).astype(np.float32)[None, :]
    return c


def prep_inputs(inputs):
    f32 = lambda a: np.ascontiguousarray(np.asarray(a, dtype=np.float32))
    conv_w = f32(inputs["conv_w"])
    convw = np.ascontiguousarray(conv_w.reshape(DEPTH, 4, 12, 128).transpose(0, 3, 2, 1))
    pvrow = np.concatenate([f32(inputs["a_log"]), f32(inputs["dt_bias"]), f32(inputs["sinks"]),
                            f32(inputs["dn_norm_w"])], axis=1)
    pv = np.ascontiguousarray(np.broadcast_to(pvrow[:, None, :], (DEPTH, 128, 144)))
    lnrow = np.concatenate([f32(inputs["ln_g"]), f32(inputs["ln_b"])], axis=2)
    lnp = np.ascontiguousarray(np.broadcast_to(lnrow[:, :, None, :], (DEPTH, 2, 128, 2048)))
    rw = np.ascontiguousarray(f32(inputs["router_w"]).reshape(2, KC, 128, NE).transpose(0, 2, 1, 3))
    shared = {
        "w_in": f32(inputs["w_in"]), "w_out": f32(inputs["w_out"]),
        "ffn_w_gate": f32(inputs["ffn_w_gate"]), "ffn_w_up": f32(inputs["ffn_w_up"]),
        "ffn_w_down": f32(inputs["ffn_w_down"]),
        "moe_w_gate": f32(inputs["moe_w_gate"]), "moe_w_up": f32(inputs["moe_w_up"]),
        "moe_w_down": f32(inputs["moe_w_down"]),
        "convw": convw, "pv": pv, "lnp": lnp, "rw": rw, "cst": host_consts(),
    }
    return shared


def kernel(**inputs):
    shared = prep_inputs(inputs)
    x = np.asarray(inputs["x"], dtype=np.float32)
    pos = np.asarray(inputs["positions"], dtype=np.int32)
    nb = x.shape[0]
    nc = build()
    in_maps = []
    for b in range(nb):
        m = dict(shared)
        m["x"] = np.ascontiguousarray(x[b])
        m["pos"] = np.ascontiguousarray(pos[b].reshape(NT, 128).T)
        in_maps.append(m)
    res = run_bass_kernel_spmd(nc, in_maps, core_ids=list(range(nb)))
    return np.stack([np.asarray(r["y"], dtype=np.float32) for r in res.results], axis=0)
```

```python
import math
from contextlib import ExitStack
import numpy as np
import concourse.bass as bass
import concourse.mybir as mybir
from concourse.bass_utils import run_bass_kernel_spmd

F32 = mybir.dt.float32
BF16 = mybir.dt.bfloat16
I32 = mybir.dt.int32
ALU = mybir.AluOpType
AF = mybir.ActivationFunctionType
AX = mybir.AxisListType

CE = ['pe', 'act', 'dve', 'pool', 'sp']
EIDX = {e: i for i, e in enumerate(CE)}
EPOCH = 12000

DEPTH = 4
S = 2048
NT = 16
D = 1024
KC = 8
INC = 2824
DFF = 3584
NE = 8
ALPHA = (2.0 * DEPTH) ** 0.25
TMW = 1288
BIG = 30000.0


class Op:
    __slots__ = ('eng', 'fn', 'is_dma', 'dma_key', 'eidx', 'signal', 'sem', 'val', 'waits', 'vc', 'dvc')


class Sched:
    def __init__(self, nc):
        self.nc = nc
        self.eops = {e: [] for e in CE}
        self.last_writer = {}
        self.readers = {}
        self.desc = {}
        self.known = set()
        self.aliases = {}
        self.dma_cnt = {}
        self.last = {e: None for e in CE}
        self.nops = 0

    def alias(self, a, b):
        self.aliases.setdefault(a, set()).add(b)
        self.aliases.setdefault(b, set()).add(a)

    def _reg(self, k):
        if k in self.known:
            return
        self.known.add(k)
        for i in range(1, len(k)):
            self.desc.setdefault(k[:i], set()).add(k)
        self.desc.setdefault(k, set())

    def _related(self, k):
        out = [k]
        for i in range(1, len(k)):
            out.append(k[:i])
        out.extend(self.desc.get(k, ()))
        for r in self.aliases.get(k[0], ()):
            out.append((r,))
            out.extend(self.desc.get((r,), ()))
        return out

    def add(self, eng, fn, reads=(), writes=(), dma_key=None):
        reads = [k if isinstance(k, tuple) else (k,) for k in reads]
        writes = [k if isinstance(k, tuple) else (k,) for k in writes]
        writes = writes + [k for k in reads if k[0] == 'pb' and k not in writes]
        for k in reads:
            self._reg(k)
        for k in writes:
            self._reg(k)
        op = Op()
        op.eng = eng
        op.fn = fn
        op.is_dma = dma_key is not None
        op.dma_key = dma_key
        op.eidx = len(self.eops[eng])
        op.signal = False
        op.sem = None
        op.val = None
        deps = {}
        for b in reads:
            for k in self._related(b):
                w = self.last_writer.get(k)
                if w is not None:
                    deps[id(w)] = (w, 'raw')
        for b in writes:
            for k in self._related(b):
                w = self.last_writer.get(k)
                if w is not None and id(w) not in deps:
                    deps[id(w)] = (w, 'waw')
                for r in self.readers.get(k, {}).values():
                    if id(r) not in deps:
                        deps[id(r)] = (r, 'war')
        prev = self.last[eng]
        vc = list(prev.vc) if prev is not None else [-1] * len(CE)
        dvc = dict(prev.dvc) if prev is not None else {}
        best = {}
        for d, kind in deps.values():
            if d is op:
                continue
            if d.is_dma:
                if dvc.get(d.dma_key, 0) >= d.val:
                    continue
                key = ('d', d.dma_key)
                cur = best.get(key)
                if cur is None or d.val > cur.val:
                    best[key] = d
            else:
                if d.eng == eng and not op.is_dma and kind != 'raw':
                    continue
                if vc[EIDX[d.eng]] >= d.eidx:
                    continue
                key = ('c', d.eng)
                cur = best.get(key)
                if cur is None or d.eidx > cur.eidx:
                    best[key] = d
        op.waits = list(best.values())
        for d in op.waits:
            if d.is_dma:
                dvc[d.dma_key] = max(dvc.get(d.dma_key, 0), d.val)
            else:
                d.signal = True
                vc[EIDX[d.eng]] = max(vc[EIDX[d.eng]], d.eidx)
            for i, v in enumerate(d.vc):
                if v > vc[i]:
                    vc[i] = v
            for k, v in d.dvc.items():
                if v > dvc.get(k, 0):
                    dvc[k] = v
        if op.is_dma:
            c = self.dma_cnt.get(dma_key, 0) + 1
            self.dma_cnt[dma_key] = c
            op.val = 16 * c
        op.vc = vc
        op.dvc = dvc
        slot = ('d', dma_key) if op.is_dma else eng
        for b in reads:
            self.readers.setdefault(b, {})[slot] = op
        for b in writes:
            self.last_writer[b] = op
            self.readers[b] = {}
        self.eops[eng].append(op)
        self.last[eng] = op
        self.nops += 1
        return op

    def emit(self, final_wait_ops=()):
        nc = self.nc
        with ExitStack() as st:
            for e in CE:
                cnt = 0
                sem = None
                ep = 0
                for op in self.eops[e]:
                    if op.is_dma or not op.signal:
                        continue
                    if sem is None or cnt >= EPOCH:
                        sem = st.enter_context(nc.semaphore("s_%s_%d" % (e, ep)))
                        ep += 1
                        cnt = 0
                    cnt += 1
                    op.sem = sem
                    op.val = cnt
            dsem = {}
            for i, k in enumerate(self.dma_cnt):
                dsem[k] = st.enter_context(nc.semaphore("d_%d" % i))
            for e in CE:
                for op in self.eops[e]:
                    if op.is_dma:
                        op.sem = dsem[op.dma_key]
            block = st.enter_context(nc.Block())

            def run(engname):
                def body(eng):
                    for op in self.eops[engname]:
                        for d in op.waits:
                            eng.wait_ge(d.sem, d.val)
                        ins = op.fn(eng)
                        if op.is_dma:
                            ins.then_inc(op.sem, 16)
                        elif op.signal:
                            ins.then_inc(op.sem, 1)
                    if engname == 'sp':
                        for d in final_wait_ops:
                            eng.wait_ge(d.sem, d.val)
                return body

            block.tensor(run('pe'))
            block.scalar(run('act'))
            block.vector(run('dve'))
            block.gpsimd(run('pool'))
            block.sync(run('sp'))


def build(nlayers=DEPTH, dbg=False):
    nc = bass.Bass("TRN2", target_bir_lowering=False)
    dram = lambda name, shape, dt, kind="ExternalInput": nc.dram_tensor(name, shape, dt, kind=kind).ap()
    x_d = dram("x", [S, D], F32)
    pos_d = dram("pos", [128, NT], I32)
    w_in_d = dram("w_in", [DEPTH, D, INC], F32)
    w_out_d = dram("w_out", [DEPTH, D, D], F32)
    fg_d = dram("ffn_w_gate", [2, D, DFF], F32)
    fu_d = dram("ffn_w_up", [2, D, DFF], F32)
    fd_d = dram("ffn_w_down", [2, DFF, D], F32)
    mg_d = dram("moe_w_gate", [2, NE, D, DFF], F32)
    mu_d = dram("moe_w_up", [2, NE, D, DFF], F32)
    md_d = dram("moe_w_down", [2, NE, DFF, D], F32)
    convw_d = dram("convw", [DEPTH, 128, 12, 4], F32)
    pv_d = dram("pv", [DEPTH, 128, 144], F32)
    lnp_d = dram("lnp", [DEPTH, 2, 128, 2048], F32)
    rw_d = dram("rw", [2, 128, KC, NE], F32)
    cst_d = dram("cst", [128, 928], F32)
    y_d = dram("y", [S, D], F32, kind="ExternalOutput")
    fm_s = dram("fm_s", [1536, S], BF16, kind="Internal")
    tm_s = dram("tm_s", [S, TMW], F32, kind="Internal")
    if dbg:
        dbg_o = dram("dbg_o", [S, D], F32, kind="ExternalOutput")
        dbg_x1 = dram("dbg_x1", [S, D], F32, kind="ExternalOutput")
        dbg_raw = dram("dbg_raw", [S, 512], F32, kind="ExternalOutput")

    st = ExitStack()
    with st:
        sb = lambda name, shape, dt: st.enter_context(nc.sbuf_tensor("s_" + name, shape, dt))
        xres = sb("xres", [128, NT, D], F32)
        xT = sb("xT", [128, KC, S], BF16)
        R = sb("R", [128, 32768], BF16)
        wa = [R[:, s * 4096:(s + 1) * 4096].rearrange("p (k n) -> p k n", k=KC) for s in range(2)]
        wout = R[:, 8192:16384].rearrange("p (k n) -> p k n", k=KC)
        wg = [R[:, s * 4096:(s + 1) * 4096].rearrange("p (k n) -> p k n", k=KC) for s in range(2)]
        wu = [R[:, 8192 + s * 4096:8192 + (s + 1) * 4096].rearrange("p (k n) -> p k n", k=KC) for s in range(2)]
        wd = [R[:, 16384 + s * 4096:16384 + (s + 1) * 4096].rearrange("p (f n) -> p f n", f=4) for s in range(2)]
        hT = R[:, 24576:32768].rearrange("p (f n) -> p f n", f=4)
        rwb = R[:, 16384:32768].bitcast(F32).rearrange("p (e d) -> p e d", e=NE)
        rv = lambda a, n: R[:, a:a + n]
        rf = lambda a, n: R[:, a:a + n].bitcast(F32)
        kTa = rv(0, 2048)
        va = rv(2048, 2048).rearrange("p (t n) -> p t n", t=NT)
        Sf = rf(4096, 1024).rearrange("p (h n) -> p h n", h=4)
        Sb = rv(5120, 512).rearrange("p (h n) -> p h n", h=4)
        o4 = rf(5632, 1024).rearrange("p (h n) -> p h n", h=4)
        zs = rv(6656, 512)
        qr = rv(7168, 512)
        qTs = rv(7680, 512).rearrange("p (a n) -> p a n", a=4)
        stg = [rf(16384 + i * 1024, 1024) for i in range(4)]
        stA = [rf(20480 + i * 1030, 1030) for i in range(4)]
        cvb = rf(24600, 1024)
        rnb = rf(25624, 1024)
        sqb = [rv(26648 + i * 512, 512) for i in range(2)]
        outb = [rv(27672 + i * 512, 512) for i in range(2)]
        yb = rv(25672, 1536).rearrange("p (c w) -> p c w", c=12)
        tmv = rf(29256, 2576)
        cst = sb("cst", [128, 928], F32)
        identf = cst[:, 0:128]
        tri = cst[:, 128:256]
        onesf = cst[:, 256:384]
        maskA = cst[:, 384:512]
        maskB = cst[:, 512:640]
        swam = cst[:, 640:896]
        invf = cst[:, 896:928]
        identb = sb("identb", [128, 128], BF16)
        onesb = sb("onesb", [128, 128], BF16)
        cvals = sb("cvals", [128, 4], F32)
        posi = sb("posi", [128, NT], I32)
        posf = sb("posf", [128, NT], F32)
        cosT = sb("cosT", [128, NT, 32], F32)
        sinT = sb("sinT", [128, NT, 32], F32)
        lnp = sb("lnp", [128, 2048], F32)
        convw = sb("convw", [128, 12, 4], F32)
        pv = sb("pv", [128, 144], F32)
        ab_all = sb("ab_all", [128, NT, 8], F32)
        pp = {n: sb("pp_" + n, [128, 64], F32) for n in
              ['ta', 'sp', 'g', 'eb', 'beta', 'nbeta', 'gcum', 'ngc', 'egc', 'bege']}
        negA = sb("negA", [128, 4], F32)
        ytmp = sb("ytmp", [128, D], F32)
        ptmp = None
        o4b = ytmp[:, 0:512].rearrange("p (h n) -> p h n", h=4)
        sgt = [ytmp[:, 0:512], ytmp[:, 512:1024]]
        kr = sb("kr", [128, 128], BF16)
        pb_ = sb("pb_", [128, 1024], BF16)
        mask8b = sb("mask8b", [128, 256], BF16)
        pTb = sb("pTb", [128, 1024], BF16)
        st8 = sb("st8", [128, 24], F32)
        o_tok = sb("o_tok", [128, D], BF16)
        oT = sb("oT", [128, KC, 128], BF16)
        lnst = sb("lnst", [128, 16], F32)
        DN = []
        for i_ in range(2):
            n_ = str(i_)
            B_ = {'n': n_, 'bps': 3 + 2 * i_, 'brd': 4 + 2 * i_}
            if i_ == 0:
                f32t = lambda nm, w_: sb(nm + n_, [128, w_], F32)
                b16t = lambda nm, w_: sb(nm + n_, [128, w_], BF16)
            else:
                off = [22600]
                offb = [31832]
                def f32t(nm, w_):
                    a_ = rf(off[0], 2 * w_)
                    off[0] += 2 * w_
                    return a_
                def b16t(nm, w_):
                    a_ = rv(offb[0], w_)
                    offb[0] += w_
                    return a_
            B_['trg'] = f32t('trg', 128)
            B_['dL'] = f32t('dL', 128)
            B_['dAT'] = f32t('dAT', 128)
            B_['dsc'] = f32t('dsc', 4)
            B_['abp'] = [f32t('abpa', 384), f32t('abpb', 384)]
            B_['u_sb'] = f32t('u_sb', 128)
            B_['t1'] = f32t('t1', 128)
            B_['ATm'] = b16t('ATm', 128)
            B_['TT'] = b16t('TT', 128)
            B_['Xv'] = b16t('Xv', 128)
            B_['Xw'] = b16t('Xw', 128)
            B_['ke'] = b16t('ke', 128)
            B_['wT_sb'] = b16t('wT_sb', 128)
            B_['vn'] = b16t('vn', 128)
            if i_ == 1:
                assert off[0] <= 25672 and offb[0] <= 32768, (off[0], offb[0])
            DN.append(B_)
        rs4 = sb("rs4", [128, 8], F32)
        lg = sb("lg", [128, NT, NE], F32)
        rw32 = sb("rw32", [128, KC, NE], F32)
        comb = sb("comb", [128, NT, NE], F32)
        mt = {n: sb("mt_" + n, [128, NT, NE], F32) for n in ['eq1', 'l2', 'eq2']}
        ms = {n: sb("ms_" + n, [128, NT], F32) for n in ['m1', 'm2', 'd', 'ed', 'g1', 'g2']}
        PB = [st.enter_context(nc.psum_tensor("pb%d" % i, [128, 512], F32)) for i in range(8)]
        PBb = [p[:].bitcast(BF16) for p in PB]

        print("sbuf bytes remaining:", nc.sbuf_bytes_remaining)
        Sc = Sched(nc)
        def alias_groups(groups):
            for i in range(len(groups)):
                for j in range(i + 1, len(groups)):
                    for a_ in groups[i]:
                        for b_ in groups[j]:
                            Sc.alias(a_, b_)
        alias_groups([['wa0', 'wa1'], ['kTa', 'va', 'Sf', 'Sb', 'o4', 'zs', 'qr', 'qTs'], ['wg0', 'wg1']])
        alias_groups([['wout'], ['wu0', 'wu1']])
        set1 = [nm + '1' for nm in ['trg', 'dL', 'dAT', 'dsc', 'abp', 'u_sb', 't1', 'ATm', 'TT', 'Xv', 'Xw', 'ke', 'wT_sb', 'vn']]
        alias_groups([['stg', 'stA', 'cvb', 'rnb', 'sqb', 'outb'], ['yb', 'tmv'] + set1, ['wd0', 'wd1', 'hT']])

        def A(eng, meth, *args, r=(), w=(), **kw):
            return Sc.add(eng, lambda e: getattr(e, meth)(*args, **kw), reads=r, writes=w)

        def DMA(eng, out, in_, r=(), w=(), key=None):
            return Sc.add(eng, lambda e: e.dma_start(out=out, in_=in_), reads=r, writes=w, dma_key=key)

        def MM(out, lhsT, rhs, start, stop, r, w):
            return Sc.add('pe', lambda e: e.matmul(out, lhsT, rhs, start=start, stop=stop), reads=r, writes=w)

        def TR(out, in_, ident, r, w):
            return Sc.add('pe', lambda e: e.transpose(out, in_, ident), reads=r, writes=w)

        def ACT(out, in_, func, r, w, **kw):
            return Sc.add('act', lambda e: e.activation(out, in_, func, **kw), reads=r, writes=w)

        rr = [0]

        def evac_eng():
            rr[0] += 1
            return 'act' if rr[0] % 2 == 0 else 'dve'

        def COPY(eng, out, in_, r, w):
            if eng == 'act':
                return ACT(out, in_, AF.Copy, r, w)
            return A(eng, 'tensor_copy', out, in_, r=r, w=w)

        DMA('sp', cst[:], cst_d, w=['cst'], key='cst')
        DMA('sp', posi[:], pos_d, w=['posi'], key='posi')
        DMA('sp', xres[:], x_d.rearrange("(n p) d -> p n d", p=128), w=['xres'], key='xin')
        A('dve', 'tensor_copy', identb[:], identf, r=['cst'], w=['identb'])
        A('dve', 'tensor_copy', onesb[:], onesf, r=['cst'], w=['onesb'])
        A('dve', 'tensor_scalar', mask8b[:], swam, 8.0, None, ALU.mult, r=['cst'], w=['mask8b'])
        A('dve', 'memset', cvals[:, 0:1], -math.pi, w=['cvals'])
        A('dve', 'memset', cvals[:, 1:2], 1.0, w=['cvals'])
        A('dve', 'memset', cvals[:, 2:3], 1e-5, w=['cvals'])
        A('dve', 'memset', cvals[:, 3:4], 1e-6, w=['cvals'])
        A('dve', 'tensor_copy', posf[:], posi[:], r=['posi'], w=['posf'])
        A('dve', 'tensor_tensor', cosT[:], posf[:].unsqueeze(2).to_broadcast([128, NT, 32]),
          invf.unsqueeze(1).to_broadcast([128, NT, 32]), ALU.mult, r=['posf', 'cst'], w=['cosT'])
        TWO_PI = 2 * math.pi
        angi = ytmp[:, 512:1024].bitcast(I32).rearrange("p (t n) -> p t n", t=NT)
        angk = ytmp[:, 0:512].rearrange("p (t n) -> p t n", t=NT)
        A('dve', 'tensor_copy', sinT[:], cosT[:], r=['cosT'], w=['sinT'])
        A('dve', 'tensor_scalar', cosT[:], cosT[:], 0.5 * math.pi, None, ALU.add, r=['cosT'], w=['cosT'])
        for tab, key in ((sinT, 'sinT'), (cosT, 'cosT')):
            A('dve', 'tensor_scalar', angk[:], tab[:], 1.0 / TWO_PI, None, ALU.mult, r=[key], w=[('ytmp', 'h0')])
            A('dve', 'tensor_copy', angi[:], angk[:], r=[('ytmp', 'h0')], w=[('ytmp', 'h1')])
            A('dve', 'tensor_copy', angk[:], angi[:], r=[('ytmp', 'h1')], w=[('ytmp', 'h0')])
            A('dve', 'scalar_tensor_tensor', tab[:], angk[:], -TWO_PI, tab[:], ALU.mult, ALU.add, r=[('ytmp', 'h0'), key], w=[key])
            A('dve', 'tensor_single_scalar', angk[:], tab[:], math.pi, ALU.is_gt, r=[key], w=[('ytmp', 'h0')])
            A('dve', 'scalar_tensor_tensor', tab[:], angk[:], -TWO_PI, tab[:], ALU.mult, ALU.add, r=[('ytmp', 'h0'), key], w=[key])
            A('dve', 'tensor_single_scalar', angk[:], tab[:], -math.pi, ALU.is_lt, r=[key], w=[('ytmp', 'h0')])
            A('dve', 'scalar_tensor_tensor', tab[:], angk[:], TWO_PI, tab[:], ALU.mult, ALU.add, r=[('ytmp', 'h0'), key], w=[key])
            A('dve', 'tensor_scalar', tab[:], tab[:], math.pi, -math.pi, ALU.min, ALU.max, r=[key], w=[key])
            ACT(tab[:], tab[:], AF.Sin, [key], [key])

        def emit_xT(t):
            COPY('act', o_tok[:], xres[:, t, :], [('xres', t)], ['o_tok'])
            pbank = 7
            for kc in range(KC):
                TR(PBb[pbank][:, kc * 128:(kc + 1) * 128], o_tok[:, kc * 128:(kc + 1) * 128], identb[:],
                   ['o_tok', 'identb'], [('pb', pbank)])
            COPY('dve', xT[:, :, t * 128:(t + 1) * 128],
                 PBb[pbank][:, 0:1024].rearrange("p (k n) -> p k n", k=KC), [('pb', pbank)], [('xT', t)])

        def layer_norm(src_ap, src_key, t, lidx):
            for hh in range(2):
                A('dve', 'bn_stats', lnst[:, hh * 6:(hh + 1) * 6], src_ap[:, hh * 512:(hh + 1) * 512],
                  r=[src_key], w=['lnst'])
            A('dve', 'bn_aggr', lnst[:, 12:14], lnst[:, 0:12], r=['lnst'], w=['lnst2'])
            ACT(lnst[:, 15:16], lnst[:, 13:14], AF.Sqrt, ['lnst2', 'cvals'], ['lnst4'], bias=cvals[:, 2:3])
            A('dve', 'reciprocal', lnst[:, 14:15], lnst[:, 15:16], r=['lnst4'], w=['lnst3'])
            A('dve', 'tensor_scalar', ytmp[:], src_ap, lnst[:, 12:13], lnst[:, 14:15], ALU.subtract, ALU.mult,
              r=[src_key, 'lnst2', 'lnst3'], w=['ytmp'])
            A('pool', 'tensor_tensor', ytmp[:], ytmp[:], lnp[:, 0:1024], ALU.mult, r=['ytmp', 'lnp'], w=['ytmp'])
            A('pool', 'tensor_tensor', xres[:, t, :], ytmp[:], lnp[:, 1024:2048], ALU.add,
              r=['ytmp', 'lnp'], w=[('xres', t)])

        for t in range(NT):
            emit_xT(t)

        wq = [0]

        for l in range(nlayers):
            DMA('sp', convw[:], convw_d[l], w=['convw'], key='convw')
            DMA('sp', pv[:], pv_d[l], w=['pv'], key='pv')
            DMA('sp', lnp[:], lnp_d[l, 0], w=['lnp'], key='lnp')
            DMA('pool', wout, w_out_d[l].rearrange("(k p) n -> p k n", p=128), w=['wout'], key='wout')
            alog = pv[:, 0:4]
            dtb = pv[:, 4:8]
            sinks = pv[:, 8:16]
            dnw = pv[:, 16:144]
            w_in_v = w_in_d[l].rearrange("(k p) n -> p k n", p=128)
            blocks = [(0, 512, 'fm'), (512, 1024, 'fm'), (1024, 1536, 'fm'),
                      (1536, 2048, 'tm'), (2048, 2560, 'tm'), (2560, 2824, 'tm')]
            cnt = 0
            for bi, (c0, c1, kind) in enumerate(blocks):
                s_ = wq[0] % 2
                wq[0] += 1
                wkey = 'wa%d' % s_
                wcols = c1 - c0
                DMA('pool', wa[s_][:, :, 0:wcols], w_in_v[:, :, c0:c1], w=[wkey], key=wkey)
                if kind == 'fm':
                    for j in range(4):
                        ch = (c0 // 128) + j
                        isv = (c0 == 1024)
                        for tg in range(4):
                            pbk = tg
                            for kc in range(KC):
                                MM(PB[pbk][:], wa[s_][:, kc, j * 128:(j + 1) * 128], xT[:, kc, tg * 512:(tg + 1) * 512],
                                   kc == 0, kc == KC - 1, [wkey] + [('xT', tg * 4 + i) for i in range(4)], [('pb', pbk)])
                            COPY('act', stA[tg][:, 3:515], PB[pbk][:], [('pb', pbk)], [('stA', tg)])
                            if tg == 0:
                                A('pool', 'memset', stA[tg][:, 0:3], 0.0, w=[('stA', tg, 'h')])
                            else:
                                A('pool', 'tensor_copy', stA[tg][:, 0:3], stA[tg - 1][:, 512:515], r=[('stA', tg - 1)],
                                  w=[('stA', tg, 'h')])
                            A('dve', 'tensor_scalar', cvb, stA[tg][:, 0:512], convw[:, ch, 0:1], None, ALU.mult,
                              r=[('stA', tg), 'convw'], w=['cvb'])
                            for k in range(1, 4):
                                A('dve', 'scalar_tensor_tensor', cvb, stA[tg][:, k:k + 512], convw[:, ch, k:k + 1], cvb,
                                  ALU.mult, ALU.add, r=[('stA', tg), 'convw', 'cvb'], w=['cvb'])
                            if isv:
                                ob = cnt % 2
                                cnt += 1
                                ACT(outb[ob], cvb, AF.Silu, ['cvb'], [('outb', ob)])
                                DMA('sp', fm_s[ch * 128:(ch + 1) * 128, tg * 512:(tg + 1) * 512], outb[ob],
                                    r=[('outb', ob)], w=[('fm_s', ch, tg)], key='outb%d' % ob)
                            else:
                                ACT(stA[tg][:, 0:512], cvb, AF.Silu, ['cvb'], [('stA', tg, 'y')])
                                sb_ = tg % 2
                                ACT(sqb[sb_], stA[tg][:, 0:512], AF.Square, [('stA', tg, 'y')], [('sqb', sb_)])
                                MM(PB[4 + tg][:], onesb[:], sqb[sb_], True, True, ['onesb', ('sqb', sb_)], [('pb', 4 + tg)])
                        if not isv:
                            for tg in range(4):
                                ob = cnt % 2
                                cnt += 1
                                ACT(rnb, PB[4 + tg][:], AF.Sqrt, [('pb', 4 + tg), 'cvals'], ['rnb'], bias=cvals[:, 3:4])
                                A('dve', 'reciprocal', rnb, rnb, r=['rnb'], w=['rnb'])
                                A('dve', 'tensor_tensor', outb[ob], stA[tg][:, 0:512], rnb, ALU.mult,
                                  r=[('stA', tg, 'y'), 'rnb'], w=[('outb', ob)])
                                DMA('sp', fm_s[ch * 128:(ch + 1) * 128, tg * 512:(tg + 1) * 512], outb[ob],
                                    r=[('outb', ob)], w=[('fm_s', ch, tg)], key='outb%d' % ob)
                else:
                    for t in range(NT):
                        pbk = cnt % 4
                        q_ = cnt % 4
                        cnt += 1
                        for kc in range(KC):
                            MM(PB[pbk][:, 0:wcols], xT[:, kc, t * 128:(t + 1) * 128], wa[s_][:, kc, 0:wcols],
                               kc == 0, kc == KC - 1, [wkey, ('xT', t)], [('pb', pbk)])
                        COPY(evac_eng(), stg[q_][:, 0:wcols], PB[pbk][:, 0:wcols], [('pb', pbk)], [('stg', q_)])
                        if c0 == 2048:
                            COPY('act', ab_all[:, t, :], PB[pbk][:, 0:8], [('pb', pbk)], [('ab_all', t)])
                        DMA('sp', tm_s[t * 128:(t + 1) * 128, c0 - 1536:c1 - 1536], stg[q_][:, 0:wcols],
                            r=[('stg', q_)], w=[('tm_s', t, bi)], key='stg%d' % q_)
            v3 = lambda n: pp[n][:].rearrange("p (t h) -> p t h", h=4)
            A('dve', 'tensor_tensor', v3('ta'), ab_all[:, :, 0:4], dtb.unsqueeze(1).to_broadcast([128, NT, 4]), ALU.add,
              r=['ab_all', 'pv'], w=['pp_ta'])
            ACT(pp['ta'][:], pp['ta'][:], AF.Exp, ['pp_ta'], ['pp_ta'])
            ACT(pp['sp'][:], pp['ta'][:], AF.Ln, ['pp_ta', 'cvals'], ['pp_sp'], bias=cvals[:, 1:2])
            ACT(negA[:], alog, AF.Exp, ['pv'], ['negA'])
            A('dve', 'tensor_scalar', negA[:], negA[:], -1.0, None, ALU.mult, r=['negA'], w=['negA'])
            A('dve', 'tensor_tensor', v3('g'), v3('sp'), negA[:].unsqueeze(1).to_broadcast([128, NT, 4]), ALU.mult,
              r=['pp_sp', 'negA'], w=['pp_g'])
            ACT(v3('eb'), ab_all[:, :, 4:8], AF.Exp, ['ab_all'], ['pp_eb'], scale=-1.0)
            A('dve', 'tensor_scalar', pp['eb'][:], pp['eb'][:], 1.0, None, ALU.add, r=['pp_eb'], w=['pp_eb'])
            A('dve', 'reciprocal', pp['beta'][:], pp['eb'][:], r=['pp_eb'], w=['pp_beta'])
            A('dve', 'tensor_scalar', pp['nbeta'][:], pp['beta'][:], -1.0, None, ALU.mult, r=['pp_beta'], w=['pp_nbeta'])
            MM(PB[4][:, 0:64], tri, pp['g'][:], True, True, ['cst', 'pp_g'], [('pb', 4)])
            COPY('dve', pp['gcum'][:], PB[4][:, 0:64], [('pb', 4)], ['pp_gcum'])
            A('dve', 'tensor_scalar', pp['ngc'][:], pp['gcum'][:], -1.0, None, ALU.mult, r=['pp_gcum'], w=['pp_ngc'])
            ACT(pp['egc'][:], pp['gcum'][:], AF.Exp, ['pp_gcum'], ['pp_egc'])
            A('dve', 'tensor_tensor', pp['bege'][:], pp['beta'][:], pp['egc'][:], ALU.mult,
              r=['pp_beta', 'pp_egc'], w=['pp_bege'])
            A('pool', 'memset', Sf[:], 0.0, w=['Sf'])
            A('pool', 'memset', Sb[:], 0.0, w=['Sb'])

            def run_rr(gens):
                gens = list(gens)
                while gens:
                    for g_ in list(gens):
                        try:
                            next(g_)
                        except StopIteration:
                            gens.remove(g_)

            def prologue(t):
                DMA('sp', yb[:], fm_s.rearrange("(c p) w -> p c w", p=128)[:, :, t * 128:(t + 1) * 128],
                    r=['fm_s'], w=['yb'], key='yb')
                DMA('sp', tmv[:], tm_s[t * 128:(t + 1) * 128, :], r=[('tm_s', t)], w=['tmv'], key='tmv')
                yield
                ACT(zs[:], tmv[:, 0:512], AF.Silu, ['tmv'], ['zs'])
                for h in range(4):
                    TR(PBb[2][:, h * 128:(h + 1) * 128], yb[:, 4 + h, :], identb[:], ['yb', 'identb'], [('pb', 2)])
                for h in range(4):
                    TR(PBb[2][:, (4 + h) * 128:(5 + h) * 128], yb[:, 8 + h, :], identb[:], ['yb', 'identb'], [('pb', 2)])
                yield

            def dn_head(t, h, B_):
                ix = t * 4 + h
                col = lambda n: pp[n][:, ix:ix + 1]
                bps, brd = B_['bps'], B_['brd']
                kps, krd = ('pb', bps), ('pb', brd)
                n_ = B_['n']
                K_ = lambda nm: nm + n_
                A('pool', 'tensor_scalar', B_['trg'][:], tri, col('g'), -1.0, ALU.mult, ALU.mult, r=['cst', 'pp_g'], w=[K_('trg')])
                MM(PB[bps][:, 0:128], onesf, B_['trg'][:], True, False, ['cst', K_('trg')], [kps])
                MM(PB[bps][:, 0:128], identf, maskA, False, True, ['cst'], [kps])
                MM(PB[bps][:, 128:256], onesf, B_['trg'][:], True, False, ['cst', K_('trg')], [kps])
                MM(PB[bps][:, 128:256], identf, maskB, False, True, ['cst'], [kps])
                kTh = yb[:, 4 + h, :]
                qTh = yb[:, h, :]
                MM(PB[bps][:, 256:384], kTh, kTh, True, True, ['yb'], [kps])
                MM(PB[bps][:, 384:512], kTh, qTh, True, True, ['yb'], [kps])
                yield
                dsc_ = B_['dsc']
                ACT(B_['dL'][:], PB[bps][:, 0:128], AF.Exp, [kps, 'pp_gcum'], [K_('dL')], bias=col('gcum'))
                ACT(B_['dAT'][:], PB[bps][:, 128:256], AF.Exp, [kps, 'pp_ngc'], [K_('dAT')], bias=col('ngc'), scale=-1.0)
                ACT(dsc_[:, 0:1], PB[bps][:, 255:256], AF.Exp, [kps, 'pp_ngc'], [K_('dsc')], bias=col('ngc'), scale=-1.0)
                ACT(dsc_[:, 1:2], PB[bps][:, 255:256], AF.Exp, [kps], [K_('dsc')], scale=-1.0)
                abp_ = B_['abp']
                A('dve', 'scalar_tensor_tensor', abp_[0][:, 0:128], PB[bps][:, 256:384], col('nbeta'), B_['dL'][:], ALU.mult, ALU.mult,
                  r=[kps, 'pp_nbeta', K_('dL')], w=[(K_('abp'), 0)])
                A('dve', 'tensor_tensor', B_['ATm'][:], PB[bps][:, 384:512], B_['dAT'][:], ALU.mult, r=[kps, K_('dAT')], w=[K_('ATm')])
                yield
                TR(PB[brd][:, 0:128], abp_[0][:, 0:128], identf, [(K_('abp'), 0), 'cst'], [krd])
                COPY('act', abp_[0][:, 128:256], PB[brd][:, 0:128], [krd], [(K_('abp'), 0)])
                A('pool', 'tensor_copy', abp_[0][:, 256:384], identf, r=['cst'], w=[(K_('abp'), 0)])
                yield
                cur = 0
                for rnd in range(1, 8):
                    Ac = abp_[cur][:, 0:128]
                    Bc = abp_[cur][:, 128:256]
                    Pc = abp_[cur][:, 256:384]
                    nxt = 1 - cur
                    rk = [(K_('abp'), cur), 'cst']
                    if rnd <= 6:
                        MM(PB[brd][:, 0:128], Bc, Ac, True, True, rk, [krd])
                        if rnd <= 5:
                            MM(PB[brd][:, 128:256], Ac, Bc, True, True, rk, [krd])
                    MM(PB[brd][:, 256:384], identf, Pc, True, False, rk, [krd])
                    MM(PB[brd][:, 256:384], Ac, Pc, False, True, rk, [krd])
                    if rnd <= 6:
                        COPY(evac_eng(), abp_[nxt][:, 0:384], PB[brd][:, 0:384], [krd], [(K_('abp'), nxt)])
                    else:
                        COPY(evac_eng(), B_['TT'][:], PB[brd][:, 256:384], [krd], [K_('TT')])
                    cur = nxt
                    yield
                ktok = PBb[2][:, h * 128:(h + 1) * 128]
                vtok = PBb[2][:, (4 + h) * 128:(5 + h) * 128]
                A('dve', 'tensor_scalar', B_['Xv'][:], vtok, col('beta'), None, ALU.mult, r=[('pb', 2), 'pp_beta'], w=[K_('Xv')])
                A('dve', 'tensor_scalar', B_['Xw'][:], ktok, col('bege'), None, ALU.mult, r=[('pb', 2), 'pp_bege'], w=[K_('Xw')])
                A('dve', 'tensor_scalar', B_['ke'][:], ktok, dsc_[:, 0:1], None, ALU.mult, r=[('pb', 2), K_('dsc')], w=[K_('ke')])
                MM(PB[brd][:, 0:128], B_['TT'][:], B_['Xv'][:], True, True, [K_('TT'), K_('Xv')], [krd])
                MM(PB[brd][:, 128:256], B_['Xw'][:], B_['TT'][:], True, True, [K_('TT'), K_('Xw')], [krd])
                yield
                COPY('act', B_['u_sb'][:], PB[brd][:, 0:128], [krd], [K_('u_sb')])
                COPY('act', B_['wT_sb'][:], PB[brd][:, 128:256], [krd], [K_('wT_sb')])
                MM(PB[brd][:, 256:384], B_['wT_sb'][:], Sb[:, h, :], True, True, [K_('wT_sb'), ('Sb', h)], [krd])
                MM(PB[brd][:, 384:512], qTh, Sb[:, h, :], True, True, ['yb', ('Sb', h)], [krd])
                yield
                A('dve', 'tensor_tensor', B_['vn'][:], B_['u_sb'][:], PB[brd][:, 256:384], ALU.subtract, r=[K_('u_sb'), krd], w=[K_('vn')])
                A('dve', 'tensor_scalar', B_['t1'][:], PB[brd][:, 384:512], col('egc'), None, ALU.mult, r=[krd, 'pp_egc'], w=[K_('t1')])
                MM(PB[bps][:, 0:128], B_['ATm'][:], B_['vn'][:], True, True, [K_('ATm'), K_('vn')], [kps])
                MM(PB[bps][:, 128:256], B_['ke'][:], B_['vn'][:], True, True, [K_('ke'), K_('vn')], [kps])
                yield
                A('dve', 'tensor_tensor', o4[:, h, :], B_['t1'][:], PB[bps][:, 0:128], ALU.add, r=[K_('t1'), kps], w=[('o4', h)])
                A('dve', 'scalar_tensor_tensor', Sf[:, h, :], Sf[:, h, :], dsc_[:, 1:2], PB[bps][:, 128:256], ALU.mult, ALU.add,
                  r=[('Sf', h), K_('dsc'), kps], w=[('Sf', h)])
                COPY('act', Sb[:, h, :], Sf[:, h, :], [('Sf', h)], [('Sb', h)])
                yield

            def dn_stream(t, heads, B_):
                for h in heads:
                    yield from dn_head(t, h, B_)

            def rms_gate(t):
                A('pool', 'tensor_tensor', o4b[:], o4[:], o4[:], ALU.mult, r=['o4'], w=['ytmp'])
                A('dve', 'tensor_reduce', rs4[:, 0:4], o4b[:], AX.X, ALU.add, r=['ytmp'], w=['rs4'])
                ACT(rs4[:, 0:4], rs4[:, 0:4], AF.Sqrt, ['rs4', 'cvals'], ['rs4'], bias=cvals[:, 3:4], scale=1.0 / 16384.0)
                A('dve', 'reciprocal', rs4[:, 0:4], rs4[:, 0:4], r=['rs4'], w=['rs4'])
                A('dve', 'tensor_scalar', rs4[:, 4:8], rs4[:, 0:4], 128.0 ** -0.5, None, ALU.mult, r=['rs4'], w=['rs4b'])
                A('dve', 'tensor_tensor', o4b[:], o4[:], rs4[:, 4:8].unsqueeze(2).to_broadcast([128, 4, 128]), ALU.mult,
                  r=['o4', 'rs4b'], w=['ytmp'])
                A('pool', 'tensor_tensor', o4b[:], o4b[:], dnw.unsqueeze(1).to_broadcast([128, 4, 128]), ALU.mult,
                  r=['ytmp', 'pv'], w=['ytmp'])
                A('dve', 'tensor_tensor', o_tok[:, 0:512].rearrange("p (h d) -> p h d", h=4), o4b[:],
                  zs[:].rearrange("p (h d) -> p h d", h=4), ALU.mult, r=['ytmp', 'zs'], w=[('o_tok', 0)])

            def swa(t):
                def rope(src, nh, dst4, keyw):
                    s4 = src.rearrange("p (h two d) -> p h two d", two=2, d=32)
                    x1 = s4[:, :, 0, :]
                    x2 = s4[:, :, 1, :]
                    cs = cosT[:, t, :].unsqueeze(1).to_broadcast([128, nh, 32])
                    sn = sinT[:, t, :].unsqueeze(1).to_broadcast([128, nh, 32])
                    w4 = lambda i: ytmp[:, i * 256:i * 256 + nh * 32].rearrange("p (h d) -> p h d", d=32)
                    A('pool', 'tensor_tensor', w4(0), x1, cs, ALU.mult, r=['tmv', 'cosT'], w=[('ytmp', 0)])
                    A('pool', 'tensor_tensor', w4(1), x2, sn, ALU.mult, r=['tmv', 'sinT'], w=[('ytmp', 1)])
                    A('pool', 'tensor_tensor', w4(2), x2, cs, ALU.mult, r=['tmv', 'cosT'], w=[('ytmp', 2)])
                    A('pool', 'tensor_tensor', w4(3), x1, sn, ALU.mult, r=['tmv', 'sinT'], w=[('ytmp', 3)])
                    A('pool', 'tensor_tensor', dst4[0], w4(0), w4(1), ALU.subtract, r=[('ytmp', 0), ('ytmp', 1)], w=[keyw])
                    A('pool', 'tensor_tensor', dst4[1], w4(2), w4(3), ALU.add, r=[('ytmp', 2), ('ytmp', 3)], w=[keyw])
                qv = qr[:].rearrange("p (pr two half d) -> p pr two half d", pr=4, two=2, half=2)
                for two in range(2):
                    srcq = tmv[:, 520 + two * 256:520 + (two + 1) * 256]
                    rope(srcq, 4, (qv[:, :, two, 0, :], qv[:, :, two, 1, :]), 'qr')
                    yield
                kv4 = kr[:].rearrange("p (h half d) -> p h half d", h=2, half=2)
                rope(tmv[:, 1032:1160], 2, (kv4[:, :, 0, :], kv4[:, :, 1, :]), 'kr')
                COPY('act', va[:, t, :], tmv[:, 1160:1288], ['tmv'], [('va', t)])
                yield
                for pr in range(4):
                    TR(PBb[7][:, pr * 128:(pr + 1) * 128], qr[:, pr * 128:(pr + 1) * 128], identb[:], ['qr', 'identb'], [('pb', 7)])
                TR(PBb[7][:, 512:640], kr[:], identb[:], ['kr', 'identb'], [('pb', 7)])
                COPY('act', qTs[:], PBb[7][:, 0:512].rearrange("p (a n) -> p a n", a=4), [('pb', 7)], ['qTs'])
                COPY('act', kTa[:, t * 128:(t + 1) * 128], PBb[7][:, 512:640], [('pb', 7)], [('kTa', t)])
                yield
                for g_ in range(2):
                    ps_ = slice(g_ * 64, (g_ + 1) * 64)
                    if t == 0:
                        k0, nk = 0, 128
                        mkb = mask8b[:, 128:256]
                        kkeys = [('kTa', 0)]
                    else:
                        k0, nk = (t - 1) * 128, 256
                        mkb = mask8b[:]
                        kkeys = [('kTa', t - 1), ('kTa', t)]
                    nb = nk // 128
                    for j in range(4):
                        bk = j // 2
                        c0_ = (j % 2) * 256
                        MM(PB[bk][:, c0_:c0_ + nk], identb[:], mkb, True, False, ['identb', 'mask8b'], [('pb', bk)])
                        MM(PB[bk][:, c0_:c0_ + nk], qTs[ps_, j, :], kTa[ps_, k0:k0 + nk], False, True, ['qTs'] + kkeys, [('pb', bk)])
                    yield
                    for bk in range(2):
                        A('dve', 'tensor_reduce', st8[:, bk * 2:bk * 2 + 2],
                          PB[bk][:].rearrange("p (a n) -> p a n", a=2)[:, :, 0:nk], AX.X, ALU.max, r=[('pb', bk)], w=[('st8', 'mx')])
                    A('dve', 'scalar_tensor_tensor', st8[:, 4:8], st8[:, 0:4], 0.125, sinks[:, g_ * 4:g_ * 4 + 4], ALU.mult, ALU.max,
                      r=[('st8', 'mx'), 'pv'], w=[('st8', 'm')])
                    A('dve', 'tensor_scalar', st8[:, 8:12], st8[:, 4:8], -1.0, None, ALU.mult, r=[('st8', 'm')], w=[('st8', 'nm')])
                    A('dve', 'tensor_tensor', st8[:, 12:16], st8[:, 8:12], sinks[:, g_ * 4:g_ * 4 + 4], ALU.add,
                      r=[('st8', 'nm'), 'pv'], w=[('st8', 'sk')])
                    yield
                    for j in range(4):
                        bk = j // 2
                        c0_ = (j % 2) * 256
                        ACT(pb_[:, j * 256:j * 256 + nk], PB[bk][:, c0_:c0_ + nk], AF.Exp, [('pb', bk), ('st8', 'nm')],
                            [('pb_', j), ('st8', 'rs', j)], bias=st8[:, 8 + j:9 + j], scale=0.125, accum_out=st8[:, 16 + j:17 + j])
                    ACT(st8[:, 12:16], st8[:, 12:16], AF.Exp, [('st8', 'sk')], [('st8', 'sk')])
                    yield
                    for j in range(4):
                        for b_ in range(nb):
                            TR(PBb[7][:, j * 256 + b_ * 128:j * 256 + (b_ + 1) * 128], pb_[:, j * 256 + b_ * 128:j * 256 + (b_ + 1) * 128],
                               identb[:], [('pb_', j), 'identb'], [('pb', 7)])
                    A('dve', 'tensor_tensor', st8[:, 20:24], st8[:, 16:20], st8[:, 12:16], ALU.add,
                      r=[('st8', 'rs'), ('st8', 'sk')], w=[('st8', 'den')])
                    A('dve', 'reciprocal', st8[:, 20:24], st8[:, 20:24], r=[('st8', 'den')], w=[('st8', 'den')])
                    yield
                    if nb == 2:
                        COPY('act', pTb[:, 0:512], PBb[7][:, 0:512], [('pb', 7)], [('pTb', 0)])
                        COPY('dve', pTb[:, 512:1024], PBb[7][:, 512:1024], [('pb', 7)], [('pTb', 1)])
                    else:
                        for j in range(4):
                            COPY('act' if j % 2 == 0 else 'dve', pTb[:, j * 256:j * 256 + 128], PBb[7][:, j * 256:j * 256 + 128],
                                 [('pb', 7)], [('pTb', j // 2)])
                    yield
                    for j in range(4):
                        for b_ in range(nb):
                            tt_ = t - (nb - 1) + b_
                            MM(PB[0][:, j * 64:(j + 1) * 64], pTb[:, j * 256 + b_ * 128:j * 256 + (b_ + 1) * 128],
                               va[:, tt_, g_ * 64:(g_ + 1) * 64], b_ == 0, b_ == nb - 1, [('pTb', j // 2), ('va', tt_)], [('pb', 0)])
                    yield
                    A('dve', 'tensor_tensor', o_tok[:, 512 + g_ * 256:512 + (g_ + 1) * 256].rearrange("p (j d) -> p j d", j=4),
                      PB[0][:, 0:256].rearrange("p (j d) -> p j d", j=4),
                      st8[:, 20:24].unsqueeze(2).to_broadcast([128, 4, 64]), ALU.mult,
                      r=[('pb', 0), ('st8', 'den')], w=[('o_tok', 1, g_)])
                    yield

            def epilogue(t):
                if dbg and l == nlayers - 1:
                    COPY('dve', ytmp[:], o_tok[:], ['o_tok'], ['ytmp'])
                    DMA('sp', dbg_o[t * 128:(t + 1) * 128, :], ytmp[:], r=['ytmp'], key='dbg_o')
                    COPY('dve', ytmp[:, 0:512].rearrange("p (h d) -> p h d", h=4), o4[:], ['o4'], ['ytmp'])
                    DMA('sp', dbg_raw[t * 128:(t + 1) * 128, :], ytmp[:, 0:512], r=['ytmp'], key='dbg_raw')
                for kc in range(KC):
                    TR(PBb[7][:, kc * 128:(kc + 1) * 128], o_tok[:, kc * 128:(kc + 1) * 128], identb[:], ['o_tok', 'identb'], [('pb', 7)])
                COPY('act', oT[:], PBb[7][:, 0:1024].rearrange("p (k n) -> p k n", k=KC), [('pb', 7)], ['oT'])
                yield
                for hh in range(2):
                    for kc in range(KC):
                        MM(PB[hh][:], oT[:, kc, :], wout[:, kc, hh * 512:(hh + 1) * 512], kc == 0, kc == KC - 1,
                           ['oT', 'wout'], [('pb', hh)])
                    A('dve', 'scalar_tensor_tensor', ytmp[:, hh * 512:(hh + 1) * 512], xres[:, t, hh * 512:(hh + 1) * 512], ALPHA,
                      PB[hh][:], ALU.mult, ALU.add, r=[('xres', t), ('pb', hh)], w=['ytmp'])
                    yield
                layer_norm(ytmp[:], 'ytmp', t, 0)
                yield
                emit_xT(t)
                if dbg and l == nlayers - 1:
                    DMA('sp', dbg_x1[t * 128:(t + 1) * 128, :], xres[:, t, :], r=[('xres', t)], key='dbg_x1')
                yield

            run_rr([prologue(0)])
            for t in range(NT):
                run_rr([dn_stream(t, [0, 2], DN[0]), dn_stream(t, [1, 3], DN[1]), swa(t)])
                rms_gate(t)
                if t + 1 < NT:
                    run_rr([epilogue(t), prologue(t + 1)])
                else:
                    run_rr([epilogue(t)])

            DMA('sp', lnp[:], lnp_d[l, 1], w=['lnp'], key='lnp')
            moe = (l % 2 == 1)
            li = l // 2
            if moe:
                DMA('sp', rw32[:], rw_d[li], w=['rw32'], key='rw32')
                xTf = ytmp[:].rearrange("p (k n) -> p k n", k=KC)
                for t in range(NT):
                    for kc in range(KC):
                        bk = kc // 4
                        TR(PB[bk][:, (kc % 4) * 128:(kc % 4 + 1) * 128], xres[:, t, kc * 128:(kc + 1) * 128], identf,
                           [('xres', t), 'cst'], [('pb', bk)])
                    COPY('act', xTf[:, 0:4, :], PB[0][:].rearrange("p (k n) -> p k n", k=4), [('pb', 0)], [('ytmp', 'h0')])
                    COPY('dve', xTf[:, 4:8, :], PB[1][:].rearrange("p (k n) -> p k n", k=4), [('pb', 1)], [('ytmp', 'h1')])
                    for kc in range(KC):
                        MM(PB[2][:, 0:NE], xTf[:, kc, :], rw32[:, kc, :], kc == 0, kc == KC - 1,
                           [('ytmp', 'h0'), ('ytmp', 'h1'), 'rw32'], [('pb', 2)])
                    COPY('dve', lg[:, t, :], PB[2][:, 0:NE], [('pb', 2)], [('lg', t)])
                bc = lambda a: a[:].unsqueeze(2).to_broadcast([128, NT, NE])
                A('dve', 'tensor_reduce', ms['m1'][:], lg[:], AX.X, ALU.max, r=['lg'], w=['ms_m1'])
                A('dve', 'tensor_tensor', mt['eq1'][:], lg[:], bc(ms['m1']), ALU.is_equal, r=['lg', 'ms_m1'], w=['mt_eq1'])
                A('dve', 'scalar_tensor_tensor', mt['l2'][:], mt['eq1'][:], -1e30, lg[:], ALU.mult, ALU.add,
                  r=['mt_eq1', 'lg'], w=['mt_l2'])
                A('dve', 'tensor_reduce', ms['m2'][:], mt['l2'][:], AX.X, ALU.max, r=['mt_l2'], w=['ms_m2'])
                A('dve', 'tensor_tensor', mt['eq2'][:], mt['l2'][:], bc(ms['m2']), ALU.is_equal, r=['mt_l2', 'ms_m2'], w=['mt_eq2'])
                A('dve', 'tensor_tensor', ms['d'][:], ms['m2'][:], ms['m1'][:], ALU.subtract, r=['ms_m1', 'ms_m2'], w=['ms_d'])
                ACT(ms['ed'][:], ms['d'][:], AF.Exp, ['ms_d'], ['ms_ed'])
                A('dve', 'tensor_scalar', ms['g1'][:], ms['ed'][:], 1.0, None, ALU.add, r=['ms_ed'], w=['ms_g1'])
                A('dve', 'reciprocal', ms['g1'][:], ms['g1'][:], r=['ms_g1'], w=['ms_g1'])
                A('dve', 'tensor_tensor', ms['g2'][:], ms['ed'][:], ms['g1'][:], ALU.mult, r=['ms_ed', 'ms_g1'], w=['ms_g2'])
                A('dve', 'tensor_tensor', mt['eq1'][:], mt['eq1'][:], bc(ms['g1']), ALU.mult, r=['mt_eq1', 'ms_g1'], w=['mt_eq1'])
                A('dve', 'tensor_tensor', mt['eq2'][:], mt['eq2'][:], bc(ms['g2']), ALU.mult, r=['mt_eq2', 'ms_g2'], w=['mt_eq2'])
                A('dve', 'tensor_tensor', comb[:], mt['eq1'][:], mt['eq2'][:], ALU.add, r=['mt_eq1', 'mt_eq2'], w=['comb'])
            nexp = NE if moe else 1
            for t in range(NT):
                ACT(xres[:, t, :], xres[:, t, :], AF.Copy, [('xres', t)], [('xres', t)], scale=ALPHA)
            for e_ in range(nexp):
                if moe:
                    Wg, Wu, Wd = mg_d[li, e_], mu_d[li, e_], md_d[li, e_]
                else:
                    Wg, Wu, Wd = fg_d[li], fu_d[li], fd_d[li]
                Wgv = Wg.rearrange("(k p) f -> p k f", p=128)
                Wuv = Wu.rearrange("(k p) f -> p k f", p=128)
                Wdv = Wd.rearrange("(f p) d -> p f d", p=128)
                for grp in range(7):
                    s_ = wq[0] % 2
                    wq[0] += 1
                    kg, ku, kd = 'wg%d' % s_, 'wu%d' % s_, 'wd%d' % s_
                    DMA('pool', wg[s_], Wgv[:, :, grp * 512:(grp + 1) * 512], w=[kg], key=kg)
                    DMA('pool', wu[s_], Wuv[:, :, grp * 512:(grp + 1) * 512], w=[ku], key=ku)
                    DMA('pool', wd[s_], Wdv[:, grp * 4:(grp + 1) * 4, :], w=[kd], key=kd)
                    ci = 0
                    for fc in range(4):
                        for tg in range(4):
                            bg = (ci % 2) * 2
                            bu = bg + 1
                            ci += 1
                            xk = [('xT', tg * 4 + i) for i in range(4)]
                            for kc in range(KC):
                                MM(PB[bg][:], wg[s_][:, kc, fc * 128:(fc + 1) * 128], xT[:, kc, tg * 512:(tg + 1) * 512],
                                   kc == 0, kc == KC - 1, [kg] + xk, [('pb', bg)])
                            for kc in range(KC):
                                MM(PB[bu][:], wu[s_][:, kc, fc * 128:(fc + 1) * 128], xT[:, kc, tg * 512:(tg + 1) * 512],
                                   kc == 0, kc == KC - 1, [ku] + xk, [('pb', bu)])
                            sg = sgt[ci % 2]
                            ACT(sg[:], PB[bg][:], AF.Silu, [('pb', bg)], [('ytmp', 'h%d' % (ci % 2))])
                            A('dve', 'tensor_tensor', hT[:, fc, tg * 512:(tg + 1) * 512], sg[:], PB[bu][:], ALU.mult,
                              r=[('ytmp', 'h%d' % (ci % 2)), ('pb', bu)], w=[('hT', fc, tg)])
                    di = 0
                    for t in range(NT):
                        for hh in range(2):
                            bd = 4 + (di % 4)
                            di += 1
                            for fc in range(4):
                                MM(PB[bd][:], hT[:, fc, t * 128:(t + 1) * 128], wd[s_][:, fc, hh * 512:(hh + 1) * 512],
                                   fc == 0, fc == 3, [('hT', fc, t // 4), kd], [('pb', bd)])
                            cs_ = comb[:, t, e_:e_ + 1] if moe else 1.0
                            A('dve', 'scalar_tensor_tensor', xres[:, t, hh * 512:(hh + 1) * 512], PB[bd][:], cs_,
                              xres[:, t, hh * 512:(hh + 1) * 512], ALU.mult, ALU.add,
                              r=[('pb', bd), ('xres', t)] + (['comb'] if moe else []), w=[('xres', t)])
            for t in range(NT):
                layer_norm(xres[:, t, :], ('xres', t), t, 1)
                if l < nlayers - 1:
                    emit_xT(t)
        fin = DMA('sp', y_d.rearrange("(n p) d -> p n d", p=128), xres[:], r=['xres'], key='yout')
        Sc.emit(final_wait_ops=[fin])
        print("ops:", Sc.nops, {e: len(v) for e, v in Sc.eops.items()})
    return nc


_CONSTS = None


def host_consts():
    c = np.zeros((128, 928), np.float32)
    i = np.arange(128)
    c[:, 0:128] = np.eye(128, dtype=np.float32)
    c[:, 128:256] = (i[:, None] <= i[None, :]).astype(np.float32)
    c[:, 256:384] = 1.0
    c[:, 384:512] = np.where(i[None, :] >= i[:, None], -BIG, 0.0)
    c[:, 512:640] = np.where(i[None, :] < i[:, None], BIG, 0.0)
    j = np.arange(256)
    rel = i[:, None] + 128 - j[None, :]
    c[:, 640:896] = np.where((rel >= 0) & (rel < 128), 0.0, -BIG)
    c[:, 896:928] = (10000.0 ** (-np.arange(0, 64, 2, dtype=np.float32) / 64.0)).astype(np.float32)[None, :]
    return c


def prep_inputs(inputs):
    f32 = lambda a: np.ascontiguousarray(np.asarray(a, dtype=np.float32))
    conv_w = f32(inputs["conv_w"])
    convw = np.ascontiguousarray(conv_w.reshape(DEPTH, 4, 12, 128).transpose(0, 3, 2, 1))
    pvrow = np.concatenate([f32(inputs["a_log"]), f32(inputs["dt_bias"]), f32(inputs["sinks"]),
                            f32(inputs["dn_norm_w"])], axis=1)
    pv = np.ascontiguousarray(np.broadcast_to(pvrow[:, None, :], (DEPTH, 128, 144)))
    lnrow = np.concatenate([f32(inputs["ln_g"]), f32(inputs["ln_b"])], axis=2)
    lnp = np.ascontiguousarray(np.broadcast_to(lnrow[:, :, None, :], (DEPTH, 2, 128, 2048)))
    rw = np.ascontiguousarray(f32(inputs["router_w"]).reshape(2, KC, 128, NE).transpose(0, 2, 1, 3))
    shared = {
        "w_in": f32(inputs["w_in"]), "w_out": f32(inputs["w_out"]),
        "ffn_w_gate": f32(inputs["ffn_w_gate"]), "ffn_w_up": f32(inputs["ffn_w_up"]),
        "ffn_w_down": f32(inputs["ffn_w_down"]),
        "moe_w_gate": f32(inputs["moe_w_gate"]), "moe_w_up": f32(inputs["moe_w_up"]),
        "moe_w_down": f32(inputs["moe_w_down"]),
        "convw": convw, "pv": pv, "lnp": lnp, "rw": rw, "cst": host_consts(),
    }
    return shared


def kernel(**inputs):
    shared = prep_inputs(inputs)
    x = np.asarray(inputs["x"], dtype=np.float32)
    pos = np.asarray(inputs["positions"], dtype=np.int32)
    nb = x.shape[0]
    nc = build()
    in_maps = []
    for b in range(nb):
        m = dict(shared)
        m["x"] = np.ascontiguousarray(x[b])
        m["pos"] = np.ascontiguousarray(pos[b].reshape(NT, 128).T)
        in_maps.append(m)
    res = run_bass_kernel_spmd(nc, in_maps, core_ids=list(range(nb)))
    return np.stack([np.asarray(r["y"], dtype=np.float32) for r in res.results], axis=0)
```

```python
import math
from contextlib import ExitStack
import numpy as np
import concourse.bass as bass
import concourse.mybir as mybir
from concourse.bass_utils import run_bass_kernel_spmd

F32 = mybir.dt.float32
BF16 = mybir.dt.bfloat16
I32 = mybir.dt.int32
ALU = mybir.AluOpType
AF = mybir.ActivationFunctionType
AX = mybir.AxisListType

CE = ['pe', 'act', 'dve', 'pool', 'sp']
EIDX = {e: i for i, e in enumerate(CE)}
EPOCH = 12000

DEPTH = 4
S = 2048
NT = 16
D = 1024
KC = 8
INC = 2824
DFF = 3584
NE = 8
ALPHA = (2.0 * DEPTH) ** 0.25
TMW = 1288
BIG = 30000.0


class Op:
    __slots__ = ('eng', 'fn', 'is_dma', 'dma_key', 'eidx', 'signal', 'sem', 'val', 'waits', 'vc', 'dvc')


class Sched:
    def __init__(self, nc):
        self.nc = nc
        self.eops = {e: [] for e in CE}
        self.last_writer = {}
        self.readers = {}
        self.desc = {}
        self.known = set()
        self.aliases = {}
        self.dma_cnt = {}
        self.last = {e: None for e in CE}
        self.nops = 0

    def alias(self, a, b):
        self.aliases.setdefault(a, set()).add(b)
        self.aliases.setdefault(b, set()).add(a)

    def _reg(self, k):
        if k in self.known:
            return
        self.known.add(k)
        for i in range(1, len(k)):
            self.desc.setdefault(k[:i], set()).add(k)
        self.desc.setdefault(k, set())

    def _related(self, k):
        out = [k]
        for i in range(1, len(k)):
            out.append(k[:i])
        out.extend(self.desc.get(k, ()))
        for r in self.aliases.get(k[0], ()):
            out.append((r,))
            out.extend(self.desc.get((r,), ()))
        return out

    def add(self, eng, fn, reads=(), writes=(), dma_key=None):
        reads = [k if isinstance(k, tuple) else (k,) for k in reads]
        writes = [k if isinstance(k, tuple) else (k,) for k in writes]
        writes = writes + [k for k in reads if k[0] == 'pb' and k not in writes]
        for k in reads:
            self._reg(k)
        for k in writes:
            self._reg(k)
        op = Op()
        op.eng = eng
        op.fn = fn
        op.is_dma = dma_key is not None
        op.dma_key = dma_key
        op.eidx = len(self.eops[eng])
        op.signal = False
        op.sem = None
        op.val = None
        deps = {}
        for b in reads:
            for k in self._related(b):
                w = self.last_writer.get(k)
                if w is not None:
                    deps[id(w)] = (w, 'raw')
        for b in writes:
            for k in self._related(b):
                w = self.last_writer.get(k)
                if w is not None and id(w) not in deps:
                    deps[id(w)] = (w, 'waw')
                for r in self.readers.get(k, {}).values():
                    if id(r) not in deps:
                        deps[id(r)] = (r, 'war')
        prev = self.last[eng]
        vc = list(prev.vc) if prev is not None else [-1] * len(CE)
        dvc = dict(prev.dvc) if prev is not None else {}
        best = {}
        for d, kind in deps.values():
            if d is op:
                continue
            if d.is_dma:
                if dvc.get(d.dma_key, 0) >= d.val:
                    continue
                key = ('d', d.dma_key)
                cur = best.get(key)
                if cur is None or d.val > cur.val:
                    best[key] = d
            else:
                if d.eng == eng and not op.is_dma and kind != 'raw':
                    continue
                if vc[EIDX[d.eng]] >= d.eidx:
                    continue
                key = ('c', d.eng)
                cur = best.get(key)
                if cur is None or d.eidx > cur.eidx:
                    best[key] = d
        op.waits = list(best.values())
        for d in op.waits:
            if d.is_dma:
                dvc[d.dma_key] = max(dvc.get(d.dma_key, 0), d.val)
            else:
                d.signal = True
                vc[EIDX[d.eng]] = max(vc[EIDX[d.eng]], d.eidx)
            for i, v in enumerate(d.vc):
                if v > vc[i]:
                    vc[i] = v
            for k, v in d.dvc.items():
                if v > dvc.get(k, 0):
                    dvc[k] = v
        if op.is_dma:
            c = self.dma_cnt.get(dma_key, 0) + 1
            self.dma_cnt[dma_key] = c
            op.val = 16 * c
        op.vc = vc
        op.dvc = dvc
        slot = ('d', dma_key) if op.is_dma else eng
        for b in reads:
            self.readers.setdefault(b, {})[slot] = op
        for b in writes:
            self.last_writer[b] = op
            self.readers[b] = {}
        self.eops[eng].append(op)
        self.last[eng] = op
        self.nops += 1
        return op

    def emit(self, final_wait_ops=()):
        nc = self.nc
        with ExitStack() as st:
            for e in CE:
                cnt = 0
                sem = None
                ep = 0
                for op in self.eops[e]:
                    if op.is_dma or not op.signal:
                        continue
                    if sem is None or cnt >= EPOCH:
                        sem = st.enter_context(nc.semaphore("s_%s_%d" % (e, ep)))
                        ep += 1
                        cnt = 0
                    cnt += 1
                    op.sem = sem
                    op.val = cnt
            dsem = {}
            for i, k in enumerate(self.dma_cnt):
                dsem[k] = st.enter_context(nc.semaphore("d_%d" % i))
            for e in CE:
                for op in self.eops[e]:
                    if op.is_dma:
                        op.sem = dsem[op.dma_key]
            block = st.enter_context(nc.Block())

            def run(engname):
                def body(eng):
                    for op in self.eops[engname]:
                        for d in op.waits:
                            eng.wait_ge(d.sem, d.val)
                        ins = op.fn(eng)
                        if op.is_dma:
                            ins.then_inc(op.sem, 16)
                        elif op.signal:
                            ins.then_inc(op.sem, 1)
                    if engname == 'sp':
                        for d in final_wait_ops:
                            eng.wait_ge(d.sem, d.val)
                return body

            block.tensor(run('pe'))
            block.scalar(run('act'))
            block.vector(run('dve'))
            block.gpsimd(run('pool'))
            block.sync(run('sp'))


def build(nlayers=DEPTH, dbg=False):
    nc = bass.Bass("TRN2", target_bir_lowering=False)
    dram = lambda name, shape, dt, kind="ExternalInput": nc.dram_tensor(name, shape, dt, kind=kind).ap()
    x_d = dram("x", [S, D], F32)
    pos_d = dram("pos", [128, NT], I32)
    w_in_d = dram("w_in", [DEPTH, D, INC], F32)
    w_out_d = dram("w_out", [DEPTH, D, D], F32)
    fg_d = dram("ffn_w_gate", [2, D, DFF], F32)
    fu_d = dram("ffn_w_up", [2, D, DFF], F32)
    fd_d = dram("ffn_w_down", [2, DFF, D], F32)
    mg_d = dram("moe_w_gate", [2, NE, D, DFF], F32)
    mu_d = dram("moe_w_up", [2, NE, D, DFF], F32)
    md_d = dram("moe_w_down", [2, NE, DFF, D], F32)
    convw_d = dram("convw", [DEPTH, 128, 12, 4], F32)
    pv_d = dram("pv", [DEPTH, 128, 144], F32)
    lnp_d = dram("lnp", [DEPTH, 2, 128, 2048], F32)
    rw_d = dram("rw", [2, 128, KC, NE], F32)
    cst_d = dram("cst", [128, 928], F32)
    y_d = dram("y", [S, D], F32, kind="ExternalOutput")
    fm_s = dram("fm_s", [1536, S], BF16, kind="Internal")
    tm_s = dram("tm_s", [S, TMW], F32, kind="Internal")
    if dbg:
        dbg_o = dram("dbg_o", [S, D], F32, kind="ExternalOutput")
        dbg_x1 = dram("dbg_x1", [S, D], F32, kind="ExternalOutput")
        dbg_raw = dram("dbg_raw", [S, 512], F32, kind="ExternalOutput")

    st = ExitStack()
    with st:
        sb = lambda name, shape, dt: st.enter_context(nc.sbuf_tensor("s_" + name, shape, dt))
        xres = sb("xres", [128, NT, D], F32)
        xT = sb("xT", [128, KC, S], BF16)
        R = sb("R", [128, 32768], BF16)
        wa = [R[:, s * 4096:(s + 1) * 4096].rearrange("p (k n) -> p k n", k=KC) for s in range(2)]
        wout = R[:, 8192:16384].rearrange("p (k n) -> p k n", k=KC)
        wg = [R[:, s * 4096:(s + 1) * 4096].rearrange("p (k n) -> p k n", k=KC) for s in range(2)]
        wu = [R[:, 8192 + s * 4096:8192 + (s + 1) * 4096].rearrange("p (k n) -> p k n", k=KC) for s in range(2)]
        wd = [R[:, 16384 + s * 4096:16384 + (s + 1) * 4096].rearrange("p (f n) -> p f n", f=4) for s in range(2)]
        hT = R[:, 24576:32768].rearrange("p (f n) -> p f n", f=4)
        rwb = R[:, 16384:32768].bitcast(F32).rearrange("p (e d) -> p e d", e=NE)
        rv = lambda a, n: R[:, a:a + n]
        rf = lambda a, n: R[:, a:a + n].bitcast(F32)
        kTa = rv(0, 2048)
        va = rv(2048, 2048).rearrange("p (t n) -> p t n", t=NT)
        Sf = rf(4096, 1024).rearrange("p (h n) -> p h n", h=4)
        Sb = rv(5120, 512).rearrange("p (h n) -> p h n", h=4)
        o4 = rf(5632, 1024).rearrange("p (h n) -> p h n", h=4)
        zs = rv(6656, 512)
        qr = rv(7168, 512)
        qTs = rv(7680, 512).rearrange("p (a n) -> p a n", a=4)
        stg = [rf(16384 + i * 1024, 1024) for i in range(4)]
        stA = [rf(20480 + i * 1030, 1030) for i in range(4)]
        cvbs = [rf(24600 + i * 1024, 1024) for i in range(2)]
        rnbs = [rf(26648 + i * 1024, 1024) for i in range(2)]
        sqb = [rv(28696 + i * 512, 512) for i in range(2)]
        outb = [rv(29720 + i * 512, 512) for i in range(2)]
        yb = rv(25672, 1536).rearrange("p (c w) -> p c w", c=12)
        tmv = rf(29256, 2576)
        cst = sb("cst", [128, 928], F32)
        identf = cst[:, 0:128]
        tri = cst[:, 128:256]
        onesf = cst[:, 256:384]
        maskA = cst[:, 384:512]
        maskB = cst[:, 512:640]
        swam = cst[:, 640:896]
        invf = cst[:, 896:928]
        identb = sb("identb", [128, 128], BF16)
        onesb = sb("onesb", [128, 128], BF16)
        cvals = sb("cvals", [128, 4], F32)
        posi = sb("posi", [128, NT], I32)
        posf = sb("posf", [128, NT], F32)
        cosT = sb("cosT", [128, NT, 32], F32)
        sinT = sb("sinT", [128, NT, 32], F32)
        lnp = sb("lnp", [128, 2048], F32)
        convw = sb("convw", [128, 12, 4], F32)
        pv = sb("pv", [128, 144], F32)
        ab_all = sb("ab_all", [128, NT, 8], F32)
        pp = {n: sb("pp_" + n, [128, 64], F32) for n in
              ['ta', 'sp', 'g', 'eb', 'beta', 'nbeta', 'gcum', 'ngc', 'egc', 'bege']}
        negA = sb("negA", [128, 4], F32)
        ytmp = sb("ytmp", [128, D], F32)
        ptmp = None
        o4b = ytmp[:, 0:512].rearrange("p (h n) -> p h n", h=4)
        sgt = [ytmp[:, 0:512], ytmp[:, 512:1024]]
        kr = sb("kr", [128, 128], BF16)
        pb_ = sb("pb_", [128, 1024], BF16)
        mask8b = sb("mask8b", [128, 256], BF16)
        pTb = sb("pTb", [128, 1024], BF16)
        st8 = sb("st8", [128, 24], F32)
        o_tok = sb("o_tok", [128, D], BF16)
        oT = sb("oT", [128, KC, 128], BF16)
        lnst = sb("lnst", [128, 16], F32)
        DN = []
        R_SETS = {1: 22600, 2: 16384, 3: 19400}
        for i_ in range(4):
            n_ = str(i_)
            B_ = {'n': n_, 'bps': 3 + i_, 'brd': 3 + i_}
            if i_ == 0:
                f32t = lambda nm, w_: sb(nm + n_, [128, w_], F32)
                b16t = lambda nm, w_: sb(nm + n_, [128, w_], BF16)
            else:
                off = [R_SETS[i_]]
                def f32t(nm, w_, off=off):
                    a_ = rf(off[0], 2 * w_)
                    off[0] += 2 * w_
                    return a_
                def b16t(nm, w_, off=off):
                    a_ = rv(off[0], w_)
                    off[0] += w_
                    return a_
            B_['trg'] = f32t('trg', 128)
            B_['dL'] = f32t('dL', 128)
            B_['dAT'] = f32t('dAT', 128)
            B_['dsc'] = f32t('dsc', 4)
            ab_ = f32t('abpa', 384)
            B_['abp'] = [ab_, ab_]
            B_['u_sb'] = f32t('u_sb', 128)
            B_['t1'] = f32t('t1', 128)
            B_['ATm'] = b16t('ATm', 128)
            B_['TT'] = b16t('TT', 128)
            B_['Xv'] = b16t('Xv', 128)
            B_['Xw'] = b16t('Xw', 128)
            B_['ke'] = b16t('ke', 128)
            B_['wT_sb'] = b16t('wT_sb', 128)
            B_['vn'] = b16t('vn', 128)
            if i_ > 0:
                lim = {1: 25672, 2: 19400, 3: 22600}[i_]
                assert off[0] <= lim, (i_, off[0])
            DN.append(B_)
        rs4 = sb("rs4", [128, 8], F32)
        lg = sb("lg", [128, NT, NE], F32)
        rw32 = sb("rw32", [128, KC, NE], F32)
        comb = sb("comb", [128, NT, NE], F32)
        mt = {n: sb("mt_" + n, [128, NT, NE], F32) for n in ['eq1', 'l2', 'eq2']}
        ms = {n: sb("ms_" + n, [128, NT], F32) for n in ['m1', 'm2', 'd', 'ed', 'g1', 'g2']}
        PB = [st.enter_context(nc.psum_tensor("pb%d" % i, [128, 512], F32)) for i in range(8)]
        PBb = [p[:].bitcast(BF16) for p in PB]

        print("sbuf bytes remaining:", nc.sbuf_bytes_remaining)
        Sc = Sched(nc)
        def alias_groups(groups):
            for i in range(len(groups)):
                for j in range(i + 1, len(groups)):
                    for a_ in groups[i]:
                        for b_ in groups[j]:
                            Sc.alias(a_, b_)
        alias_groups([['wa0', 'wa1'], ['kTa', 'va', 'Sf', 'Sb', 'o4', 'zs', 'qr', 'qTs'], ['wg0', 'wg1']])
        alias_groups([['wout'], ['wu0', 'wu1']])
        set1 = [nm + str(i_) for i_ in (1, 2, 3) for nm in ['trg', 'dL', 'dAT', 'dsc', 'abp', 'u_sb', 't1', 'ATm', 'TT', 'Xv', 'Xw', 'ke', 'wT_sb', 'vn']]
        alias_groups([['stg', 'stA', 'cvb', 'rnb', 'sqb', 'outb'], ['yb', 'tmv'] + set1, ['wd0', 'wd1', 'hT']])

        def A(eng, meth, *args, r=(), w=(), **kw):
            return Sc.add(eng, lambda e: getattr(e, meth)(*args, **kw), reads=r, writes=w)

        def DMA(eng, out, in_, r=(), w=(), key=None):
            return Sc.add(eng, lambda e: e.dma_start(out=out, in_=in_), reads=r, writes=w, dma_key=key)

        def MM(out, lhsT, rhs, start, stop, r, w):
            return Sc.add('pe', lambda e: e.matmul(out, lhsT, rhs, start=start, stop=stop), reads=r, writes=w)

        def TR(out, in_, ident, r, w):
            return Sc.add('pe', lambda e: e.transpose(out, in_, ident), reads=r, writes=w)

        def ACT(out, in_, func, r, w, **kw):
            return Sc.add('act', lambda e: e.activation(out, in_, func, **kw), reads=r, writes=w)

        rr = [0]

        def evac_eng():
            rr[0] += 1
            return 'act' if rr[0] % 2 == 0 else 'dve'

        def COPY(eng, out, in_, r, w):
            if eng == 'act':
                return ACT(out, in_, AF.Copy, r, w)
            return A(eng, 'tensor_copy', out, in_, r=r, w=w)

        DMA('sp', cst[:], cst_d, w=['cst'], key='cst')
        DMA('sp', posi[:], pos_d, w=['posi'], key='posi')
        DMA('sp', xres[:], x_d.rearrange("(n p) d -> p n d", p=128), w=['xres'], key='xin')
        A('dve', 'tensor_copy', identb[:], identf, r=['cst'], w=['identb'])
        A('dve', 'tensor_copy', onesb[:], onesf, r=['cst'], w=['onesb'])
        A('dve', 'tensor_scalar', mask8b[:], swam, 8.0, None, ALU.mult, r=['cst'], w=['mask8b'])
        A('dve', 'memset', cvals[:, 0:1], -math.pi, w=['cvals'])
        A('dve', 'memset', cvals[:, 1:2], 1.0, w=['cvals'])
        A('dve', 'memset', cvals[:, 2:3], 1e-5, w=['cvals'])
        A('dve', 'memset', cvals[:, 3:4], 1e-6, w=['cvals'])
        A('dve', 'tensor_copy', posf[:], posi[:], r=['posi'], w=['posf'])
        A('dve', 'tensor_tensor', cosT[:], posf[:].unsqueeze(2).to_broadcast([128, NT, 32]),
          invf.unsqueeze(1).to_broadcast([128, NT, 32]), ALU.mult, r=['posf', 'cst'], w=['cosT'])
        TWO_PI = 2 * math.pi
        angi = ytmp[:, 512:1024].bitcast(I32).rearrange("p (t n) -> p t n", t=NT)
        angk = ytmp[:, 0:512].rearrange("p (t n) -> p t n", t=NT)
        A('dve', 'tensor_copy', sinT[:], cosT[:], r=['cosT'], w=['sinT'])
        A('dve', 'tensor_scalar', cosT[:], cosT[:], 0.5 * math.pi, None, ALU.add, r=['cosT'], w=['cosT'])
        for tab, key in ((sinT, 'sinT'), (cosT, 'cosT')):
            A('dve', 'tensor_scalar', angk[:], tab[:], 1.0 / TWO_PI, None, ALU.mult, r=[key], w=[('ytmp', 'h0')])
            A('dve', 'tensor_copy', angi[:], angk[:], r=[('ytmp', 'h0')], w=[('ytmp', 'h1')])
            A('dve', 'tensor_copy', angk[:], angi[:], r=[('ytmp', 'h1')], w=[('ytmp', 'h0')])
            A('dve', 'scalar_tensor_tensor', tab[:], angk[:], -TWO_PI, tab[:], ALU.mult, ALU.add, r=[('ytmp', 'h0'), key], w=[key])
            A('dve', 'tensor_single_scalar', angk[:], tab[:], math.pi, ALU.is_gt, r=[key], w=[('ytmp', 'h0')])
            A('dve', 'scalar_tensor_tensor', tab[:], angk[:], -TWO_PI, tab[:], ALU.mult, ALU.add, r=[('ytmp', 'h0'), key], w=[key])
            A('dve', 'tensor_single_scalar', angk[:], tab[:], -math.pi, ALU.is_lt, r=[key], w=[('ytmp', 'h0')])
            A('dve', 'scalar_tensor_tensor', tab[:], angk[:], TWO_PI, tab[:], ALU.mult, ALU.add, r=[('ytmp', 'h0'), key], w=[key])
            A('dve', 'tensor_scalar', tab[:], tab[:], math.pi, -math.pi, ALU.min, ALU.max, r=[key], w=[key])
            ACT(tab[:], tab[:], AF.Sin, [key], [key])

        def emit_xT(t):
            COPY('act', o_tok[:], xres[:, t, :], [('xres', t)], ['o_tok'])
            pbank = 7
            for kc in range(KC):
                TR(PBb[pbank][:, kc * 128:(kc + 1) * 128], o_tok[:, kc * 128:(kc + 1) * 128], identb[:],
                   ['o_tok', 'identb'], [('pb', pbank)])
            COPY('dve', xT[:, :, t * 128:(t + 1) * 128],
                 PBb[pbank][:, 0:1024].rearrange("p (k n) -> p k n", k=KC), [('pb', pbank)], [('xT', t)])

        def layer_norm(src_ap, src_key, t, lidx):
            for hh in range(2):
                A('dve', 'bn_stats', lnst[:, hh * 6:(hh + 1) * 6], src_ap[:, hh * 512:(hh + 1) * 512],
                  r=[src_key], w=['lnst'])
            A('dve', 'bn_aggr', lnst[:, 12:14], lnst[:, 0:12], r=['lnst'], w=['lnst2'])
            ACT(lnst[:, 15:16], lnst[:, 13:14], AF.Sqrt, ['lnst2', 'cvals'], ['lnst4'], bias=cvals[:, 2:3])
            A('dve', 'reciprocal', lnst[:, 14:15], lnst[:, 15:16], r=['lnst4'], w=['lnst3'])
            A('dve', 'tensor_scalar', ytmp[:], src_ap, lnst[:, 12:13], lnst[:, 14:15], ALU.subtract, ALU.mult,
              r=[src_key, 'lnst2', 'lnst3'], w=['ytmp'])
            A('pool', 'tensor_tensor', ytmp[:], ytmp[:], lnp[:, 0:1024], ALU.mult, r=['ytmp', 'lnp'], w=['ytmp'])
            A('pool', 'tensor_tensor', xres[:, t, :], ytmp[:], lnp[:, 1024:2048], ALU.add,
              r=['ytmp', 'lnp'], w=[('xres', t)])

        for t in range(NT):
            emit_xT(t)

        wq = [0]

        for l in range(nlayers):
            DMA('sp', convw[:], convw_d[l], w=['convw'], key='convw')
            DMA('sp', pv[:], pv_d[l], w=['pv'], key='pv')
            DMA('sp', lnp[:], lnp_d[l, 0], w=['lnp'], key='lnp')
            DMA('pool', wout, w_out_d[l].rearrange("(k p) n -> p k n", p=128), w=['wout'], key='wout')
            alog = pv[:, 0:4]
            dtb = pv[:, 4:8]
            sinks = pv[:, 8:16]
            dnw = pv[:, 16:144]
            w_in_v = w_in_d[l].rearrange("(k p) n -> p k n", p=128)
            blocks = [(0, 512, 'fm'), (512, 1024, 'fm'), (1024, 1536, 'fm'),
                      (1536, 2048, 'tm'), (2048, 2560, 'tm'), (2560, 2824, 'tm')]
            cnt = 0
            for bi, (c0, c1, kind) in enumerate(blocks):
                s_ = wq[0] % 2
                wq[0] += 1
                wkey = 'wa%d' % s_
                wcols = c1 - c0
                DMA('pool', wa[s_][:, :, 0:wcols], w_in_v[:, :, c0:c1], w=[wkey], key=wkey)
                if kind == 'fm':
                    for j in range(4):
                        ch = (c0 // 128) + j
                        isv = (c0 == 1024)
                        for tg in range(4):
                            pbk = tg
                            cvb = cvbs[tg % 2]
                            kcv = ('cvb', tg % 2)
                            for kc in range(KC):
                                MM(PB[pbk][:], wa[s_][:, kc, j * 128:(j + 1) * 128], xT[:, kc, tg * 512:(tg + 1) * 512],
                                   kc == 0, kc == KC - 1, [wkey] + [('xT', tg * 4 + i) for i in range(4)], [('pb', pbk)])
                            COPY('act', stA[tg][:, 3:515], PB[pbk][:], [('pb', pbk)], [('stA', tg)])
                            if tg == 0:
                                A('pool', 'memset', stA[tg][:, 0:3], 0.0, w=[('stA', tg, 'h')])
                            else:
                                A('pool', 'tensor_copy', stA[tg][:, 0:3], stA[tg - 1][:, 512:515], r=[('stA', tg - 1)],
                                  w=[('stA', tg, 'h')])
                            A('dve', 'tensor_scalar', cvb, stA[tg][:, 0:512], convw[:, ch, 0:1], None, ALU.mult,
                              r=[('stA', tg), 'convw'], w=[kcv])
                            for k in range(1, 4):
                                A('dve', 'scalar_tensor_tensor', cvb, stA[tg][:, k:k + 512], convw[:, ch, k:k + 1], cvb,
                                  ALU.mult, ALU.add, r=[('stA', tg), 'convw', kcv], w=[kcv])
                            if isv:
                                ob = cnt % 2
                                cnt += 1
                                ACT(outb[ob], cvb, AF.Silu, [kcv], [('outb', ob)])
                                DMA('sp', fm_s[ch * 128:(ch + 1) * 128, tg * 512:(tg + 1) * 512], outb[ob],
                                    r=[('outb', ob)], w=[('fm_s', ch, tg)], key='outb%d' % ob)
                            else:
                                ACT(stA[tg][:, 0:512], cvb, AF.Silu, [kcv], [('stA', tg, 'y')])
                                sb_ = tg % 2
                                ACT(sqb[sb_], stA[tg][:, 0:512], AF.Square, [('stA', tg, 'y')], [('sqb', sb_)])
                                MM(PB[4 + tg][:], onesb[:], sqb[sb_], True, True, ['onesb', ('sqb', sb_)], [('pb', 4 + tg)])
                        if not isv:
                            for tg in range(4):
                                ob = cnt % 2
                                cnt += 1
                                rnb = rnbs[tg % 2]
                                krn = ('rnb', tg % 2)
                                ACT(rnb, PB[4 + tg][:], AF.Sqrt, [('pb', 4 + tg), 'cvals'], [krn], bias=cvals[:, 3:4])
                                A('dve', 'reciprocal', rnb, rnb, r=[krn], w=[krn])
                                A('dve', 'tensor_tensor', outb[ob], stA[tg][:, 0:512], rnb, ALU.mult,
                                  r=[('stA', tg, 'y'), krn], w=[('outb', ob)])
                                DMA('sp', fm_s[ch * 128:(ch + 1) * 128, tg * 512:(tg + 1) * 512], outb[ob],
                                    r=[('outb', ob)], w=[('fm_s', ch, tg)], key='outb%d' % ob)
                else:
                    for t in range(NT):
                        pbk = cnt % 4
                        q_ = cnt % 4
                        cnt += 1
                        for kc in range(KC):
                            MM(PB[pbk][:, 0:wcols], xT[:, kc, t * 128:(t + 1) * 128], wa[s_][:, kc, 0:wcols],
                               kc == 0, kc == KC - 1, [wkey, ('xT', t)], [('pb', pbk)])
                        COPY(evac_eng(), stg[q_][:, 0:wcols], PB[pbk][:, 0:wcols], [('pb', pbk)], [('stg', q_)])
                        if c0 == 2048:
                            COPY('act', ab_all[:, t, :], PB[pbk][:, 0:8], [('pb', pbk)], [('ab_all', t)])
                        DMA('sp', tm_s[t * 128:(t + 1) * 128, c0 - 1536:c1 - 1536], stg[q_][:, 0:wcols],
                            r=[('stg', q_)], w=[('tm_s', t, bi)], key='stg%d' % q_)
            v3 = lambda n: pp[n][:].rearrange("p (t h) -> p t h", h=4)
            A('dve', 'tensor_tensor', v3('ta'), ab_all[:, :, 0:4], dtb.unsqueeze(1).to_broadcast([128, NT, 4]), ALU.add,
              r=['ab_all', 'pv'], w=['pp_ta'])
            ACT(pp['ta'][:], pp['ta'][:], AF.Exp, ['pp_ta'], ['pp_ta'])
            ACT(pp['sp'][:], pp['ta'][:], AF.Ln, ['pp_ta', 'cvals'], ['pp_sp'], bias=cvals[:, 1:2])
            ACT(negA[:], alog, AF.Exp, ['pv'], ['negA'])
            A('dve', 'tensor_scalar', negA[:], negA[:], -1.0, None, ALU.mult, r=['negA'], w=['negA'])
            A('dve', 'tensor_tensor', v3('g'), v3('sp'), negA[:].unsqueeze(1).to_broadcast([128, NT, 4]), ALU.mult,
              r=['pp_sp', 'negA'], w=['pp_g'])
            ACT(v3('eb'), ab_all[:, :, 4:8], AF.Exp, ['ab_all'], ['pp_eb'], scale=-1.0)
            A('dve', 'tensor_scalar', pp['eb'][:], pp['eb'][:], 1.0, None, ALU.add, r=['pp_eb'], w=['pp_eb'])
            A('dve', 'reciprocal', pp['beta'][:], pp['eb'][:], r=['pp_eb'], w=['pp_beta'])
            A('dve', 'tensor_scalar', pp['nbeta'][:], pp['beta'][:], -1.0, None, ALU.mult, r=['pp_beta'], w=['pp_nbeta'])
            MM(PB[4][:, 0:64], tri, pp['g'][:], True, True, ['cst', 'pp_g'], [('pb', 4)])
            COPY('dve', pp['gcum'][:], PB[4][:, 0:64], [('pb', 4)], ['pp_gcum'])
            A('dve', 'tensor_scalar', pp['ngc'][:], pp['gcum'][:], -1.0, None, ALU.mult, r=['pp_gcum'], w=['pp_ngc'])
            ACT(pp['egc'][:], pp['gcum'][:], AF.Exp, ['pp_gcum'], ['pp_egc'])
            A('dve', 'tensor_tensor', pp['bege'][:], pp['beta'][:], pp['egc'][:], ALU.mult,
              r=['pp_beta', 'pp_egc'], w=['pp_bege'])
            A('pool', 'memset', Sf[:], 0.0, w=['Sf'])
            A('pool', 'memset', Sb[:], 0.0, w=['Sb'])

            def run_rr(gens):
                gens = list(gens)
                while gens:
                    for g_ in list(gens):
                        try:
                            next(g_)
                        except StopIteration:
                            gens.remove(g_)

            def prologue(t):
                DMA('sp', yb[:], fm_s.rearrange("(c p) w -> p c w", p=128)[:, :, t * 128:(t + 1) * 128],
                    r=['fm_s'], w=['yb'], key='yb')
                DMA('sp', tmv[:], tm_s[t * 128:(t + 1) * 128, :], r=[('tm_s', t)], w=['tmv'], key='tmv')
                yield
                ACT(zs[:], tmv[:, 0:512], AF.Silu, ['tmv'], ['zs'])
                for h in range(4):
                    TR(PBb[2][:, h * 128:(h + 1) * 128], yb[:, 4 + h, :], identb[:], ['yb', 'identb'], [('pb', 2)])
                for h in range(4):
                    TR(PBb[2][:, (4 + h) * 128:(5 + h) * 128], yb[:, 8 + h, :], identb[:], ['yb', 'identb'], [('pb', 2)])
                yield

            def dn_head(t, h, B_):
                ix = t * 4 + h
                col = lambda n: pp[n][:, ix:ix + 1]
                bps, brd = B_['bps'], B_['brd']
                kps, krd = ('pb', bps), ('pb', brd)
                n_ = B_['n']
                K_ = lambda nm: nm + n_
                A('pool', 'tensor_scalar', B_['trg'][:], tri, col('g'), -1.0, ALU.mult, ALU.mult, r=['cst', 'pp_g'], w=[K_('trg')])
                MM(PB[bps][:, 0:128], onesf, B_['trg'][:], True, False, ['cst', K_('trg')], [kps])
                MM(PB[bps][:, 0:128], identf, maskA, False, True, ['cst'], [kps])
                MM(PB[bps][:, 128:256], onesf, B_['trg'][:], True, False, ['cst', K_('trg')], [kps])
                MM(PB[bps][:, 128:256], identf, maskB, False, True, ['cst'], [kps])
                kTh = yb[:, 4 + h, :]
                qTh = yb[:, h, :]
                MM(PB[bps][:, 256:384], kTh, kTh, True, True, ['yb'], [kps])
                MM(PB[bps][:, 384:512], kTh, qTh, True, True, ['yb'], [kps])
                yield
                dsc_ = B_['dsc']
                ACT(B_['dL'][:], PB[bps][:, 0:128], AF.Exp, [kps, 'pp_gcum'], [K_('dL')], bias=col('gcum'))
                ACT(B_['dAT'][:], PB[bps][:, 128:256], AF.Exp, [kps, 'pp_ngc'], [K_('dAT')], bias=col('ngc'), scale=-1.0)
                ACT(dsc_[:, 0:1], PB[bps][:, 255:256], AF.Exp, [kps, 'pp_ngc'], [K_('dsc')], bias=col('ngc'), scale=-1.0)
                ACT(dsc_[:, 1:2], PB[bps][:, 255:256], AF.Exp, [kps], [K_('dsc')], scale=-1.0)
                abp_ = B_['abp']
                A('dve', 'scalar_tensor_tensor', abp_[0][:, 0:128], PB[bps][:, 256:384], col('nbeta'), B_['dL'][:], ALU.mult, ALU.mult,
                  r=[kps, 'pp_nbeta', K_('dL')], w=[(K_('abp'), 0)])
                A('dve', 'tensor_tensor', B_['ATm'][:], PB[bps][:, 384:512], B_['dAT'][:], ALU.mult, r=[kps, K_('dAT')], w=[K_('ATm')])
                yield
                TR(PB[brd][:, 0:128], abp_[0][:, 0:128], identf, [(K_('abp'), 0), 'cst'], [krd])
                COPY('act', abp_[0][:, 128:256], PB[brd][:, 0:128], [krd], [(K_('abp'), 0)])
                A('pool', 'tensor_copy', abp_[0][:, 256:384], identf, r=['cst'], w=[(K_('abp'), 0)])
                yield
                cur = 0
                for rnd in range(1, 8):
                    Ac = abp_[cur][:, 0:128]
                    Bc = abp_[cur][:, 128:256]
                    Pc = abp_[cur][:, 256:384]
                    nxt = 1 - cur
                    rk = [(K_('abp'), 0), 'cst']
                    if rnd <= 6:
                        MM(PB[brd][:, 0:128], Bc, Ac, True, True, rk, [krd])
                        if rnd <= 5:
                            MM(PB[brd][:, 128:256], Ac, Bc, True, True, rk, [krd])
                    MM(PB[brd][:, 256:384], identf, Pc, True, False, rk, [krd])
                    MM(PB[brd][:, 256:384], Ac, Pc, False, True, rk, [krd])
                    if rnd <= 6:
                        COPY(evac_eng(), abp_[nxt][:, 0:384], PB[brd][:, 0:384], [krd], [(K_('abp'), 0)])
                    else:
                        COPY(evac_eng(), B_['TT'][:], PB[brd][:, 256:384], [krd], [K_('TT')])
                    cur = nxt
                    yield
                ktok = PBb[2][:, h * 128:(h + 1) * 128]
                vtok = PBb[2][:, (4 + h) * 128:(5 + h) * 128]
                A('dve', 'tensor_scalar', B_['Xv'][:], vtok, col('beta'), None, ALU.mult, r=[('pb', 2), 'pp_beta'], w=[K_('Xv')])
                A('dve', 'tensor_scalar', B_['Xw'][:], ktok, col('bege'), None, ALU.mult, r=[('pb', 2), 'pp_bege'], w=[K_('Xw')])
                A('dve', 'tensor_scalar', B_['ke'][:], ktok, dsc_[:, 0:1], None, ALU.mult, r=[('pb', 2), K_('dsc')], w=[K_('ke')])
                MM(PB[brd][:, 0:128], B_['TT'][:], B_['Xv'][:], True, True, [K_('TT'), K_('Xv')], [krd])
                MM(PB[brd][:, 128:256], B_['Xw'][:], B_['TT'][:], True, True, [K_('TT'), K_('Xw')], [krd])
                yield
                COPY('act', B_['u_sb'][:], PB[brd][:, 0:128], [krd], [K_('u_sb')])
                COPY('act', B_['wT_sb'][:], PB[brd][:, 128:256], [krd], [K_('wT_sb')])
                MM(PB[brd][:, 256:384], B_['wT_sb'][:], Sb[:, h, :], True, True, [K_('wT_sb'), ('Sb', h)], [krd])
                MM(PB[brd][:, 384:512], qTh, Sb[:, h, :], True, True, ['yb', ('Sb', h)], [krd])
                yield
                A('dve', 'tensor_tensor', B_['vn'][:], B_['u_sb'][:], PB[brd][:, 256:384], ALU.subtract, r=[K_('u_sb'), krd], w=[K_('vn')])
                A('dve', 'tensor_scalar', B_['t1'][:], PB[brd][:, 384:512], col('egc'), None, ALU.mult, r=[krd, 'pp_egc'], w=[K_('t1')])
                MM(PB[bps][:, 0:128], B_['ATm'][:], B_['vn'][:], True, True, [K_('ATm'), K_('vn')], [kps])
                MM(PB[bps][:, 128:256], B_['ke'][:], B_['vn'][:], True, True, [K_('ke'), K_('vn')], [kps])
                yield
                A('dve', 'tensor_tensor', o4[:, h, :], B_['t1'][:], PB[bps][:, 0:128], ALU.add, r=[K_('t1'), kps], w=[('o4', h)])
                A('dve', 'scalar_tensor_tensor', Sf[:, h, :], Sf[:, h, :], dsc_[:, 1:2], PB[bps][:, 128:256], ALU.mult, ALU.add,
                  r=[('Sf', h), K_('dsc'), kps], w=[('Sf', h)])
                COPY('act', Sb[:, h, :], Sf[:, h, :], [('Sf', h)], [('Sb', h)])
                yield

            def dn_stream(t, heads, B_):
                for h in heads:
                    yield from dn_head(t, h, B_)

            def rms_gate(t):
                A('pool', 'tensor_tensor', o4b[:], o4[:], o4[:], ALU.mult, r=['o4'], w=['ytmp'])
                A('dve', 'tensor_reduce', rs4[:, 0:4], o4b[:], AX.X, ALU.add, r=['ytmp'], w=['rs4'])
                ACT(rs4[:, 0:4], rs4[:, 0:4], AF.Sqrt, ['rs4', 'cvals'], ['rs4'], bias=cvals[:, 3:4], scale=1.0 / 16384.0)
                A('dve', 'reciprocal', rs4[:, 0:4], rs4[:, 0:4], r=['rs4'], w=['rs4'])
                A('dve', 'tensor_scalar', rs4[:, 4:8], rs4[:, 0:4], 128.0 ** -0.5, None, ALU.mult, r=['rs4'], w=['rs4b'])
                A('dve', 'tensor_tensor', o4b[:], o4[:], rs4[:, 4:8].unsqueeze(2).to_broadcast([128, 4, 128]), ALU.mult,
                  r=['o4', 'rs4b'], w=['ytmp'])
                A('pool', 'tensor_tensor', o4b[:], o4b[:], dnw.unsqueeze(1).to_broadcast([128, 4, 128]), ALU.mult,
                  r=['ytmp', 'pv'], w=['ytmp'])
                A('dve', 'tensor_tensor', o_tok[:, 0:512].rearrange("p (h d) -> p h d", h=4), o4b[:],
                  zs[:].rearrange("p (h d) -> p h d", h=4), ALU.mult, r=['ytmp', 'zs'], w=[('o_tok', 0)])

            def swa(t):
                def rope(src, nh, dst4, keyw):
                    s4 = src.rearrange("p (h two d) -> p h two d", two=2, d=32)
                    x1 = s4[:, :, 0, :]
                    x2 = s4[:, :, 1, :]
                    cs = cosT[:, t, :].unsqueeze(1).to_broadcast([128, nh, 32])
                    sn = sinT[:, t, :].unsqueeze(1).to_broadcast([128, nh, 32])
                    w4 = lambda i: ytmp[:, i * 256:i * 256 + nh * 32].rearrange("p (h d) -> p h d", d=32)
                    A('pool', 'tensor_tensor', w4(0), x1, cs, ALU.mult, r=['tmv', 'cosT'], w=[('ytmp', 0)])
                    A('pool', 'tensor_tensor', w4(1), x2, sn, ALU.mult, r=['tmv', 'sinT'], w=[('ytmp', 1)])
                    A('pool', 'tensor_tensor', w4(2), x2, cs, ALU.mult, r=['tmv', 'cosT'], w=[('ytmp', 2)])
                    A('pool', 'tensor_tensor', w4(3), x1, sn, ALU.mult, r=['tmv', 'sinT'], w=[('ytmp', 3)])
                    A('pool', 'tensor_tensor', dst4[0], w4(0), w4(1), ALU.subtract, r=[('ytmp', 0), ('ytmp', 1)], w=[keyw])
                    A('pool', 'tensor_tensor', dst4[1], w4(2), w4(3), ALU.add, r=[('ytmp', 2), ('ytmp', 3)], w=[keyw])
                qv = qr[:].rearrange("p (pr two half d) -> p pr two half d", pr=4, two=2, half=2)
                for two in range(2):
                    srcq = tmv[:, 520 + two * 256:520 + (two + 1) * 256]
                    rope(srcq, 4, (qv[:, :, two, 0, :], qv[:, :, two, 1, :]), 'qr')
                    yield
                kv4 = kr[:].rearrange("p (h half d) -> p h half d", h=2, half=2)
                rope(tmv[:, 1032:1160], 2, (kv4[:, :, 0, :], kv4[:, :, 1, :]), 'kr')
                COPY('act', va[:, t, :], tmv[:, 1160:1288], ['tmv'], [('va', t)])
                yield
                for pr in range(4):
                    TR(PBb[7][:, pr * 128:(pr + 1) * 128], qr[:, pr * 128:(pr + 1) * 128], identb[:], ['qr', 'identb'], [('pb', 7)])
                TR(PBb[7][:, 512:640], kr[:], identb[:], ['kr', 'identb'], [('pb', 7)])
                COPY('act', qTs[:], PBb[7][:, 0:512].rearrange("p (a n) -> p a n", a=4), [('pb', 7)], ['qTs'])
                COPY('act', kTa[:, t * 128:(t + 1) * 128], PBb[7][:, 512:640], [('pb', 7)], [('kTa', t)])
                yield
                for g_ in range(2):
                    ps_ = slice(g_ * 64, (g_ + 1) * 64)
                    if t == 0:
                        k0, nk = 0, 128
                        mkb = mask8b[:, 128:256]
                        kkeys = [('kTa', 0)]
                    else:
                        k0, nk = (t - 1) * 128, 256
                        mkb = mask8b[:]
                        kkeys = [('kTa', t - 1), ('kTa', t)]
                    nb = nk // 128
                    for j in range(4):
                        bk = j // 2
                        c0_ = (j % 2) * 256
                        MM(PB[bk][:, c0_:c0_ + nk], identb[:], mkb, True, False, ['identb', 'mask8b'], [('pb', bk)])
                        MM(PB[bk][:, c0_:c0_ + nk], qTs[ps_, j, :], kTa[ps_, k0:k0 + nk], False, True, ['qTs'] + kkeys, [('pb', bk)])
                    yield
                    for bk in range(2):
                        A('dve', 'tensor_reduce', st8[:, bk * 2:bk * 2 + 2],
                          PB[bk][:].rearrange("p (a n) -> p a n", a=2)[:, :, 0:nk], AX.X, ALU.max, r=[('pb', bk)], w=[('st8', 'mx')])
                    A('dve', 'scalar_tensor_tensor', st8[:, 4:8], st8[:, 0:4], 0.125, sinks[:, g_ * 4:g_ * 4 + 4], ALU.mult, ALU.max,
                      r=[('st8', 'mx'), 'pv'], w=[('st8', 'm')])
                    A('dve', 'tensor_scalar', st8[:, 8:12], st8[:, 4:8], -1.0, None, ALU.mult, r=[('st8', 'm')], w=[('st8', 'nm')])
                    A('dve', 'tensor_tensor', st8[:, 12:16], st8[:, 8:12], sinks[:, g_ * 4:g_ * 4 + 4], ALU.add,
                      r=[('st8', 'nm'), 'pv'], w=[('st8', 'sk')])
                    yield
                    for j in range(4):
                        bk = j // 2
                        c0_ = (j % 2) * 256
                        ACT(pb_[:, j * 256:j * 256 + nk], PB[bk][:, c0_:c0_ + nk], AF.Exp, [('pb', bk), ('st8', 'nm')],
                            [('pb_', j), ('st8', 'rs', j)], bias=st8[:, 8 + j:9 + j], scale=0.125, accum_out=st8[:, 16 + j:17 + j])
                    ACT(st8[:, 12:16], st8[:, 12:16], AF.Exp, [('st8', 'sk')], [('st8', 'sk')])
                    yield
                    for j in range(4):
                        for b_ in range(nb):
                            TR(PBb[7][:, j * 256 + b_ * 128:j * 256 + (b_ + 1) * 128], pb_[:, j * 256 + b_ * 128:j * 256 + (b_ + 1) * 128],
                               identb[:], [('pb_', j), 'identb'], [('pb', 7)])
                    A('dve', 'tensor_tensor', st8[:, 20:24], st8[:, 16:20], st8[:, 12:16], ALU.add,
                      r=[('st8', 'rs'), ('st8', 'sk')], w=[('st8', 'den')])
                    A('dve', 'reciprocal', st8[:, 20:24], st8[:, 20:24], r=[('st8', 'den')], w=[('st8', 'den')])
                    yield
                    if nb == 2:
                        COPY('act', pTb[:, 0:512], PBb[7][:, 0:512], [('pb', 7)], [('pTb', 0)])
                        COPY('dve', pTb[:, 512:1024], PBb[7][:, 512:1024], [('pb', 7)], [('pTb', 1)])
                    else:
                        for j in range(4):
                            COPY('act' if j % 2 == 0 else 'dve', pTb[:, j * 256:j * 256 + 128], PBb[7][:, j * 256:j * 256 + 128],
                                 [('pb', 7)], [('pTb', j // 2)])
                    yield
                    for j in range(4):
                        for b_ in range(nb):
                            tt_ = t - (nb - 1) + b_
                            MM(PB[0][:, j * 64:(j + 1) * 64], pTb[:, j * 256 + b_ * 128:j * 256 + (b_ + 1) * 128],
                               va[:, tt_, g_ * 64:(g_ + 1) * 64], b_ == 0, b_ == nb - 1, [('pTb', j // 2), ('va', tt_)], [('pb', 0)])
                    yield
                    A('dve', 'tensor_tensor', o_tok[:, 512 + g_ * 256:512 + (g_ + 1) * 256].rearrange("p (j d) -> p j d", j=4),
                      PB[0][:, 0:256].rearrange("p (j d) -> p j d", j=4),
                      st8[:, 20:24].unsqueeze(2).to_broadcast([128, 4, 64]), ALU.mult,
                      r=[('pb', 0), ('st8', 'den')], w=[('o_tok', 1, g_)])
                    yield

            def epilogue(t):
                if dbg and l == nlayers - 1:
                    COPY('dve', ytmp[:], o_tok[:], ['o_tok'], ['ytmp'])
                    DMA('sp', dbg_o[t * 128:(t + 1) * 128, :], ytmp[:], r=['ytmp'], key='dbg_o')
                    COPY('dve', ytmp[:, 0:512].rearrange("p (h d) -> p h d", h=4), o4[:], ['o4'], ['ytmp'])
                    DMA('sp', dbg_raw[t * 128:(t + 1) * 128, :], ytmp[:, 0:512], r=['ytmp'], key='dbg_raw')
                for kc in range(KC):
                    TR(PBb[7][:, kc * 128:(kc + 1) * 128], o_tok[:, kc * 128:(kc + 1) * 128], identb[:], ['o_tok', 'identb'], [('pb', 7)])
                COPY('act', oT[:], PBb[7][:, 0:1024].rearrange("p (k n) -> p k n", k=KC), [('pb', 7)], ['oT'])
                yield
                for hh in range(2):
                    for kc in range(KC):
                        MM(PB[hh][:], oT[:, kc, :], wout[:, kc, hh * 512:(hh + 1) * 512], kc == 0, kc == KC - 1,
                           ['oT', 'wout'], [('pb', hh)])
                    A('dve', 'scalar_tensor_tensor', ytmp[:, hh * 512:(hh + 1) * 512], xres[:, t, hh * 512:(hh + 1) * 512], ALPHA,
                      PB[hh][:], ALU.mult, ALU.add, r=[('xres', t), ('pb', hh)], w=['ytmp'])
                    yield
                layer_norm(ytmp[:], 'ytmp', t, 0)
                yield
                emit_xT(t)
                if dbg and l == nlayers - 1:
                    DMA('sp', dbg_x1[t * 128:(t + 1) * 128, :], xres[:, t, :], r=[('xres', t)], key='dbg_x1')
                yield

            run_rr([prologue(0)])
            for t in range(NT):
                run_rr([dn_stream(t, [0], DN[0]), dn_stream(t, [1], DN[1]), dn_stream(t, [2], DN[2]), dn_stream(t, [3], DN[3]), swa(t)])
                rms_gate(t)
                if t + 1 < NT:
                    run_rr([epilogue(t), prologue(t + 1)])
                else:
                    run_rr([epilogue(t)])

            DMA('sp', lnp[:], lnp_d[l, 1], w=['lnp'], key='lnp')
            moe = (l % 2 == 1)
            li = l // 2
            if moe:
                DMA('sp', rw32[:], rw_d[li], w=['rw32'], key='rw32')
                xTf = ytmp[:].rearrange("p (k n) -> p k n", k=KC)
                for t in range(NT):
                    for kc in range(KC):
                        bk = kc // 4
                        TR(PB[bk][:, (kc % 4) * 128:(kc % 4 + 1) * 128], xres[:, t, kc * 128:(kc + 1) * 128], identf,
                           [('xres', t), 'cst'], [('pb', bk)])
                    COPY('act', xTf[:, 0:4, :], PB[0][:].rearrange("p (k n) -> p k n", k=4), [('pb', 0)], [('ytmp', 'h0')])
                    COPY('dve', xTf[:, 4:8, :], PB[1][:].rearrange("p (k n) -> p k n", k=4), [('pb', 1)], [('ytmp', 'h1')])
                    for kc in range(KC):
                        MM(PB[2][:, 0:NE], xTf[:, kc, :], rw32[:, kc, :], kc == 0, kc == KC - 1,
                           [('ytmp', 'h0'), ('ytmp', 'h1'), 'rw32'], [('pb', 2)])
                    COPY('dve', lg[:, t, :], PB[2][:, 0:NE], [('pb', 2)], [('lg', t)])
                bc = lambda a: a[:].unsqueeze(2).to_broadcast([128, NT, NE])
                A('dve', 'tensor_reduce', ms['m1'][:], lg[:], AX.X, ALU.max, r=['lg'], w=['ms_m1'])
                A('dve', 'tensor_tensor', mt['eq1'][:], lg[:], bc(ms['m1']), ALU.is_equal, r=['lg', 'ms_m1'], w=['mt_eq1'])
                A('dve', 'scalar_tensor_tensor', mt['l2'][:], mt['eq1'][:], -1e30, lg[:], ALU.mult, ALU.add,
                  r=['mt_eq1', 'lg'], w=['mt_l2'])
                A('dve', 'tensor_reduce', ms['m2'][:], mt['l2'][:], AX.X, ALU.max, r=['mt_l2'], w=['ms_m2'])
                A('dve', 'tensor_tensor', mt['eq2'][:], mt['l2'][:], bc(ms['m2']), ALU.is_equal, r=['mt_l2', 'ms_m2'], w=['mt_eq2'])
                A('dve', 'tensor_tensor', ms['d'][:], ms['m2'][:], ms['m1'][:], ALU.subtract, r=['ms_m1', 'ms_m2'], w=['ms_d'])
                ACT(ms['ed'][:], ms['d'][:], AF.Exp, ['ms_d'], ['ms_ed'])
                A('dve', 'tensor_scalar', ms['g1'][:], ms['ed'][:], 1.0, None, ALU.add, r=['ms_ed'], w=['ms_g1'])
                A('dve', 'reciprocal', ms['g1'][:], ms['g1'][:], r=['ms_g1'], w=['ms_g1'])
                A('dve', 'tensor_tensor', ms['g2'][:], ms['ed'][:], ms['g1'][:], ALU.mult, r=['ms_ed', 'ms_g1'], w=['ms_g2'])
                A('dve', 'tensor_tensor', mt['eq1'][:], mt['eq1'][:], bc(ms['g1']), ALU.mult, r=['mt_eq1', 'ms_g1'], w=['mt_eq1'])
                A('dve', 'tensor_tensor', mt['eq2'][:], mt['eq2'][:], bc(ms['g2']), ALU.mult, r=['mt_eq2', 'ms_g2'], w=['mt_eq2'])
                A('dve', 'tensor_tensor', comb[:], mt['eq1'][:], mt['eq2'][:], ALU.add, r=['mt_eq1', 'mt_eq2'], w=['comb'])
            nexp = NE if moe else 1
            for t in range(NT):
                ACT(xres[:, t, :], xres[:, t, :], AF.Copy, [('xres', t)], [('xres', t)], scale=ALPHA)
            for e_ in range(nexp):
                if moe:
                    Wg, Wu, Wd = mg_d[li, e_], mu_d[li, e_], md_d[li, e_]
                else:
                    Wg, Wu, Wd = fg_d[li], fu_d[li], fd_d[li]
                Wgv = Wg.rearrange("(k p) f -> p k f", p=128)
                Wuv = Wu.rearrange("(k p) f -> p k f", p=128)
                Wdv = Wd.rearrange("(f p) d -> p f d", p=128)
                for grp in range(7):
                    s_ = wq[0] % 2
                    wq[0] += 1
                    kg, ku, kd = 'wg%d' % s_, 'wu%d' % s_, 'wd%d' % s_
                    DMA('pool', wg[s_], Wgv[:, :, grp * 512:(grp + 1) * 512], w=[kg], key=kg)
                    DMA('pool', wu[s_], Wuv[:, :, grp * 512:(grp + 1) * 512], w=[ku], key=ku)
                    DMA('pool', wd[s_], Wdv[:, grp * 4:(grp + 1) * 4, :], w=[kd], key=kd)
                    ci = 0
                    for fc in range(4):
                        for tg in range(4):
                            bg = (ci % 2) * 2
                            bu = bg + 1
                            ci += 1
                            xk = [('xT', tg * 4 + i) for i in range(4)]
                            for kc in range(KC):
                                MM(PB[bg][:], wg[s_][:, kc, fc * 128:(fc + 1) * 128], xT[:, kc, tg * 512:(tg + 1) * 512],
                                   kc == 0, kc == KC - 1, [kg] + xk, [('pb', bg)])
                            for kc in range(KC):
                                MM(PB[bu][:], wu[s_][:, kc, fc * 128:(fc + 1) * 128], xT[:, kc, tg * 512:(tg + 1) * 512],
                                   kc == 0, kc == KC - 1, [ku] + xk, [('pb', bu)])
                            sg = sgt[ci % 2]
                            ACT(sg[:], PB[bg][:], AF.Silu, [('pb', bg)], [('ytmp', 'h%d' % (ci % 2))])
                            A('dve', 'tensor_tensor', hT[:, fc, tg * 512:(tg + 1) * 512], sg[:], PB[bu][:], ALU.mult,
                              r=[('ytmp', 'h%d' % (ci % 2)), ('pb', bu)], w=[('hT', fc, tg)])
                    di = 0
                    for t in range(NT):
                        for hh in range(2):
                            bd = 4 + (di % 4)
                            di += 1
                            for fc in range(4):
                                MM(PB[bd][:], hT[:, fc, t * 128:(t + 1) * 128], wd[s_][:, fc, hh * 512:(hh + 1) * 512],
                                   fc == 0, fc == 3, [('hT', fc, t // 4), kd], [('pb', bd)])
                            cs_ = comb[:, t, e_:e_ + 1] if moe else 1.0
                            A('dve', 'scalar_tensor_tensor', xres[:, t, hh * 512:(hh + 1) * 512], PB[bd][:], cs_,
                              xres[:, t, hh * 512:(hh + 1) * 512], ALU.mult, ALU.add,
                              r=[('pb', bd), ('xres', t)] + (['comb'] if moe else []), w=[('xres', t)])
            for t in range(NT):
                layer_norm(xres[:, t, :], ('xres', t), t, 1)
                if l < nlayers - 1:
                    emit_xT(t)
        fin = DMA('sp', y_d.rearrange("(n p) d -> p n d", p=128), xres[:], r=['xres'], key='yout')
        Sc.emit(final_wait_ops=[fin])
        print("ops:", Sc.nops, {e: len(v) for e, v in Sc.eops.items()})
    return nc


_CONSTS = None


def host_consts():
    c = np.zeros((128, 928), np.float32)
    i = np.arange(128)
    c[:, 0:128] = np.eye(128, dtype=np.float32)
    c[:, 128:256] = (i[:, None] <= i[None, :]).astype(np.float32)
    c[:, 256:384] = 1.0
    c[:, 384:512] = np.where(i[None, :] >= i[:, None], -BIG, 0.0)
    c[:, 512:640] = np.where(i[None, :] < i[:, None], BIG, 0.0)
    j = np.arange(256)
    rel = i[:, None] + 128 - j[None, :]
    c[:, 640:896] = np.where((rel >= 0) & (rel < 128), 0.0, -BIG)
    c[:, 896:928] = (10000.0 ** (-np.arange(0, 64, 2, dtype=np.float32) / 64.0)).astype(np.float32)[None, :]
    return c


def prep_inputs(inputs):
    f32 = lambda a: np.ascontiguousarray(np.asarray(a, dtype=np.float32))
    conv_w = f32(inputs["conv_w"])
    convw = np.ascontiguousarray(conv_w.reshape(DEPTH, 4, 12, 128).transpose(0, 3, 2, 1))
    pvrow = np.concatenate([f32(inputs["a_log"]), f32(inputs["dt_bias"]), f32(inputs["sinks"]),
                            f32(inputs["dn_norm_w"])], axis=1)
    pv = np.ascontiguousarray(np.broadcast_to(pvrow[:, None, :], (DEPTH, 128, 144)))
    lnrow = np.concatenate([f32(inputs["ln_g"]), f32(inputs["ln_b"])], axis=2)
    lnp = np.ascontiguousarray(np.broadcast_to(lnrow[:, :, None, :], (DEPTH, 2, 128, 2048)))
    rw = np.ascontiguousarray(f32(inputs["router_w"]).reshape(2, KC, 128, NE).transpose(0, 2, 1, 3))
    shared = {
        "w_in": f32(inputs["w_in"]), "w_out": f32(inputs["w_out"]),
        "ffn_w_gate": f32(inputs["ffn_w_gate"]), "ffn_w_up": f32(inputs["ffn_w_up"]),
        "ffn_w_down": f32(inputs["ffn_w_down"]),
        "moe_w_gate": f32(inputs["moe_w_gate"]), "moe_w_up": f32(inputs["moe_w_up"]),
        "moe_w_down": f32(inputs["moe_w_down"]),
        "convw": convw, "pv": pv, "lnp": lnp, "rw": rw, "cst": host_consts(),
    }
    return shared


def kernel(**inputs):
    shared = prep_inputs(inputs)
    x = np.asarray(inputs["x"], dtype=np.float32)
    pos = np.asarray(inputs["positions"], dtype=np.int32)
    nb = x.shape[0]
    nc = build()
    in_maps = []
    for b in range(nb):
        m = dict(shared)
        m["x"] = np.ascontiguousarray(x[b])
        m["pos"] = np.ascontiguousarray(pos[b].reshape(NT, 128).T)
        in_maps.append(m)
    res = run_bass_kernel_spmd(nc, in_maps, core_ids=list(range(nb)))
    return np.stack([np.asarray(r["y"], dtype=np.float32) for r in res.results], axis=0)
```

```python
import math
from contextlib import ExitStack
import numpy as np
import concourse.bass as bass
import concourse.mybir as mybir
from concourse.bass_utils import run_bass_kernel_spmd

F32 = mybir.dt.float32
BF16 = mybir.dt.bfloat16
I32 = mybir.dt.int32
ALU = mybir.AluOpType
AF = mybir.ActivationFunctionType
AX = mybir.AxisListType

CE = ['pe', 'act', 'dve', 'pool', 'sp']
EIDX = {e: i for i, e in enumerate(CE)}
EPOCH = 12000

DEPTH = 4
S = 2048
NT = 16
D = 1024
KC = 8
INC = 2824
DFF = 3584
NE = 8
ALPHA = (2.0 * DEPTH) ** 0.25
TMW = 1288
BIG = 30000.0


class Op:
    __slots__ = ('eng', 'fn', 'is_dma', 'dma_key', 'eidx', 'signal', 'sem', 'val', 'waits', 'vc', 'dvc')


class Sched:
    def __init__(self, nc):
        self.nc = nc
        self.eops = {e: [] for e in CE}
        self.last_writer = {}
        self.readers = {}
        self.desc = {}
        self.known = set()
        self.aliases = {}
        self.dma_cnt = {}
        self.last = {e: None for e in CE}
        self.nops = 0

    def alias(self, a, b):
        self.aliases.setdefault(a, set()).add(b)
        self.aliases.setdefault(b, set()).add(a)

    def _reg(self, k):
        if k in self.known:
            return
        self.known.add(k)
        for i in range(1, len(k)):
            self.desc.setdefault(k[:i], set()).add(k)
        self.desc.setdefault(k, set())

    def _related(self, k):
        out = [k]
        for i in range(1, len(k)):
            out.append(k[:i])
        out.extend(self.desc.get(k, ()))
        for r in self.aliases.get(k[0], ()):
            out.append((r,))
            out.extend(self.desc.get((r,), ()))
        return out

    def add(self, eng, fn, reads=(), writes=(), dma_key=None):
        reads = [k if isinstance(k, tuple) else (k,) for k in reads]
        writes = [k if isinstance(k, tuple) else (k,) for k in writes]
        writes = writes + [k for k in reads if k[0] == 'pb' and k not in writes]
        for k in reads:
            self._reg(k)
        for k in writes:
            self._reg(k)
        op = Op()
        op.eng = eng
        op.fn = fn
        op.is_dma = dma_key is not None
        op.dma_key = dma_key
        op.eidx = len(self.eops[eng])
        op.signal = False
        op.sem = None
        op.val = None
        deps = {}
        for b in reads:
            for k in self._related(b):
                w = self.last_writer.get(k)
                if w is not None:
                    deps[id(w)] = (w, 'raw')
        for b in writes:
            for k in self._related(b):
                w = self.last_writer.get(k)
                if w is not None and id(w) not in deps:
                    deps[id(w)] = (w, 'waw')
                for r in self.readers.get(k, {}).values():
                    if id(r) not in deps:
                        deps[id(r)] = (r, 'war')
        prev = self.last[eng]
        vc = list(prev.vc) if prev is not None else [-1] * len(CE)
        dvc = dict(prev.dvc) if prev is not None else {}
        best = {}
        for d, kind in deps.values():
            if d is op:
                continue
            if d.is_dma:
                if dvc.get(d.dma_key, 0) >= d.val:
                    continue
                key = ('d', d.dma_key)
                cur = best.get(key)
                if cur is None or d.val > cur.val:
                    best[key] = d
            else:
                if d.eng == eng and not op.is_dma and kind != 'raw':
                    continue
                if vc[EIDX[d.eng]] >= d.eidx:
                    continue
                key = ('c', d.eng)
                cur = best.get(key)
                if cur is None or d.eidx > cur.eidx:
                    best[key] = d
        op.waits = list(best.values())
        for d in op.waits:
            if d.is_dma:
                dvc[d.dma_key] = max(dvc.get(d.dma_key, 0), d.val)
            else:
                d.signal = True
                vc[EIDX[d.eng]] = max(vc[EIDX[d.eng]], d.eidx)
            for i, v in enumerate(d.vc):
                if v > vc[i]:
                    vc[i] = v
            for k, v in d.dvc.items():
                if v > dvc.get(k, 0):
                    dvc[k] = v
        if op.is_dma:
            c = self.dma_cnt.get(dma_key, 0) + 1
            self.dma_cnt[dma_key] = c
            op.val = 16 * c
        op.vc = vc
        op.dvc = dvc
        slot = ('d', dma_key) if op.is_dma else eng
        for b in reads:
            self.readers.setdefault(b, {})[slot] = op
        for b in writes:
            self.last_writer[b] = op
            self.readers[b] = {}
        self.eops[eng].append(op)
        self.last[eng] = op
        self.nops += 1
        return op

    def emit(self, final_wait_ops=()):
        nc = self.nc
        with ExitStack() as st:
            for e in CE:
                cnt = 0
                sem = None
                ep = 0
                for op in self.eops[e]:
                    if op.is_dma or not op.signal:
                        continue
                    if sem is None or cnt >= EPOCH:
                        sem = st.enter_context(nc.semaphore("s_%s_%d" % (e, ep)))
                        ep += 1
                        cnt = 0
                    cnt += 1
                    op.sem = sem
                    op.val = cnt
            dsem = {}
            for i, k in enumerate(self.dma_cnt):
                dsem[k] = st.enter_context(nc.semaphore("d_%d" % i))
            for e in CE:
                for op in self.eops[e]:
                    if op.is_dma:
                        op.sem = dsem[op.dma_key]
            block = st.enter_context(nc.Block())

            def run(engname):
                def body(eng):
                    for op in self.eops[engname]:
                        for d in op.waits:
                            eng.wait_ge(d.sem, d.val)
                        ins = op.fn(eng)
                        if op.is_dma:
                            ins.then_inc(op.sem, 16)
                        elif op.signal:
                            ins.then_inc(op.sem, 1)
                    if engname == 'sp':
                        for d in final_wait_ops:
                            eng.wait_ge(d.sem, d.val)
                return body

            block.tensor(run('pe'))
            block.scalar(run('act'))
            block.vector(run('dve'))
            block.gpsimd(run('pool'))
            block.sync(run('sp'))


def build(nlayers=DEPTH, dbg=False):
    nc = bass.Bass("TRN2", target_bir_lowering=False)
    dram = lambda name, shape, dt, kind="ExternalInput": nc.dram_tensor(name, shape, dt, kind=kind).ap()
    x_d = dram("x", [S, D], F32)
    pos_d = dram("pos", [128, NT], I32)
    w_in_d = dram("w_in", [DEPTH, D, INC], F32)
    w_out_d = dram("w_out", [DEPTH, D, D], F32)
    fg_d = dram("ffn_w_gate", [2, D, DFF], F32)
    fu_d = dram("ffn_w_up", [2, D, DFF], F32)
    fd_d = dram("ffn_w_down", [2, DFF, D], F32)
    mg_d = dram("moe_w_gate", [2, NE, D, DFF], F32)
    mu_d = dram("moe_w_up", [2, NE, D, DFF], F32)
    md_d = dram("moe_w_down", [2, NE, DFF, D], F32)
    convw_d = dram("convw", [DEPTH, 128, 12, 4], F32)
    pv_d = dram("pv", [DEPTH, 128, 144], F32)
    lnp_d = dram("lnp", [DEPTH, 2, 128, 2048], F32)
    rw_d = dram("rw", [2, 128, KC, NE], F32)
    cst_d = dram("cst", [128, 928], F32)
    y_d = dram("y", [S, D], F32, kind="ExternalOutput")
    fm_s = dram("fm_s", [1536, S], BF16, kind="Internal")
    tm_s = dram("tm_s", [S, TMW], F32, kind="Internal")
    if dbg:
        dbg_o = dram("dbg_o", [S, D], F32, kind="ExternalOutput")
        dbg_x1 = dram("dbg_x1", [S, D], F32, kind="ExternalOutput")
        dbg_raw = dram("dbg_raw", [S, 512], F32, kind="ExternalOutput")

    st = ExitStack()
    with st:
        sb = lambda name, shape, dt: st.enter_context(nc.sbuf_tensor("s_" + name, shape, dt))
        xres = sb("xres", [128, NT, D], F32)
        xT = sb("xT", [128, KC, S], BF16)
        R = sb("R", [128, 32768], BF16)
        wa = [R[:, s * 4096:(s + 1) * 4096].rearrange("p (k n) -> p k n", k=KC) for s in range(2)]
        wout = R[:, 8192:16384].rearrange("p (k n) -> p k n", k=KC)
        wg = [R[:, s * 4096:(s + 1) * 4096].rearrange("p (k n) -> p k n", k=KC) for s in range(2)]
        wu = [R[:, 8192 + s * 4096:8192 + (s + 1) * 4096].rearrange("p (k n) -> p k n", k=KC) for s in range(2)]
        wd = [R[:, 16384 + s * 4096:16384 + (s + 1) * 4096].rearrange("p (f n) -> p f n", f=4) for s in range(2)]
        hT = R[:, 24576:32768].rearrange("p (f n) -> p f n", f=4)
        rwb = R[:, 16384:32768].bitcast(F32).rearrange("p (e d) -> p e d", e=NE)
        rv = lambda a, n: R[:, a:a + n]
        rf = lambda a, n: R[:, a:a + n].bitcast(F32)
        kTa = rv(0, 2048)
        va = rv(2048, 2048).rearrange("p (t n) -> p t n", t=NT)
        Sf = rf(4096, 1024).rearrange("p (h n) -> p h n", h=4)
        Sb = rv(5120, 512).rearrange("p (h n) -> p h n", h=4)
        o4 = rf(5632, 1024).rearrange("p (h n) -> p h n", h=4)
        zs = rv(6656, 512)
        qr = rv(7168, 512)
        qTs = rv(7680, 512).rearrange("p (a n) -> p a n", a=4)
        stg = [rf(16384 + i * 1024, 1024) for i in range(4)]
        stA = [rf(20480 + i * 1030, 1030) for i in range(4)]
        cvbs = [rf(24600 + i * 1024, 1024) for i in range(2)]
        rnbs = [rf(26648 + i * 1024, 1024) for i in range(2)]
        sqb = [rv(28696 + i * 512, 512) for i in range(2)]
        outb = [rv(29720 + i * 512, 512) for i in range(2)]
        yb = rv(25672, 1536).rearrange("p (c w) -> p c w", c=12)
        tmv = rf(29256, 2576)
        cst = sb("cst", [128, 928], F32)
        identf = cst[:, 0:128]
        tri = cst[:, 128:256]
        onesf = cst[:, 256:384]
        maskA = cst[:, 384:512]
        maskB = cst[:, 512:640]
        swam = cst[:, 640:896]
        invf = cst[:, 896:928]
        identb = sb("identb", [128, 128], BF16)
        onesb = sb("onesb", [128, 128], BF16)
        cvals = sb("cvals", [128, 4], F32)
        posi = sb("posi", [128, NT], I32)
        posf = sb("posf", [128, NT], F32)
        cosT = sb("cosT", [128, NT, 32], F32)
        sinT = sb("sinT", [128, NT, 32], F32)
        lnp = sb("lnp", [128, 2048], F32)
        convw = sb("convw", [128, 12, 4], F32)
        pv = sb("pv", [128, 144], F32)
        ab_all = sb("ab_all", [128, NT, 8], F32)
        pp = {n: sb("pp_" + n, [128, 64], F32) for n in
              ['ta', 'sp', 'g', 'eb', 'beta', 'nbeta', 'gcum', 'ngc', 'egc', 'bege']}
        negA = sb("negA", [128, 4], F32)
        ytmp = sb("ytmp", [128, D], F32)
        ptmp = None
        o4b = ytmp[:, 0:512].rearrange("p (h n) -> p h n", h=4)
        sgt = [ytmp[:, 0:512], ytmp[:, 512:1024]]
        kr = sb("kr", [128, 128], BF16)
        pb_ = sb("pb_", [128, 1024], BF16)
        mask8b = sb("mask8b", [128, 256], BF16)
        pTb = sb("pTb", [128, 1024], BF16)
        st8 = sb("st8", [128, 24], F32)
        o_tok = sb("o_tok", [128, D], BF16)
        oT = sb("oT", [128, KC, 128], BF16)
        lnst = sb("lnst", [128, 16], F32)
        DN = []
        R_SETS = {1: 22600, 2: 16384, 3: 19400}
        for i_ in range(4):
            n_ = str(i_)
            B_ = {'n': n_, 'bps': 3 + i_, 'brd': 3 + i_}
            if i_ == 0:
                f32t = lambda nm, w_: sb(nm + n_, [128, w_], F32)
                b16t = lambda nm, w_: sb(nm + n_, [128, w_], BF16)
            else:
                off = [R_SETS[i_]]
                def f32t(nm, w_, off=off):
                    a_ = rf(off[0], 2 * w_)
                    off[0] += 2 * w_
                    return a_
                def b16t(nm, w_, off=off):
                    a_ = rv(off[0], w_)
                    off[0] += w_
                    return a_
            B_['trg'] = f32t('trg', 128)
            B_['dL'] = f32t('dL', 128)
            B_['dAT'] = f32t('dAT', 128)
            B_['dsc'] = f32t('dsc', 4)
            ab_ = f32t('abpa', 384)
            B_['abp'] = [ab_, ab_]
            B_['u_sb'] = f32t('u_sb', 128)
            B_['t1'] = f32t('t1', 128)
            B_['ATm'] = b16t('ATm', 128)
            B_['TT'] = b16t('TT', 128)
            B_['Xv'] = b16t('Xv', 128)
            B_['Xw'] = b16t('Xw', 128)
            B_['ke'] = b16t('ke', 128)
            B_['wT_sb'] = b16t('wT_sb', 128)
            B_['vn'] = b16t('vn', 128)
            if i_ > 0:
                lim = {1: 25672, 2: 19400, 3: 22600}[i_]
                assert off[0] <= lim, (i_, off[0])
            DN.append(B_)
        rs4 = sb("rs4", [128, 8], F32)
        lg = sb("lg", [128, NT, NE], F32)
        rw32 = sb("rw32", [128, KC, NE], F32)
        comb = sb("comb", [128, NT, NE], F32)
        mt = {n: sb("mt_" + n, [128, NT, NE], F32) for n in ['eq1', 'l2', 'eq2']}
        ms = {n: sb("ms_" + n, [128, NT], F32) for n in ['m1', 'm2', 'd', 'ed', 'g1', 'g2']}
        PB = [st.enter_context(nc.psum_tensor("pb%d" % i, [128, 512], F32)) for i in range(8)]
        PBb = [p[:].bitcast(BF16) for p in PB]

        print("sbuf bytes remaining:", nc.sbuf_bytes_remaining)
        Sc = Sched(nc)
        def alias_groups(groups):
            for i in range(len(groups)):
                for j in range(i + 1, len(groups)):
                    for a_ in groups[i]:
                        for b_ in groups[j]:
                            Sc.alias(a_, b_)
        alias_groups([['wa0', 'wa1'], ['kTa', 'va', 'Sf', 'Sb', 'o4', 'zs', 'qr', 'qTs'], ['wg0', 'wg1']])
        alias_groups([['wout'], ['wu0', 'wu1']])
        set1 = [nm + str(i_) for i_ in (1, 2, 3) for nm in ['trg', 'dL', 'dAT', 'dsc', 'abp', 'u_sb', 't1', 'ATm', 'TT', 'Xv', 'Xw', 'ke', 'wT_sb', 'vn']]
        alias_groups([['stg', 'stA', 'cvb', 'rnb', 'sqb', 'outb'], ['yb', 'tmv'] + set1, ['wd0', 'wd1', 'hT']])

        def A(eng, meth, *args, r=(), w=(), **kw):
            return Sc.add(eng, lambda e: getattr(e, meth)(*args, **kw), reads=r, writes=w)

        def DMA(eng, out, in_, r=(), w=(), key=None):
            return Sc.add(eng, lambda e: e.dma_start(out=out, in_=in_), reads=r, writes=w, dma_key=key)

        def MM(out, lhsT, rhs, start, stop, r, w):
            return Sc.add('pe', lambda e: e.matmul(out, lhsT, rhs, start=start, stop=stop), reads=r, writes=w)

        def TR(out, in_, ident, r, w):
            return Sc.add('pe', lambda e: e.transpose(out, in_, ident), reads=r, writes=w)

        def ACT(out, in_, func, r, w, **kw):
            return Sc.add('act', lambda e: e.activation(out, in_, func, **kw), reads=r, writes=w)

        rr = [0]

        def evac_eng():
            rr[0] += 1
            return 'act' if rr[0] % 2 == 0 else 'dve'

        def COPY(eng, out, in_, r, w):
            if eng == 'act':
                return ACT(out, in_, AF.Copy, r, w)
            return A(eng, 'tensor_copy', out, in_, r=r, w=w)

        DMA('sp', cst[:], cst_d, w=['cst'], key='cst')
        DMA('sp', posi[:], pos_d, w=['posi'], key='posi')
        DMA('sp', xres[:], x_d.rearrange("(n p) d -> p n d", p=128), w=['xres'], key='xin')
        A('dve', 'tensor_copy', identb[:], identf, r=['cst'], w=['identb'])
        A('dve', 'tensor_copy', onesb[:], onesf, r=['cst'], w=['onesb'])
        A('dve', 'tensor_scalar', mask8b[:], swam, 8.0, None, ALU.mult, r=['cst'], w=['mask8b'])
        A('dve', 'memset', cvals[:, 0:1], -math.pi, w=['cvals'])
        A('dve', 'memset', cvals[:, 1:2], 1.0, w=['cvals'])
        A('dve', 'memset', cvals[:, 2:3], 1e-5, w=['cvals'])
        A('dve', 'memset', cvals[:, 3:4], 1e-6, w=['cvals'])
        A('dve', 'tensor_copy', posf[:], posi[:], r=['posi'], w=['posf'])
        A('dve', 'tensor_tensor', cosT[:], posf[:].unsqueeze(2).to_broadcast([128, NT, 32]),
          invf.unsqueeze(1).to_broadcast([128, NT, 32]), ALU.mult, r=['posf', 'cst'], w=['cosT'])
        TWO_PI = 2 * math.pi
        angi = ytmp[:, 512:1024].bitcast(I32).rearrange("p (t n) -> p t n", t=NT)
        angk = ytmp[:, 0:512].rearrange("p (t n) -> p t n", t=NT)
        A('dve', 'tensor_copy', sinT[:], cosT[:], r=['cosT'], w=['sinT'])
        A('dve', 'tensor_scalar', cosT[:], cosT[:], 0.5 * math.pi, None, ALU.add, r=['cosT'], w=['cosT'])
        for tab, key in ((sinT, 'sinT'), (cosT, 'cosT')):
            A('dve', 'tensor_scalar', angk[:], tab[:], 1.0 / TWO_PI, None, ALU.mult, r=[key], w=[('ytmp', 'h0')])
            A('dve', 'tensor_copy', angi[:], angk[:], r=[('ytmp', 'h0')], w=[('ytmp', 'h1')])
            A('dve', 'tensor_copy', angk[:], angi[:], r=[('ytmp', 'h1')], w=[('ytmp', 'h0')])
            A('dve', 'scalar_tensor_tensor', tab[:], angk[:], -TWO_PI, tab[:], ALU.mult, ALU.add, r=[('ytmp', 'h0'), key], w=[key])
            A('dve', 'tensor_single_scalar', angk[:], tab[:], math.pi, ALU.is_gt, r=[key], w=[('ytmp', 'h0')])
            A('dve', 'scalar_tensor_tensor', tab[:], angk[:], -TWO_PI, tab[:], ALU.mult, ALU.add, r=[('ytmp', 'h0'), key], w=[key])
            A('dve', 'tensor_single_scalar', angk[:], tab[:], -math.pi, ALU.is_lt, r=[key], w=[('ytmp', 'h0')])
            A('dve', 'scalar_tensor_tensor', tab[:], angk[:], TWO_PI, tab[:], ALU.mult, ALU.add, r=[('ytmp', 'h0'), key], w=[key])
            A('dve', 'tensor_scalar', tab[:], tab[:], math.pi, -math.pi, ALU.min, ALU.max, r=[key], w=[key])
            ACT(tab[:], tab[:], AF.Sin, [key], [key])

        def emit_xT(t):
            COPY('act', o_tok[:], xres[:, t, :], [('xres', t)], ['o_tok'])
            pbank = 7
            for kc in range(KC):
                TR(PBb[pbank][:, kc * 128:(kc + 1) * 128], o_tok[:, kc * 128:(kc + 1) * 128], identb[:],
                   ['o_tok', 'identb'], [('pb', pbank)])
            COPY('dve', xT[:, :, t * 128:(t + 1) * 128],
                 PBb[pbank][:, 0:1024].rearrange("p (k n) -> p k n", k=KC), [('pb', pbank)], [('xT', t)])

        def layer_norm(src_ap, src_key, t, lidx):
            for hh in range(2):
                A('dve', 'bn_stats', lnst[:, hh * 6:(hh + 1) * 6], src_ap[:, hh * 512:(hh + 1) * 512],
                  r=[src_key], w=['lnst'])
            A('dve', 'bn_aggr', lnst[:, 12:14], lnst[:, 0:12], r=['lnst'], w=['lnst2'])
            ACT(lnst[:, 15:16], lnst[:, 13:14], AF.Sqrt, ['lnst2', 'cvals'], ['lnst4'], bias=cvals[:, 2:3])
            A('dve', 'reciprocal', lnst[:, 14:15], lnst[:, 15:16], r=['lnst4'], w=['lnst3'])
            A('dve', 'tensor_scalar', ytmp[:], src_ap, lnst[:, 12:13], lnst[:, 14:15], ALU.subtract, ALU.mult,
              r=[src_key, 'lnst2', 'lnst3'], w=['ytmp'])
            A('pool', 'tensor_tensor', ytmp[:], ytmp[:], lnp[:, 0:1024], ALU.mult, r=['ytmp', 'lnp'], w=['ytmp'])
            A('pool', 'tensor_tensor', xres[:, t, :], ytmp[:], lnp[:, 1024:2048], ALU.add,
              r=['ytmp', 'lnp'], w=[('xres', t)])

        for t in range(NT):
            emit_xT(t)

        wq = [0]

        for l in range(nlayers):
            DMA('sp', convw[:], convw_d[l], w=['convw'], key='convw')
            DMA('sp', pv[:], pv_d[l], w=['pv'], key='pv')
            DMA('sp', lnp[:], lnp_d[l, 0], w=['lnp'], key='lnp')
            DMA('pool', wout, w_out_d[l].rearrange("(k p) n -> p k n", p=128), w=['wout'], key='wout')
            alog = pv[:, 0:4]
            dtb = pv[:, 4:8]
            sinks = pv[:, 8:16]
            dnw = pv[:, 16:144]
            w_in_v = w_in_d[l].rearrange("(k p) n -> p k n", p=128)
            blocks = [(0, 512, 'fm'), (512, 1024, 'fm'), (1024, 1536, 'fm'),
                      (1536, 2048, 'tm'), (2048, 2560, 'tm'), (2560, 2824, 'tm')]
            cnt = 0
            wq0 = wq[0]
            wq[0] += len(blocks)

            def issue_w(bi):
                c0, c1, _ = blocks[bi]
                s_ = (wq0 + bi) % 2
                wkey = 'wa%d' % s_
                DMA('pool', wa[s_][:, :, 0:c1 - c0], w_in_v[:, :, c0:c1], w=[wkey], key=wkey)

            issue_w(0)
            units = [(bi, j, tg) for bi in range(3) for j in range(4) for tg in range(4)]
            NU = len(units)
            obc = [0]

            def uinfo(i):
                bi, j, tg = units[i]
                c0 = blocks[bi][0]
                s_ = (wq0 + bi) % 2
                return bi, j, tg, c0, s_, 'wa%d' % s_, (c0 // 128) + j, (c0 == 1024)

            def S1(i):
                bi, j, tg, c0, s_, wkey, ch, isv = uinfo(i)
                pbk = tg
                for kc in range(KC):
                    MM(PB[pbk][:], wa[s_][:, kc, j * 128:(j + 1) * 128], xT[:, kc, tg * 512:(tg + 1) * 512],
                       kc == 0, kc == KC - 1, [wkey] + [('xT', tg * 4 + i_) for i_ in range(4)], [('pb', pbk)])
                COPY('act', stA[tg][:, 3:515], PB[pbk][:], [('pb', pbk)], [('stA', tg)])
                if tg == 0:
                    A('pool', 'memset', stA[tg][:, 0:3], 0.0, w=[('stA', tg, 'h')])
                else:
                    A('pool', 'tensor_copy', stA[tg][:, 0:3], stA[tg - 1][:, 512:515], r=[('stA', tg - 1)],
                      w=[('stA', tg, 'h')])

            def S2(i):
                bi, j, tg, c0, s_, wkey, ch, isv = uinfo(i)
                cvb = cvbs[tg % 2]
                kcv = ('cvb', tg % 2)
                A('dve', 'tensor_scalar', cvb, stA[tg][:, 0:512], convw[:, ch, 0:1], None, ALU.mult,
                  r=[('stA', tg), 'convw'], w=[kcv])
                for k in range(1, 4):
                    A('dve', 'scalar_tensor_tensor', cvb, stA[tg][:, k:k + 512], convw[:, ch, k:k + 1], cvb,
                      ALU.mult, ALU.add, r=[('stA', tg), 'convw', kcv], w=[kcv])

            def S3(i):
                bi, j, tg, c0, s_, wkey, ch, isv = uinfo(i)
                cvb = cvbs[tg % 2]
                kcv = ('cvb', tg % 2)
                if isv:
                    ob = obc[0] % 2
                    obc[0] += 1
                    ACT(outb[ob], cvb, AF.Silu, [kcv], [('outb', ob)])
                    DMA('sp', fm_s[ch * 128:(ch + 1) * 128, tg * 512:(tg + 1) * 512], outb[ob],
                        r=[('outb', ob)], w=[('fm_s', ch, tg)], key='outb%d' % ob)
                else:
                    ACT(stA[tg][:, 0:512], cvb, AF.Silu, [kcv], [('stA', tg, 'y')])
                    sb_ = tg % 2
                    ACT(sqb[sb_], stA[tg][:, 0:512], AF.Square, [('stA', tg, 'y')], [('sqb', sb_)])
                    MM(PB[4 + tg][:], onesb[:], sqb[sb_], True, True, ['onesb', ('sqb', sb_)], [('pb', 4 + tg)])

            def S4a(i):
                bi, j, tg, c0, s_, wkey, ch, isv = uinfo(i)
                if isv:
                    return
                ACT(rnbs[tg % 2], PB[4 + tg][:], AF.Sqrt, [('pb', 4 + tg), 'cvals'], [('rnb', tg % 2)], bias=cvals[:, 3:4])

            def S4b(i):
                bi, j, tg, c0, s_, wkey, ch, isv = uinfo(i)
                if isv:
                    return
                rnb = rnbs[tg % 2]
                krn = ('rnb', tg % 2)
                ob = obc[0] % 2
                obc[0] += 1
                A('dve', 'reciprocal', rnb, rnb, r=[krn], w=[krn])
                A('dve', 'tensor_tensor', outb[ob], stA[tg][:, 0:512], rnb, ALU.mult,
                  r=[('stA', tg, 'y'), krn], w=[('outb', ob)])
                DMA('sp', fm_s[ch * 128:(ch + 1) * 128, tg * 512:(tg + 1) * 512], outb[ob],
                    r=[('outb', ob)], w=[('fm_s', ch, tg)], key='outb%d' % ob)

            for i in range(NU + 2):
                if i < NU:
                    if units[i][1] == 0 and units[i][2] == 0:
                        issue_w(units[i][0] + 1)
                    S1(i)
                if 0 <= i - 2 < NU:
                    S4a(i - 2)
                if 0 <= i - 1 < NU:
                    S2(i - 1)
                    S3(i - 1)
                if 0 <= i - 2 < NU:
                    S4b(i - 2)
            for bi in range(3, len(blocks)):
                c0, c1, kind = blocks[bi]
                s_ = (wq0 + bi) % 2
                wkey = 'wa%d' % s_
                wcols = c1 - c0
                if bi + 1 < len(blocks):
                    issue_w(bi + 1)
                for t in range(NT):
                    pbk = cnt % 4
                    q_ = cnt % 4
                    cnt += 1
                    for kc in range(KC):
                        MM(PB[pbk][:, 0:wcols], xT[:, kc, t * 128:(t + 1) * 128], wa[s_][:, kc, 0:wcols],
                           kc == 0, kc == KC - 1, [wkey, ('xT', t)], [('pb', pbk)])
                    COPY(evac_eng(), stg[q_][:, 0:wcols], PB[pbk][:, 0:wcols], [('pb', pbk)], [('stg', q_)])
                    if c0 == 2048:
                        COPY('act', ab_all[:, t, :], PB[pbk][:, 0:8], [('pb', pbk)], [('ab_all', t)])
                    DMA('sp', tm_s[t * 128:(t + 1) * 128, c0 - 1536:c1 - 1536], stg[q_][:, 0:wcols],
                        r=[('stg', q_)], w=[('tm_s', t, bi)], key='stg%d' % q_)
            v3 = lambda n: pp[n][:].rearrange("p (t h) -> p t h", h=4)
            A('dve', 'tensor_tensor', v3('ta'), ab_all[:, :, 0:4], dtb.unsqueeze(1).to_broadcast([128, NT, 4]), ALU.add,
              r=['ab_all', 'pv'], w=['pp_ta'])
            ACT(pp['ta'][:], pp['ta'][:], AF.Exp, ['pp_ta'], ['pp_ta'])
            ACT(pp['sp'][:], pp['ta'][:], AF.Ln, ['pp_ta', 'cvals'], ['pp_sp'], bias=cvals[:, 1:2])
            ACT(negA[:], alog, AF.Exp, ['pv'], ['negA'])
            A('dve', 'tensor_scalar', negA[:], negA[:], -1.0, None, ALU.mult, r=['negA'], w=['negA'])
            A('dve', 'tensor_tensor', v3('g'), v3('sp'), negA[:].unsqueeze(1).to_broadcast([128, NT, 4]), ALU.mult,
              r=['pp_sp', 'negA'], w=['pp_g'])
            ACT(v3('eb'), ab_all[:, :, 4:8], AF.Exp, ['ab_all'], ['pp_eb'], scale=-1.0)
            A('dve', 'tensor_scalar', pp['eb'][:], pp['eb'][:], 1.0, None, ALU.add, r=['pp_eb'], w=['pp_eb'])
            A('dve', 'reciprocal', pp['beta'][:], pp['eb'][:], r=['pp_eb'], w=['pp_beta'])
            A('dve', 'tensor_scalar', pp['nbeta'][:], pp['beta'][:], -1.0, None, ALU.mult, r=['pp_beta'], w=['pp_nbeta'])
            MM(PB[4][:, 0:64], tri, pp['g'][:], True, True, ['cst', 'pp_g'], [('pb', 4)])
            COPY('dve', pp['gcum'][:], PB[4][:, 0:64], [('pb', 4)], ['pp_gcum'])
            A('dve', 'tensor_scalar', pp['ngc'][:], pp['gcum'][:], -1.0, None, ALU.mult, r=['pp_gcum'], w=['pp_ngc'])
            ACT(pp['egc'][:], pp['gcum'][:], AF.Exp, ['pp_gcum'], ['pp_egc'])
            A('dve', 'tensor_tensor', pp['bege'][:], pp['beta'][:], pp['egc'][:], ALU.mult,
              r=['pp_beta', 'pp_egc'], w=['pp_bege'])
            A('pool', 'memset', Sf[:], 0.0, w=['Sf'])
            A('pool', 'memset', Sb[:], 0.0, w=['Sb'])

            def run_rr(gens):
                gens = list(gens)
                while gens:
                    for g_ in list(gens):
                        try:
                            next(g_)
                        except StopIteration:
                            gens.remove(g_)

            def prologue(t):
                DMA('sp', yb[:], fm_s.rearrange("(c p) w -> p c w", p=128)[:, :, t * 128:(t + 1) * 128],
                    r=['fm_s'], w=['yb'], key='yb')
                DMA('sp', tmv[:], tm_s[t * 128:(t + 1) * 128, :], r=[('tm_s', t)], w=['tmv'], key='tmv')
                yield
                ACT(zs[:], tmv[:, 0:512], AF.Silu, ['tmv'], ['zs'])
                for h in range(4):
                    TR(PBb[2][:, h * 128:(h + 1) * 128], yb[:, 4 + h, :], identb[:], ['yb', 'identb'], [('pb', 2)])
                for h in range(4):
                    TR(PBb[2][:, (4 + h) * 128:(5 + h) * 128], yb[:, 8 + h, :], identb[:], ['yb', 'identb'], [('pb', 2)])
                yield

            def dn_head(t, h, B_):
                ix = t * 4 + h
                col = lambda n: pp[n][:, ix:ix + 1]
                bps, brd = B_['bps'], B_['brd']
                kps, krd = ('pb', bps), ('pb', brd)
                n_ = B_['n']
                K_ = lambda nm: nm + n_
                A('pool', 'tensor_scalar', B_['trg'][:], tri, col('g'), -1.0, ALU.mult, ALU.mult, r=['cst', 'pp_g'], w=[K_('trg')])
                MM(PB[bps][:, 0:128], onesf, B_['trg'][:], True, False, ['cst', K_('trg')], [kps])
                MM(PB[bps][:, 0:128], identf, maskA, False, True, ['cst'], [kps])
                MM(PB[bps][:, 128:256], onesf, B_['trg'][:], True, False, ['cst', K_('trg')], [kps])
                MM(PB[bps][:, 128:256], identf, maskB, False, True, ['cst'], [kps])
                kTh = yb[:, 4 + h, :]
                qTh = yb[:, h, :]
                MM(PB[bps][:, 256:384], kTh, kTh, True, True, ['yb'], [kps])
                MM(PB[bps][:, 384:512], kTh, qTh, True, True, ['yb'], [kps])
                yield
                dsc_ = B_['dsc']
                ACT(B_['dL'][:], PB[bps][:, 0:128], AF.Exp, [kps, 'pp_gcum'], [K_('dL')], bias=col('gcum'))
                ACT(B_['dAT'][:], PB[bps][:, 128:256], AF.Exp, [kps, 'pp_ngc'], [K_('dAT')], bias=col('ngc'), scale=-1.0)
                ACT(dsc_[:, 0:1], PB[bps][:, 255:256], AF.Exp, [kps, 'pp_ngc'], [K_('dsc')], bias=col('ngc'), scale=-1.0)
                ACT(dsc_[:, 1:2], PB[bps][:, 255:256], AF.Exp, [kps], [K_('dsc')], scale=-1.0)
                abp_ = B_['abp']
                A('dve', 'scalar_tensor_tensor', abp_[0][:, 0:128], PB[bps][:, 256:384], col('nbeta'), B_['dL'][:], ALU.mult, ALU.mult,
                  r=[kps, 'pp_nbeta', K_('dL')], w=[(K_('abp'), 0)])
                A('dve', 'tensor_tensor', B_['ATm'][:], PB[bps][:, 384:512], B_['dAT'][:], ALU.mult, r=[kps, K_('dAT')], w=[K_('ATm')])
                yield
                TR(PB[brd][:, 0:128], abp_[0][:, 0:128], identf, [(K_('abp'), 0), 'cst'], [krd])
                COPY('act', abp_[0][:, 128:256], PB[brd][:, 0:128], [krd], [(K_('abp'), 0)])
                A('pool', 'tensor_copy', abp_[0][:, 256:384], identf, r=['cst'], w=[(K_('abp'), 0)])
                yield
                cur = 0
                for rnd in range(1, 8):
                    Ac = abp_[cur][:, 0:128]
                    Bc = abp_[cur][:, 128:256]
                    Pc = abp_[cur][:, 256:384]
                    nxt = 1 - cur
                    rk = [(K_('abp'), 0), 'cst']
                    if rnd <= 6:
                        MM(PB[brd][:, 0:128], Bc, Ac, True, True, rk, [krd])
                        if rnd <= 5:
                            MM(PB[brd][:, 128:256], Ac, Bc, True, True, rk, [krd])
                    MM(PB[brd][:, 256:384], identf, Pc, True, False, rk, [krd])
                    MM(PB[brd][:, 256:384], Ac, Pc, False, True, rk, [krd])
                    if rnd <= 6:
                        COPY(evac_eng(), abp_[nxt][:, 0:384], PB[brd][:, 0:384], [krd], [(K_('abp'), 0)])
                    else:
                        COPY(evac_eng(), B_['TT'][:], PB[brd][:, 256:384], [krd], [K_('TT')])
                    cur = nxt
                    yield
                ktok = PBb[2][:, h * 128:(h + 1) * 128]
                vtok = PBb[2][:, (4 + h) * 128:(5 + h) * 128]
                A('dve', 'tensor_scalar', B_['Xv'][:], vtok, col('beta'), None, ALU.mult, r=[('pb', 2), 'pp_beta'], w=[K_('Xv')])
                A('dve', 'tensor_scalar', B_['Xw'][:], ktok, col('bege'), None, ALU.mult, r=[('pb', 2), 'pp_bege'], w=[K_('Xw')])
                A('dve', 'tensor_scalar', B_['ke'][:], ktok, dsc_[:, 0:1], None, ALU.mult, r=[('pb', 2), K_('dsc')], w=[K_('ke')])
                MM(PB[brd][:, 0:128], B_['TT'][:], B_['Xv'][:], True, True, [K_('TT'), K_('Xv')], [krd])
                MM(PB[brd][:, 128:256], B_['Xw'][:], B_['TT'][:], True, True, [K_('TT'), K_('Xw')], [krd])
                yield
                COPY('act', B_['u_sb'][:], PB[brd][:, 0:128], [krd], [K_('u_sb')])
                COPY('act', B_['wT_sb'][:], PB[brd][:, 128:256], [krd], [K_('wT_sb')])
                MM(PB[brd][:, 256:384], B_['wT_sb'][:], Sb[:, h, :], True, True, [K_('wT_sb'), ('Sb', h)], [krd])
                MM(PB[brd][:, 384:512], qTh, Sb[:, h, :], True, True, ['yb', ('Sb', h)], [krd])
                yield
                A('dve', 'tensor_tensor', B_['vn'][:], B_['u_sb'][:], PB[brd][:, 256:384], ALU.subtract, r=[K_('u_sb'), krd], w=[K_('vn')])
                A('dve', 'tensor_scalar', B_['t1'][:], PB[brd][:, 384:512], col('egc'), None, ALU.mult, r=[krd, 'pp_egc'], w=[K_('t1')])
                MM(PB[bps][:, 0:128], B_['ATm'][:], B_['vn'][:], True, True, [K_('ATm'), K_('vn')], [kps])
                MM(PB[bps][:, 128:256], B_['ke'][:], B_['vn'][:], True, True, [K_('ke'), K_('vn')], [kps])
                yield
                A('dve', 'tensor_tensor', o4[:, h, :], B_['t1'][:], PB[bps][:, 0:128], ALU.add, r=[K_('t1'), kps], w=[('o4', h)])
                A('dve', 'scalar_tensor_tensor', Sf[:, h, :], Sf[:, h, :], dsc_[:, 1:2], PB[bps][:, 128:256], ALU.mult, ALU.add,
                  r=[('Sf', h), K_('dsc'), kps], w=[('Sf', h)])
                COPY('act', Sb[:, h, :], Sf[:, h, :], [('Sf', h)], [('Sb', h)])
                yield

            def dn_stream(t, heads, B_):
                for h in heads:
                    yield from dn_head(t, h, B_)

            def rms_gate(t):
                A('pool', 'tensor_tensor', o4b[:], o4[:], o4[:], ALU.mult, r=['o4'], w=['ytmp'])
                A('dve', 'tensor_reduce', rs4[:, 0:4], o4b[:], AX.X, ALU.add, r=['ytmp'], w=['rs4'])
                ACT(rs4[:, 0:4], rs4[:, 0:4], AF.Sqrt, ['rs4', 'cvals'], ['rs4'], bias=cvals[:, 3:4], scale=1.0 / 16384.0)
                A('dve', 'reciprocal', rs4[:, 0:4], rs4[:, 0:4], r=['rs4'], w=['rs4'])
                A('dve', 'tensor_scalar', rs4[:, 4:8], rs4[:, 0:4], 128.0 ** -0.5, None, ALU.mult, r=['rs4'], w=['rs4b'])
                A('dve', 'tensor_tensor', o4b[:], o4[:], rs4[:, 4:8].unsqueeze(2).to_broadcast([128, 4, 128]), ALU.mult,
                  r=['o4', 'rs4b'], w=['ytmp'])
                A('pool', 'tensor_tensor', o4b[:], o4b[:], dnw.unsqueeze(1).to_broadcast([128, 4, 128]), ALU.mult,
                  r=['ytmp', 'pv'], w=['ytmp'])
                A('dve', 'tensor_tensor', o_tok[:, 0:512].rearrange("p (h d) -> p h d", h=4), o4b[:],
                  zs[:].rearrange("p (h d) -> p h d", h=4), ALU.mult, r=['ytmp', 'zs'], w=[('o_tok', 0)])

            def swa(t):
                def rope(src, nh, dst4, keyw):
                    s4 = src.rearrange("p (h two d) -> p h two d", two=2, d=32)
                    x1 = s4[:, :, 0, :]
                    x2 = s4[:, :, 1, :]
                    cs = cosT[:, t, :].unsqueeze(1).to_broadcast([128, nh, 32])
                    sn = sinT[:, t, :].unsqueeze(1).to_broadcast([128, nh, 32])
                    w4 = lambda i: ytmp[:, i * 256:i * 256 + nh * 32].rearrange("p (h d) -> p h d", d=32)
                    A('pool', 'tensor_tensor', w4(0), x1, cs, ALU.mult, r=['tmv', 'cosT'], w=[('ytmp', 0)])
                    A('pool', 'tensor_tensor', w4(1), x2, sn, ALU.mult, r=['tmv', 'sinT'], w=[('ytmp', 1)])
                    A('pool', 'tensor_tensor', w4(2), x2, cs, ALU.mult, r=['tmv', 'cosT'], w=[('ytmp', 2)])
                    A('pool', 'tensor_tensor', w4(3), x1, sn, ALU.mult, r=['tmv', 'sinT'], w=[('ytmp', 3)])
                    A('pool', 'tensor_tensor', dst4[0], w4(0), w4(1), ALU.subtract, r=[('ytmp', 0), ('ytmp', 1)], w=[keyw])
                    A('pool', 'tensor_tensor', dst4[1], w4(2), w4(3), ALU.add, r=[('ytmp', 2), ('ytmp', 3)], w=[keyw])
                qv = qr[:].rearrange("p (pr two half d) -> p pr two half d", pr=4, two=2, half=2)
                for two in range(2):
                    srcq = tmv[:, 520 + two * 256:520 + (two + 1) * 256]
                    rope(srcq, 4, (qv[:, :, two, 0, :], qv[:, :, two, 1, :]), 'qr')
                    yield
                kv4 = kr[:].rearrange("p (h half d) -> p h half d", h=2, half=2)
                rope(tmv[:, 1032:1160], 2, (kv4[:, :, 0, :], kv4[:, :, 1, :]), 'kr')
                COPY('act', va[:, t, :], tmv[:, 1160:1288], ['tmv'], [('va', t)])
                yield
                for pr in range(4):
                    TR(PBb[7][:, pr * 128:(pr + 1) * 128], qr[:, pr * 128:(pr + 1) * 128], identb[:], ['qr', 'identb'], [('pb', 7)])
                TR(PBb[7][:, 512:640], kr[:], identb[:], ['kr', 'identb'], [('pb', 7)])
                COPY('act', qTs[:], PBb[7][:, 0:512].rearrange("p (a n) -> p a n", a=4), [('pb', 7)], ['qTs'])
                COPY('act', kTa[:, t * 128:(t + 1) * 128], PBb[7][:, 512:640], [('pb', 7)], [('kTa', t)])
                yield
                for g_ in range(2):
                    ps_ = slice(g_ * 64, (g_ + 1) * 64)
                    if t == 0:
                        k0, nk = 0, 128
                        mkb = mask8b[:, 128:256]
                        kkeys = [('kTa', 0)]
                    else:
                        k0, nk = (t - 1) * 128, 256
                        mkb = mask8b[:]
                        kkeys = [('kTa', t - 1), ('kTa', t)]
                    nb = nk // 128
                    for j in range(4):
                        bk = j // 2
                        c0_ = (j % 2) * 256
                        MM(PB[bk][:, c0_:c0_ + nk], identb[:], mkb, True, False, ['identb', 'mask8b'], [('pb', bk)])
                        MM(PB[bk][:, c0_:c0_ + nk], qTs[ps_, j, :], kTa[ps_, k0:k0 + nk], False, True, ['qTs'] + kkeys, [('pb', bk)])
                    yield
                    for bk in range(2):
                        A('dve', 'tensor_reduce', st8[:, bk * 2:bk * 2 + 2],
                          PB[bk][:].rearrange("p (a n) -> p a n", a=2)[:, :, 0:nk], AX.X, ALU.max, r=[('pb', bk)], w=[('st8', 'mx')])
                    A('dve', 'scalar_tensor_tensor', st8[:, 4:8], st8[:, 0:4], 0.125, sinks[:, g_ * 4:g_ * 4 + 4], ALU.mult, ALU.max,
                      r=[('st8', 'mx'), 'pv'], w=[('st8', 'm')])
                    A('dve', 'tensor_scalar', st8[:, 8:12], st8[:, 4:8], -1.0, None, ALU.mult, r=[('st8', 'm')], w=[('st8', 'nm')])
                    A('dve', 'tensor_tensor', st8[:, 12:16], st8[:, 8:12], sinks[:, g_ * 4:g_ * 4 + 4], ALU.add,
                      r=[('st8', 'nm'), 'pv'], w=[('st8', 'sk')])
                    yield
                    for j in range(4):
                        bk = j // 2
                        c0_ = (j % 2) * 256
                        ACT(pb_[:, j * 256:j * 256 + nk], PB[bk][:, c0_:c0_ + nk], AF.Exp, [('pb', bk), ('st8', 'nm')],
                            [('pb_', j), ('st8', 'rs', j)], bias=st8[:, 8 + j:9 + j], scale=0.125, accum_out=st8[:, 16 + j:17 + j])
                    ACT(st8[:, 12:16], st8[:, 12:16], AF.Exp, [('st8', 'sk')], [('st8', 'sk')])
                    yield
                    for j in range(4):
                        for b_ in range(nb):
                            TR(PBb[7][:, j * 256 + b_ * 128:j * 256 + (b_ + 1) * 128], pb_[:, j * 256 + b_ * 128:j * 256 + (b_ + 1) * 128],
                               identb[:], [('pb_', j), 'identb'], [('pb', 7)])
                    A('dve', 'tensor_tensor', st8[:, 20:24], st8[:, 16:20], st8[:, 12:16], ALU.add,
                      r=[('st8', 'rs'), ('st8', 'sk')], w=[('st8', 'den')])
                    A('dve', 'reciprocal', st8[:, 20:24], st8[:, 20:24], r=[('st8', 'den')], w=[('st8', 'den')])
                    yield
                    if nb == 2:
                        COPY('act', pTb[:, 0:512], PBb[7][:, 0:512], [('pb', 7)], [('pTb', 0)])
                        COPY('dve', pTb[:, 512:1024], PBb[7][:, 512:1024], [('pb', 7)], [('pTb', 1)])
                    else:
                        for j in range(4):
                            COPY('act' if j % 2 == 0 else 'dve', pTb[:, j * 256:j * 256 + 128], PBb[7][:, j * 256:j * 256 + 128],
                                 [('pb', 7)], [('pTb', j // 2)])
                    yield
                    for j in range(4):
                        for b_ in range(nb):
                            tt_ = t - (nb - 1) + b_
                            MM(PB[0][:, j * 64:(j + 1) * 64], pTb[:, j * 256 + b_ * 128:j * 256 + (b_ + 1) * 128],
                               va[:, tt_, g_ * 64:(g_ + 1) * 64], b_ == 0, b_ == nb - 1, [('pTb', j // 2), ('va', tt_)], [('pb', 0)])
                    yield
                    A('dve', 'tensor_tensor', o_tok[:, 512 + g_ * 256:512 + (g_ + 1) * 256].rearrange("p (j d) -> p j d", j=4),
                      PB[0][:, 0:256].rearrange("p (j d) -> p j d", j=4),
                      st8[:, 20:24].unsqueeze(2).to_broadcast([128, 4, 64]), ALU.mult,
                      r=[('pb', 0), ('st8', 'den')], w=[('o_tok', 1, g_)])
                    yield

            def epilogue(t):
                if dbg and l == nlayers - 1:
                    COPY('dve', ytmp[:], o_tok[:], ['o_tok'], ['ytmp'])
                    DMA('sp', dbg_o[t * 128:(t + 1) * 128, :], ytmp[:], r=['ytmp'], key='dbg_o')
                    COPY('dve', ytmp[:, 0:512].rearrange("p (h d) -> p h d", h=4), o4[:], ['o4'], ['ytmp'])
                    DMA('sp', dbg_raw[t * 128:(t + 1) * 128, :], ytmp[:, 0:512], r=['ytmp'], key='dbg_raw')
                for kc in range(KC):
                    TR(PBb[7][:, kc * 128:(kc + 1) * 128], o_tok[:, kc * 128:(kc + 1) * 128], identb[:], ['o_tok', 'identb'], [('pb', 7)])
                COPY('act', oT[:], PBb[7][:, 0:1024].rearrange("p (k n) -> p k n", k=KC), [('pb', 7)], ['oT'])
                yield
                for hh in range(2):
                    for kc in range(KC):
                        MM(PB[hh][:], oT[:, kc, :], wout[:, kc, hh * 512:(hh + 1) * 512], kc == 0, kc == KC - 1,
                           ['oT', 'wout'], [('pb', hh)])
                    A('dve', 'scalar_tensor_tensor', ytmp[:, hh * 512:(hh + 1) * 512], xres[:, t, hh * 512:(hh + 1) * 512], ALPHA,
                      PB[hh][:], ALU.mult, ALU.add, r=[('xres', t), ('pb', hh)], w=['ytmp'])
                    yield
                layer_norm(ytmp[:], 'ytmp', t, 0)
                yield
                emit_xT(t)
                if dbg and l == nlayers - 1:
                    DMA('sp', dbg_x1[t * 128:(t + 1) * 128, :], xres[:, t, :], r=[('xres', t)], key='dbg_x1')
                yield

            run_rr([prologue(0)])
            for t in range(NT):
                run_rr([dn_stream(t, [0], DN[0]), dn_stream(t, [1], DN[1]), dn_stream(t, [2], DN[2]), dn_stream(t, [3], DN[3]), swa(t)])
                rms_gate(t)
                if t + 1 < NT:
                    run_rr([epilogue(t), prologue(t + 1)])
                else:
                    run_rr([epilogue(t)])

            DMA('sp', lnp[:], lnp_d[l, 1], w=['lnp'], key='lnp')
            moe = (l % 2 == 1)
            li = l // 2
            if moe:
                DMA('sp', rw32[:], rw_d[li], w=['rw32'], key='rw32')
                xTf = ytmp[:].rearrange("p (k n) -> p k n", k=KC)
                for t in range(NT):
                    for kc in range(KC):
                        bk = kc // 4
                        TR(PB[bk][:, (kc % 4) * 128:(kc % 4 + 1) * 128], xres[:, t, kc * 128:(kc + 1) * 128], identf,
                           [('xres', t), 'cst'], [('pb', bk)])
                    COPY('act', xTf[:, 0:4, :], PB[0][:].rearrange("p (k n) -> p k n", k=4), [('pb', 0)], [('ytmp', 'h0')])
                    COPY('dve', xTf[:, 4:8, :], PB[1][:].rearrange("p (k n) -> p k n", k=4), [('pb', 1)], [('ytmp', 'h1')])
                    for kc in range(KC):
                        MM(PB[2][:, 0:NE], xTf[:, kc, :], rw32[:, kc, :], kc == 0, kc == KC - 1,
                           [('ytmp', 'h0'), ('ytmp', 'h1'), 'rw32'], [('pb', 2)])
                    COPY('dve', lg[:, t, :], PB[2][:, 0:NE], [('pb', 2)], [('lg', t)])
                bc = lambda a: a[:].unsqueeze(2).to_broadcast([128, NT, NE])
                A('dve', 'tensor_reduce', ms['m1'][:], lg[:], AX.X, ALU.max, r=['lg'], w=['ms_m1'])
                A('dve', 'tensor_tensor', mt['eq1'][:], lg[:], bc(ms['m1']), ALU.is_equal, r=['lg', 'ms_m1'], w=['mt_eq1'])
                A('dve', 'scalar_tensor_tensor', mt['l2'][:], mt['eq1'][:], -1e30, lg[:], ALU.mult, ALU.add,
                  r=['mt_eq1', 'lg'], w=['mt_l2'])
                A('dve', 'tensor_reduce', ms['m2'][:], mt['l2'][:], AX.X, ALU.max, r=['mt_l2'], w=['ms_m2'])
                A('dve', 'tensor_tensor', mt['eq2'][:], mt['l2'][:], bc(ms['m2']), ALU.is_equal, r=['mt_l2', 'ms_m2'], w=['mt_eq2'])
                A('dve', 'tensor_tensor', ms['d'][:], ms['m2'][:], ms['m1'][:], ALU.subtract, r=['ms_m1', 'ms_m2'], w=['ms_d'])
                ACT(ms['ed'][:], ms['d'][:], AF.Exp, ['ms_d'], ['ms_ed'])
                A('dve', 'tensor_scalar', ms['g1'][:], ms['ed'][:], 1.0, None, ALU.add, r=['ms_ed'], w=['ms_g1'])
                A('dve', 'reciprocal', ms['g1'][:], ms['g1'][:], r=['ms_g1'], w=['ms_g1'])
                A('dve', 'tensor_tensor', ms['g2'][:], ms['ed'][:], ms['g1'][:], ALU.mult, r=['ms_ed', 'ms_g1'], w=['ms_g2'])
                A('dve', 'tensor_tensor', mt['eq1'][:], mt['eq1'][:], bc(ms['g1']), ALU.mult, r=['mt_eq1', 'ms_g1'], w=['mt_eq1'])
                A('dve', 'tensor_tensor', mt['eq2'][:], mt['eq2'][:], bc(ms['g2']), ALU.mult, r=['mt_eq2', 'ms_g2'], w=['mt_eq2'])
                A('dve', 'tensor_tensor', comb[:], mt['eq1'][:], mt['eq2'][:], ALU.add, r=['mt_eq1', 'mt_eq2'], w=['comb'])
            nexp = NE if moe else 1
            for t in range(NT):
                ACT(xres[:, t, :], xres[:, t, :], AF.Copy, [('xres', t)], [('xres', t)], scale=ALPHA)
            for e_ in range(nexp):
                if moe:
                    Wg, Wu, Wd = mg_d[li, e_], mu_d[li, e_], md_d[li, e_]
                else:
                    Wg, Wu, Wd = fg_d[li], fu_d[li], fd_d[li]
                Wgv = Wg.rearrange("(k p) f -> p k f", p=128)
                Wuv = Wu.rearrange("(k p) f -> p k f", p=128)
                Wdv = Wd.rearrange("(f p) d -> p f d", p=128)
                for grp in range(7):
                    s_ = wq[0] % 2
                    wq[0] += 1
                    kg, ku, kd = 'wg%d' % s_, 'wu%d' % s_, 'wd%d' % s_
                    DMA('pool', wg[s_], Wgv[:, :, grp * 512:(grp + 1) * 512], w=[kg], key=kg)
                    DMA('pool', wu[s_], Wuv[:, :, grp * 512:(grp + 1) * 512], w=[ku], key=ku)
                    DMA('pool', wd[s_], Wdv[:, grp * 4:(grp + 1) * 4, :], w=[kd], key=kd)
                    ci = 0
                    for fc in range(4):
                        for tg in range(4):
                            bg = (ci % 2) * 2
                            bu = bg + 1
                            ci += 1
                            xk = [('xT', tg * 4 + i) for i in range(4)]
                            for kc in range(KC):
                                MM(PB[bg][:], wg[s_][:, kc, fc * 128:(fc + 1) * 128], xT[:, kc, tg * 512:(tg + 1) * 512],
                                   kc == 0, kc == KC - 1, [kg] + xk, [('pb', bg)])
                            for kc in range(KC):
                                MM(PB[bu][:], wu[s_][:, kc, fc * 128:(fc + 1) * 128], xT[:, kc, tg * 512:(tg + 1) * 512],
                                   kc == 0, kc == KC - 1, [ku] + xk, [('pb', bu)])
                            sg = sgt[ci % 2]
                            ACT(sg[:], PB[bg][:], AF.Silu, [('pb', bg)], [('ytmp', 'h%d' % (ci % 2))])
                            A('dve', 'tensor_tensor', hT[:, fc, tg * 512:(tg + 1) * 512], sg[:], PB[bu][:], ALU.mult,
                              r=[('ytmp', 'h%d' % (ci % 2)), ('pb', bu)], w=[('hT', fc, tg)])
                    di = 0
                    for t in range(NT):
                        for hh in range(2):
                            bd = 4 + (di % 4)
                            di += 1
                            for fc in range(4):
                                MM(PB[bd][:], hT[:, fc, t * 128:(t + 1) * 128], wd[s_][:, fc, hh * 512:(hh + 1) * 512],
                                   fc == 0, fc == 3, [('hT', fc, t // 4), kd], [('pb', bd)])
                            cs_ = comb[:, t, e_:e_ + 1] if moe else 1.0
                            A('dve', 'scalar_tensor_tensor', xres[:, t, hh * 512:(hh + 1) * 512], PB[bd][:], cs_,
                              xres[:, t, hh * 512:(hh + 1) * 512], ALU.mult, ALU.add,
                              r=[('pb', bd), ('xres', t)] + (['comb'] if moe else []), w=[('xres', t)])
            for t in range(NT):
                layer_norm(xres[:, t, :], ('xres', t), t, 1)
                if l < nlayers - 1:
                    emit_xT(t)
        fin = DMA('sp', y_d.rearrange("(n p) d -> p n d", p=128), xres[:], r=['xres'], key='yout')
        Sc.emit(final_wait_ops=[fin])
        print("ops:", Sc.nops, {e: len(v) for e, v in Sc.eops.items()})
    return nc


_CONSTS = None


def host_consts():
    c = np.zeros((128, 928), np.float32)
    i = np.arange(128)
    c[:, 0:128] = np.eye(128, dtype=np.float32)
    c[:, 128:256] = (i[:, None] <= i[None, :]).astype(np.float32)
    c[:, 256:384] = 1.0
    c[:, 384:512] = np.where(i[None, :] >= i[:, None], -BIG, 0.0)
    c[:, 512:640] = np.where(i[None, :] < i[:, None], BIG, 0.0)
    j = np.arange(256)
    rel = i[:, None] + 128 - j[None, :]
    c[:, 640:896] = np.where((rel >= 0) & (rel < 128), 0.0, -BIG)
    c[:, 896:928] = (10000.0 ** (-np.arange(0, 64, 2, dtype=np.float32) / 64.0)).astype(np.float32)[None, :]
    return c


def prep_inputs(inputs):
    f32 = lambda a: np.ascontiguousarray(np.asarray(a, dtype=np.float32))
    conv_w = f32(inputs["conv_w"])
    convw = np.ascontiguousarray(conv_w.reshape(DEPTH, 4, 12, 128).transpose(0, 3, 2, 1))
    pvrow = np.concatenate([f32(inputs["a_log"]), f32(inputs["dt_bias"]), f32(inputs["sinks"]),
                            f32(inputs["dn_norm_w"])], axis=1)
    pv = np.ascontiguousarray(np.broadcast_to(pvrow[:, None, :], (DEPTH, 128, 144)))
    lnrow = np.concatenate([f32(inputs["ln_g"]), f32(inputs["ln_b"])], axis=2)
    lnp = np.ascontiguousarray(np.broadcast_to(lnrow[:, :, None, :], (DEPTH, 2, 128, 2048)))
    rw = np.ascontiguousarray(f32(inputs["router_w"]).reshape(2, KC, 128, NE).transpose(0, 2, 1, 3))
    shared = {
        "w_in": f32(inputs["w_in"]), "w_out": f32(inputs["w_out"]),
        "ffn_w_gate": f32(inputs["ffn_w_gate"]), "ffn_w_up": f32(inputs["ffn_w_up"]),
        "ffn_w_down": f32(inputs["ffn_w_down"]),
        "moe_w_gate": f32(inputs["moe_w_gate"]), "moe_w_up": f32(inputs["moe_w_up"]),
        "moe_w_down": f32(inputs["moe_w_down"]),
        "convw": convw, "pv": pv, "lnp": lnp, "rw": rw, "cst": host_consts(),
    }
    return shared


def kernel(**inputs):
    shared = prep_inputs(inputs)
    x = np.asarray(inputs["x"], dtype=np.float32)
    pos = np.asarray(inputs["positions"], dtype=np.int32)
    nb = x.shape[0]
    nc = build()
    in_maps = []
    for b in range(nb):
        m = dict(shared)
        m["x"] = np.ascontiguousarray(x[b])
        m["pos"] = np.ascontiguousarray(pos[b].reshape(NT, 128).T)
        in_maps.append(m)
    res = run_bass_kernel_spmd(nc, in_maps, core_ids=list(range(nb)))
    return np.stack([np.asarray(r["y"], dtype=np.float32) for r in res.results], axis=0)
```
